# Optimizing a Trainium2 kernel written in Bass

```python
import math
import jax
import jax.numpy as jnp
from jax import lax
import numpy as np

D_MODEL = 2048
BATCH = 2
SEQ = 16384
DEPTH = 1

RMS_EPS = 1e-6

RW_HEADS = 16
RW_HEAD_DIM = 64
RW_WIDTH = RW_HEADS * RW_HEAD_DIM
RW_DECAY_LORA = 64
RW_ICL_LORA = 64
RW_GATE_LORA = 160
RW_GN_EPS = 64e-5
RW_IN_SIZES = (RW_WIDTH, RW_WIDTH, RW_WIDTH, RW_DECAY_LORA, RW_DECAY_LORA,
               RW_ICL_LORA, RW_ICL_LORA, RW_GATE_LORA)
RW_IN = 3 * RW_WIDTH + 2 * RW_DECAY_LORA + 2 * RW_ICL_LORA + RW_GATE_LORA

GDN_QK_HEADS = 4
GDN_V_HEADS = 8
GDN_HEAD_DIM = 128
GDN_QK_WIDTH = GDN_QK_HEADS * GDN_HEAD_DIM
GDN_V_WIDTH = GDN_V_HEADS * GDN_HEAD_DIM
GDN_CONV_CH = 2 * GDN_QK_WIDTH + GDN_V_WIDTH
GDN_CONV = 5
GDN_CHUNK = 64
GDN_NORM_EPS = 1e-6
GDN_IN = GDN_CONV_CH + GDN_V_WIDTH + 4 * GDN_V_HEADS

GATE_IN = 2 * D_MODEL
IN_WIDTH = RW_IN + GDN_IN + GATE_IN

FFN_HIDDEN = -(-8 * D_MODEL // 768) * 256

kernel_name = "hybrid_rwkv7_gdn_gated_merge_encoder"


def _split(t, sizes):
    offs = []
    acc = 0
    for s in sizes[:-1]:
        acc += s
        offs.append(acc)
    return jnp.split(t, offs, axis=-1)


def rms_norm(x, gain):
    xf = x.astype(jnp.float32)
    y = xf * lax.rsqrt(jnp.mean(xf * xf, axis=-1, keepdims=True) + RMS_EPS)
    return (y * gain.astype(jnp.float32)).astype(x.dtype)


def l2_normalize(t):
    return t * lax.rsqrt(jnp.sum(t * t, axis=-1, keepdims=True) + 1e-6)


def centred_shift_mix(p, mu):
    zero = jnp.zeros_like(p[:, :1])
    prev = jnp.concatenate([zero, p[:, :-1]], axis=1)
    nxt = jnp.concatenate([p[:, 1:], zero], axis=1)
    return p + mu * (0.5 * (prev + nxt) - p)


def depthwise_conv_centred(x, w):
    c = x.shape[-1]
    k = w.shape[0]
    return lax.conv_general_dilated(
        x, w[:, None, :], window_strides=(1,), padding=[(k // 2, k // 2)],
        dimension_numbers=("NWC", "WIO", "NWC"), feature_group_count=c)


def rwkv7_scan(r, w, k, v, kk, a, reverse):
    b, s, h, n = r.shape
    xs = tuple(jnp.swapaxes(t, 0, 1) for t in (r, w, k, v, kk, a))

    def step(state, inp):
        r_t, w_t, k_t, v_t, kk_t, a_t = inp
        sa = jnp.einsum("bhvk,bhk->bhv", state, kk_t)
        state = (state * w_t[:, :, None, :]
                 - sa[..., None] * (kk_t * a_t)[:, :, None, :]
                 + v_t[..., None] * k_t[:, :, None, :])
        return state, jnp.einsum("bhvk,bhk->bhv", state, r_t)

    s0 = jnp.zeros((b, h, n, n), jnp.float32)
    _, ys = lax.scan(step, s0, xs, reverse=reverse)
    return jnp.swapaxes(ys, 0, 1)


def rwkv7_branch(p, mu, w0_f, w2_f, w0_b, w2_b, a0_f, a2_f, a0_b, a2_b,
                 g2, k_k, k_a, r_k, gn_w, gn_b):
    dtype = p.dtype
    b, s, _ = p.shape
    p = centred_shift_mix(p.astype(jnp.float32), mu)
    r, k, v, wd_f, wd_b, ad_f, ad_b, gd = _split(p, RW_IN_SIZES)

    def heads(t):
        return t.reshape(b, s, RW_HEADS, RW_HEAD_DIM)

    kk = l2_normalize(heads(k * k_k))
    g = jax.nn.sigmoid(gd) @ g2

    def direction(wd, w0, w2, ad, a0, a2, reverse):
        w_log = -jax.nn.softplus(-(w0 + jnp.tanh(wd) @ w2)) - 0.5
        decay = jnp.exp(-jnp.exp(w_log))
        a = jax.nn.sigmoid(a0 + ad @ a2)
        k_dir = k * (1.0 + (a - 1.0) * k_a)
        y = rwkv7_scan(heads(r), heads(decay), heads(k_dir), heads(v), kk, heads(a), reverse)
        return y, k_dir

    y_f, k_f = direction(wd_f, w0_f, w2_f, ad_f, a0_f, a2_f, False)
    y_b, k_b = direction(wd_b, w0_b, w2_b, ad_b, a0_b, a2_b, True)
    y = y_f + y_b
    mean = jnp.mean(y, axis=-1, keepdims=True)
    var = jnp.mean(jnp.square(y - mean), axis=-1, keepdims=True)
    y = ((y - mean) * lax.rsqrt(var + RW_GN_EPS) * gn_w.reshape(RW_HEADS, RW_HEAD_DIM)
         + gn_b.reshape(RW_HEADS, RW_HEAD_DIM))
    bonus = jnp.sum(heads(r) * heads(0.5 * (k_f + k_b)) * r_k, axis=-1, keepdims=True) * heads(v)
    y = (y + bonus).reshape(b, s, RW_WIDTH) * g
    return y.astype(dtype)


def gated_delta_chunked(q, k, v, g, beta):
    b, s, h, dk = q.shape
    dv = v.shape[-1]
    c = GDN_CHUNK
    n = s // c

    def chunks(t):
        t = t.astype(jnp.float32).reshape((b, n, c, h) + t.shape[3:])
        return jnp.moveaxis(t, 3, 1)

    q, k, v, g, beta = (chunks(t) for t in (q, k, v, g, beta))
    g = jnp.cumsum(g, axis=-1)
    kb = k * beta[..., None]
    vb = v * beta[..., None]
    idx = jnp.arange(c)
    incl = idx[:, None] >= idx[None, :]
    strict = idx[:, None] > idx[None, :]
    decay = jnp.exp(jnp.where(incl, g[..., :, None] - g[..., None, :], -jnp.inf))
    a_mat = jnp.einsum("bhnid,bhnjd->bhnij", kb, k) * decay
    m = jnp.eye(c, dtype=jnp.float32) + jnp.where(strict, a_mat, 0.0)
    rhs = jnp.concatenate([vb, kb * jnp.exp(g)[..., None]], axis=-1)
    sol = lax.linalg.triangular_solve(m, rhs, left_side=True, lower=True, unit_diagonal=True)
    u, w = sol[..., :dv], sol[..., dv:]
    qk = jnp.einsum("bhnid,bhnjd->bhnij", q, k) * decay
    qg = q * jnp.exp(g)[..., None]
    kg = k * jnp.exp(g[..., -1:] - g)[..., None]
    g_last = jnp.exp(g[..., -1])
    xs = tuple(jnp.moveaxis(t, 2, 0) for t in (u, w, qk, qg, kg, g_last))

    def step(state, inp):
        u_c, w_c, qk_c, qg_c, kg_c, gl_c = inp
        v_new = u_c - jnp.einsum("bhcd,bhde->bhce", w_c, state)
        o_c = (jnp.einsum("bhcd,bhde->bhce", qg_c, state)
               + jnp.einsum("bhij,bhje->bhie", qk_c, v_new))
        state = state * gl_c[..., None, None] + jnp.einsum("bhcd,bhce->bhde", kg_c, v_new)
        return state, o_c

    s0 = jnp.zeros((b, h, dk, dv), jnp.float32)
    _, o = lax.scan(step, s0, xs)
    return jnp.transpose(o, (1, 0, 3, 2, 4)).reshape(b, s, h, dv)


def gdn_branch(p, conv_w, a_log_f, dt_bias_f, a_log_b, dt_bias_b, norm_w):
    dtype = p.dtype
    b, s, _ = p.shape
    vh = GDN_V_HEADS
    p = p.astype(jnp.float32)
    qkv, z, al_f, al_b, be_f, be_b = _split(p, (GDN_CONV_CH, GDN_V_WIDTH, vh, vh, vh, vh))
    qkv = jax.nn.silu(depthwise_conv_centred(qkv, conv_w.astype(jnp.float32)))
    q, k, v = _split(qkv, (GDN_QK_WIDTH, GDN_QK_WIDTH, GDN_V_WIDTH))
    rep = GDN_V_HEADS // GDN_QK_HEADS
    q = jnp.repeat(l2_normalize(q.reshape(b, s, GDN_QK_HEADS, GDN_HEAD_DIM)), rep, axis=2) * (GDN_HEAD_DIM ** -0.5)
    k = jnp.repeat(l2_normalize(k.reshape(b, s, GDN_QK_HEADS, GDN_HEAD_DIM)), rep, axis=2)
    v = v.reshape(b, s, vh, GDN_HEAD_DIM)
    g_f = -jnp.exp(a_log_f) * jax.nn.softplus(al_f + dt_bias_f)
    g_b = -jnp.exp(a_log_b) * jax.nn.softplus(al_b + dt_bias_b)
    o_f = gated_delta_chunked(q, k, v, g_f, jax.nn.sigmoid(be_f))

    def flip(t):
        return jnp.flip(t, axis=1)

    o_b = flip(gated_delta_chunked(flip(q), flip(k), flip(v), flip(g_b), flip(jax.nn.sigmoid(be_b))))
    o = o_f + o_b
    o = (o * lax.rsqrt(jnp.mean(o * o, axis=-1, keepdims=True) + GDN_NORM_EPS) * norm_w
         * jax.nn.silu(z.reshape(b, s, vh, GDN_HEAD_DIM)))
    return o.reshape(b, s, GDN_V_WIDTH).astype(dtype)


def setup_inputs(seed: int = 0) -> dict:
    key = jax.random.key(seed)
    keys = iter(jax.random.split(key, 40))
    f32 = jnp.float32
    L = DEPTH

    def normal(shape, scale):
        return scale * jax.random.normal(next(keys), shape, f32)

    def uniform(shape, lo, hi):
        return jax.random.uniform(next(keys), shape, f32, lo, hi)

    def gain(n):
        return 1.0 + normal((L, n), 0.02)

    def dt_bias():
        dt = jnp.exp(uniform((L, GDN_V_HEADS), math.log(1e-3), math.log(1e-1)))
        return dt + jnp.log(-jnp.expm1(-dt))

    return {
        "x": normal((BATCH, SEQ, D_MODEL), 1.0),
        "norm_pre_mix": gain(D_MODEL),
        "w_in": normal((L, D_MODEL, IN_WIDTH), D_MODEL ** -0.5),
        "rw_shift_mu": uniform((L, RW_IN), 0.0, 1.0),
        "rw_w0_f": uniform((L, RW_WIDTH), -6.5, -1.5),
        "rw_w2_f": normal((L, RW_DECAY_LORA, RW_WIDTH), 0.1 * RW_DECAY_LORA ** -0.5),
        "rw_w0_b": uniform((L, RW_WIDTH), -6.5, -1.5),
        "rw_w2_b": normal((L, RW_DECAY_LORA, RW_WIDTH), 0.1 * RW_DECAY_LORA ** -0.5),
        "rw_a0_f": normal((L, RW_WIDTH), 0.1),
        "rw_a2_f": normal((L, RW_ICL_LORA, RW_WIDTH), RW_ICL_LORA ** -0.5),
        "rw_a0_b": normal((L, RW_WIDTH), 0.1),
        "rw_a2_b": normal((L, RW_ICL_LORA, RW_WIDTH), RW_ICL_LORA ** -0.5),
        "rw_g2": normal((L, RW_GATE_LORA, RW_WIDTH), RW_GATE_LORA ** -0.5),
        "rw_k_k": 0.85 + normal((L, RW_WIDTH), 0.02),
        "rw_k_a": 1.0 + normal((L, RW_WIDTH), 0.02),
        "rw_r_k": normal((L, RW_HEADS, RW_HEAD_DIM), 0.1),
        "rw_gn_w": gain(RW_WIDTH),
        "rw_gn_b": normal((L, RW_WIDTH), 0.02),
        "gdn_conv_w": normal((L, GDN_CONV, GDN_CONV_CH), GDN_CONV ** -0.5),
        "gdn_a_log_f": jnp.log(uniform((L, GDN_V_HEADS), 1.0, 16.0)),
        "gdn_dt_bias_f": dt_bias(),
        "gdn_a_log_b": jnp.log(uniform((L, GDN_V_HEADS), 1.0, 16.0)),
        "gdn_dt_bias_b": dt_bias(),
        "gdn_norm_w": gain(GDN_HEAD_DIM),
        "w_branch_rw": normal((L, RW_WIDTH, D_MODEL), RW_WIDTH ** -0.5),
        "w_branch_gdn": normal((L, GDN_V_WIDTH, D_MODEL), GDN_V_WIDTH ** -0.5),
        "w_out": normal((L, D_MODEL, D_MODEL), D_MODEL ** -0.5),
        "norm_post_mix": gain(D_MODEL),
        "norm_pre_ffn": gain(D_MODEL),
        "w_ffn_gate": normal((L, D_MODEL, FFN_HIDDEN), D_MODEL ** -0.5),
        "w_ffn_up": normal((L, D_MODEL, FFN_HIDDEN), D_MODEL ** -0.5),
        "w_ffn_down": normal((L, FFN_HIDDEN, D_MODEL), FFN_HIDDEN ** -0.5),
        "norm_post_ffn": gain(D_MODEL),
    }


def reference(x, norm_pre_mix, w_in, rw_shift_mu, rw_w0_f, rw_w2_f, rw_w0_b, rw_w2_b,
              rw_a0_f, rw_a2_f, rw_a0_b, rw_a2_b, rw_g2, rw_k_k, rw_k_a, rw_r_k,
              rw_gn_w, rw_gn_b, gdn_conv_w, gdn_a_log_f, gdn_dt_bias_f, gdn_a_log_b,
              gdn_dt_bias_b, gdn_norm_w, w_branch_rw, w_branch_gdn, w_out,
              norm_post_mix, norm_pre_ffn, w_ffn_gate, w_ffn_up, w_ffn_down, norm_post_ffn):
    h = x
    for l in range(DEPTH):
        u = rms_norm(h, norm_pre_mix[l])
        p = jnp.einsum("bsd,de->bse", u, w_in[l])
        p_rw, p_gdn, p_gate = _split(p, (RW_IN, GDN_IN, GATE_IN))
        y_rw = rwkv7_branch(p_rw, rw_shift_mu[l], rw_w0_f[l], rw_w2_f[l], rw_w0_b[l], rw_w2_b[l],
                            rw_a0_f[l], rw_a2_f[l], rw_a0_b[l], rw_a2_b[l], rw_g2[l],
                            rw_k_k[l], rw_k_a[l], rw_r_k[l], rw_gn_w[l], rw_gn_b[l])
        y_gdn = gdn_branch(p_gdn, gdn_conv_w[l], gdn_a_log_f[l], gdn_dt_bias_f[l],
                           gdn_a_log_b[l], gdn_dt_bias_b[l], gdn_norm_w[l])
        gate_rw, gate_gdn = _split(p_gate, (D_MODEL, D_MODEL))
        merged = (jax.nn.sigmoid(gate_rw) * jnp.einsum("bse,ed->bsd", y_rw, w_branch_rw[l])
                  + jax.nn.sigmoid(gate_gdn) * jnp.einsum("bse,ed->bsd", y_gdn, w_branch_gdn[l]))
        h = h + rms_norm(jnp.einsum("bsd,de->bse", merged, w_out[l]), norm_post_mix[l])
        u = rms_norm(h, norm_pre_ffn[l])
        f = (jax.nn.silu(jnp.einsum("bsd,df->bsf", u, w_ffn_gate[l]))
             * jnp.einsum("bsd,df->bsf", u, w_ffn_up[l]))
        h = h + rms_norm(jnp.einsum("bsf,fd->bsd", f, w_ffn_down[l]), norm_post_ffn[l])
    return h
```

```python
import contextlib
import numpy as np
import concourse.bass as bass
import concourse.mybir as mybir
from concourse.bass_utils import run_bass_kernel_spmd

F32 = mybir.dt.float32
BF16 = mybir.dt.bfloat16
AF = mybir.ActivationFunctionType
ALU = mybir.AluOpType
AX = mybir.AxisListType

D_MODEL = 2048
NJ = 16
RW_IN = 3488
G0 = 3488
GATE0 = 3488 + 3104
FFN = 5632
C0 = -0.6065306597126334
NEG = -30000.0
DBG = {}


class Res:
    __slots__ = ("name", "w", "rs", "excl")

    def __init__(self, name="", excl=False):
        self.name = name
        self.w = None
        self.rs = []
        self.excl = excl


class V:
    __slots__ = ("ap", "res")

    def __init__(self, ap, res):
        self.ap = ap
        self.res = res

    def __getitem__(self, idx):
        return V(self.ap[idx], self.res)


class T:
    def __init__(self, h, name, excl=False):
        self.h = h
        self.res = Res(name, excl)

    def __getitem__(self, idx):
        return V(self.h[idx], self.res)

    def v(self, fn):
        return V(fn(self.h), self.res)


class Sched:
    ENG = ("pe", "dve", "act", "pool", "sp")

    def __init__(self, nc, n_dma_sems=16):
        self.nc = nc
        self.ops = {e: [] for e in self.ENG}
        self.cnt = {e: 0 for e in self.ENG}
        self.seen = {e: {} for e in self.ENG}
        self.n_dma_sems = n_dma_sems
        self.dma_use = [0] * n_dma_sems
        self.dma_rr = 0
        self.total = 0
        self.pending = {e: {} for e in self.ENG}
        self.last_ev = None
        self.last_cev = None
        self.last_g = None
        self.last_dev = None

    def barrier(self):
        allev = {}
        for e in self.ENG:
            if self.cnt[e] > 0 and e != "sp":
                allev[e] = self.cnt[e]
        for i in range(self.n_dma_sems):
            if self.dma_use[i] > 0:
                allev[("dma", i)] = 16 * self.dma_use[i]
        for e in self.ENG:
            for kk, vv in allev.items():
                if self.pending[e].get(kk, 0) < vv:
                    self.pending[e][kk] = vv

    @staticmethod
    def _add(deps, ev):
        if ev is None:
            return
        k, v = ev
        if deps.get(k, 0) < v:
            deps[k] = v

    def op(self, eng, fn, reads=(), writes=(), dma=False, ndma=1):
        if eng == "pool" and DBG.get("nopool", 0):
            eng = "dve"
        deps = {}
        for r in reads:
            self._add(deps, r.w)
            if r.excl:
                for ev in r.rs:
                    self._add(deps, ev)
        for w in writes:
            self._add(deps, w.w)
            for ev in w.rs:
                self._add(deps, ev)
        if dma:
            i = self.dma_rr
            self.dma_rr = (self.dma_rr + 1) % self.n_dma_sems
            if self.dma_use[i] > 0:
                self._add(deps, (("dma", i), 16 * self.dma_use[i]))
            self.dma_use[i] += ndma
            ev = (("dma", i), 16 * self.dma_use[i])
        else:
            self.cnt[eng] += 1
            ev = (eng, self.cnt[eng])
        if eng == "pe":
            deps.pop("pe", None)
        sm = DBG.get("serial", 0)
        if sm == 1 and self.last_ev is not None:
            self._add(deps, self.last_ev)
        if sm == 2 and not dma and self.last_cev is not None:
            self._add(deps, self.last_cev)
        grp = {5: ("act", "dve", "pool"), 6: ("pe", "act"), 7: ("pe", "dve"), 8: ("act", "dve")}.get(sm)
        if grp and not dma and eng in grp and self.last_g is not None:
            self._add(deps, self.last_g)
        if sm == 3 and dma and self.last_ev is not None:
            self._add(deps, self.last_ev)
        if sm == 3 and not dma and self.last_dev is not None:
            self._add(deps, self.last_dev)
        if self.pending[eng]:
            for kk, vv in self.pending[eng].items():
                if deps.get(kk, 0) < vv:
                    deps[kk] = vv
            self.pending[eng] = {}
        seen = self.seen[eng]
        waits = []
        for k, v in deps.items():
            if seen.get(k, 0) < v:
                seen[k] = v
                waits.append((k, v))
        self.ops[eng].append((fn, waits, ev))
        self.last_ev = ev
        grp = {5: ("act", "dve", "pool"), 6: ("pe", "act"), 7: ("pe", "dve"), 8: ("act", "dve")}.get(DBG.get("serial", 0))
        if grp and not dma and eng in grp:
            self.last_g = ev
        if dma:
            self.last_dev = ev
        else:
            self.last_cev = ev
        for r in reads:
            r.rs.append(ev)
        for w in writes:
            w.w = ev
            w.rs = []
        self.total += 1
        return ev

    def emit(self, final_waits=()):
        nc = self.nc
        sems = {}
        with contextlib.ExitStack() as st:
            for e in self.ENG:
                sems[e] = st.enter_context(nc.semaphore("s_" + e))
            for i in range(self.n_dma_sems):
                sems[("dma", i)] = st.enter_context(nc.semaphore("s_dma%d" % i))
            deps = {}
            for r in final_waits:
                self._add(deps, r.w)
            fw = list(deps.items())
            block = st.enter_context(nc.Block())

            def mk(ename):
                def body(eng):
                    for fn, waits, ev in self.ops[ename]:
                        for k, v in waits:
                            eng.wait_ge(sems[k], v)
                        ins = fn(eng)
                        k, v = ev
                        if isinstance(ins, list):
                            for i_ in ins:
                                i_.then_inc(sems[k], 16)
                        else:
                            ins.then_inc(sems[k], 16 if isinstance(k, tuple) else 1)
                    if ename == "sp":
                        for k, v in fw:
                            eng.wait_ge(sems[k], v)
                        for i in range(self.n_dma_sems):
                            if self.dma_use[i] > 0:
                                eng.wait_ge(sems[("dma", i)], 16 * self.dma_use[i])
                return body

            block.tensor(mk("pe"))
            block.vector(mk("dve"))
            block.scalar(mk("act"))
            block.gpsimd(mk("pool"))
            block.sync(mk("sp"))


class KB:
    def __init__(self, nc):
        self.nc = nc
        self.S = Sched(nc)
        self.st = contextlib.ExitStack()
        self.cur = self.st
        self.nps = 0
        self.psr = []
        self.psi = 0

    def sb(self, name, shape, dt=F32):
        return T(self.cur.enter_context(self.nc.sbuf_tensor("sb_" + name, list(shape), dt)), name)

    @contextlib.contextmanager
    def phase(self):
        old = self.cur
        with contextlib.ExitStack() as sub:
            self.cur = sub
            yield
            self.cur = old
        self.S.barrier()

    def ring(self, name, n, shape, dt=F32):
        return Ring([self.sb("%s%d" % (name, i), shape, dt) for i in range(n)])

    def init_psum(self, n=8):
        self.psr = [T(self.st.enter_context(self.nc.psum_tensor("ps%d" % i, [128, 512], F32)), "ps%d" % i, True)
                    for i in range(n)]

    def ps(self):
        t = self.psr[self.psi]
        self.psi = (self.psi + 1) % len(self.psr)
        return t

    def dram(self, name, shape, dt=F32, kind="Internal"):
        return T(self.nc.dram_tensor(name, list(shape), dt, kind=kind).ap(), name)

    @staticmethod
    def _rs(*vs):
        return [x.res for x in vs if isinstance(x, V)]

    @staticmethod
    def _a(x):
        return x.ap if isinstance(x, V) else x

    def mm(self, out, lhsT, rhs, start=True, stop=True):
        self.S.op("pe", lambda e: e.matmul(out.ap, lhsT=lhsT.ap, rhs=rhs.ap, start=start, stop=stop),
                  reads=[lhsT.res, rhs.res], writes=[out.res])

    def tr(self, out, in_, ident):
        self.S.op("pe", lambda e: e.transpose(out.ap, in_.ap, ident.ap),
                  reads=[in_.res, ident.res], writes=[out.res])

    def act(self, out, in_, func, scale=1.0, bias=0.0, eng="act"):
        a = self._a
        self.S.op(eng, lambda e: e.activation(out=out.ap, in_=in_.ap, func=func, bias=a(bias), scale=a(scale)),
                  reads=self._rs(in_, scale, bias), writes=[out.res])

    def tt(self, eng, out, a, b, op):
        self.S.op(eng, lambda e: e.tensor_tensor(out=out.ap, in0=a.ap, in1=b.ap, op=op),
                  reads=[a.res, b.res], writes=[out.res])

    def ts(self, eng, out, a, s1, s2, op0, op1=None):
        g = self._a
        if op1 is None:
            self.S.op(eng, lambda e: e.tensor_scalar(out=out.ap, in0=a.ap, scalar1=g(s1), scalar2=None, op0=op0),
                      reads=self._rs(a, s1), writes=[out.res])
        else:
            self.S.op(eng, lambda e: e.tensor_scalar(out=out.ap, in0=a.ap, scalar1=g(s1), scalar2=g(s2),
                                                     op0=op0, op1=op1),
                      reads=self._rs(a, s1, s2), writes=[out.res])

    def stt(self, eng, out, a, s, b, op0, op1):
        g = self._a
        self.S.op(eng, lambda e: e.scalar_tensor_tensor(out=out.ap, in0=a.ap, scalar=g(s), in1=b.ap,
                                                        op0=op0, op1=op1),
                  reads=self._rs(a, s, b), writes=[out.res])

    def cp(self, eng, out, a):
        if eng == "act":
            self.S.op(eng, lambda e: e.copy(out=out.ap, in_=a.ap), reads=[a.res], writes=[out.res])
        else:
            self.S.op(eng, lambda e: e.tensor_copy(out=out.ap, in_=a.ap), reads=[a.res], writes=[out.res])

    def red(self, eng, out, a, op=None):
        self.S.op(eng, lambda e: e.tensor_reduce(out=out.ap, in_=a.ap, axis=AX.X, op=op or ALU.add),
                  reads=[a.res], writes=[out.res])

    def memset(self, eng, out, val):
        self.S.op(eng, lambda e: e.memset(out.ap, val), writes=[out.res])

    def dma(self, out, in_, eng="sp"):
        self.S.op(eng, lambda e: e.dma_start(out=out.ap, in_=in_.ap), reads=[in_.res], writes=[out.res], dma=True)

    def dma_multi(self, pairs, eng="sp"):
        self.S.op(eng, lambda e: [e.dma_start(out=o.ap, in_=i.ap) for o, i in pairs],
                  reads=[i.res for o, i in pairs], writes=[o.res for o, i in pairs], dma=True, ndma=len(pairs))

    def rsqrt(self, out, in_, scale=1.0, bias=0.0):
        self.act(out, in_, AF.Ln, scale=scale, bias=bias)
        self.act(out, out, AF.Exp, scale=-0.5)


class Ring:
    def __init__(self, tiles):
        self.tiles = tiles
        self.i = 0

    def next(self):
        t = self.tiles[self.i]
        self.i = (self.i + 1) % len(self.tiles)
        return t


IDN, ONE, LS, LI, US, UI, NLS, NUS = range(8)
NCST = 8
DIRS = {
    0: dict(Ms=LS, MTs=US, MTi=UI, Ti=UI, Ts=US, Tr=LS, Ns=NLS, NTs=NUS, last=127),
    1: dict(Ms=US, MTs=LS, MTi=LI, Ti=LI, Ts=LS, Tr=US, Ns=NUS, NTs=NLS, last=0),
}


def make_consts():
    i = np.arange(128)
    ls = (i[:, None] > i[None, :]).astype(np.float32)
    li = (i[:, None] >= i[None, :]).astype(np.float32)
    us = ls.T.copy()
    ui = li.T.copy()
    c = np.zeros((128, NCST, 128), np.float32)
    c[:, IDN] = np.eye(128)
    c[:, ONE] = 1.0
    c[:, LS] = ls
    c[:, LI] = li
    c[:, US] = us
    c[:, UI] = ui
    c[:, NLS] = (ls - 1.0) * (-NEG)
    c[:, NUS] = (us - 1.0) * (-NEG)
    return c


RC_W0, RC_A0, RC_KK, RC_KA, RC_RK, RC_GNW, RC_GNB, RC_DTB, RC_ALOG, RC_GNORM = 0, 256, 512, 640, 896, 1024, 1152, 1280, 1282, 1284
NROWC = 1284 + 128
PR_GAIN, PR_MU, PR_CW = 0, 16, 23
NPRM = 23 + 15


def build_phase_a(NB, SEQ, upto='a3'):
    NTOK = NB * SEQ
    NCH = NTOK // 128
    NBLK = NTOK // 256
    CPS = SEQ // 128
    nc = bass.Bass("TRN2", target_bir_lowering=False)
    k = KB(nc)
    with k.st:
        xT = k.dram("xT", [D_MODEL, NTOK], kind="ExternalInput")
        WA = k.dram("WA", [11, 128, NJ * 128], kind="ExternalInput")
        prm_d = k.dram("prm", [128, NPRM], kind="ExternalInput")
        rowc_d = k.dram("rowc", [128, NROWC], kind="ExternalInput")
        lw_d = k.dram("lw", [128, 768], kind="ExternalInput")
        cst_d = k.dram("cst", [128, NCST * 128], kind="ExternalInput")
        yT = k.dram("yT", [256, NTOK], kind="ExternalOutput")
        rw_fm = k.dram("rw_fm", [NCH, 4, 64, 512])
        rw_tm = k.dram("rw_tm", [NCH, 128, 4 * 256])
        rw_dv = k.dram("rw_dv", [NCH, 64, 4])
        rw_post = k.dram("rw_post", [NCH, 128, 258])
        rw_y = k.dram("rw_y", [NCH, 128, 256])
        gd_fm = k.dram("gd_fm", [NCH, 2, 128, 512])
        gd_tm = k.dram("gd_tm", [NCH, 2, 128, 384])
        gd_lr = k.dram("gd_lr", [NCH, 2, 2, 256])
        gd_dv = k.dram("gd_dv", [NCH, 2, 128, 1])
        gd_post = k.dram("gd_post", [NCH, 128, 128])
        gd_o = k.dram("gd_o", [NCH, 2, 128, 128])

        k.init_psum(8)
        cst = k.sb("cst", [128, NCST, 128])
        k.dma(cst.v(lambda h: h[:].rearrange("p a b -> p (a b)")), cst_d[:, :])
        prm = k.sb("prm", [128, NPRM])
        k.dma(prm[:], prm_d[:, :])
        rowc = k.sb("rowc", [128, NROWC])
        k.dma(rowc[:], rowc_d[:, :])
        lw = k.sb("lw", [128, 768])
        k.dma(lw[:], lw_d[:, :])

        def CM(i):
            return cst[:, i, :]

        ident = CM(IDN)
        ones = CM(ONE)

        with k.phase():
            _A1R.clear()
            _A1G.clear()
            if upto != 'a0':
                _phase_a1(k, nc, NB, SEQ, NTOK, NCH, NBLK, xT, WA, prm, rowc, lw, cst, CM,
                          rw_fm, rw_tm, rw_dv, rw_post, gd_fm, gd_tm, gd_lr, gd_dv, gd_post)
        if upto in ('a2', 'a3'):
            with k.phase():
                _phase_a2(k, nc, NB, SEQ, NCH, CPS, cst, CM, rw_fm, rw_tm, rw_dv, rw_y, gd_fm, gd_tm, gd_lr, gd_dv, gd_o)
        if upto == 'a3':
            with k.phase():
                _phase_a3(k, nc, NCH, rowc, CM, rw_post, rw_y, gd_post, gd_o, yT)
        else:
            k.dma(yT[0:128, 0:128], cst[:, 0, :])
        k.S.emit(final_waits=[yT.res])
    return nc


def _phase_a1(k, nc, NB, SEQ, NTOK, NCH, NBLK, xT, WA, prm, rowc, lw, cst, CM,
              rw_fm, rw_tm, rw_dv, rw_post, gd_fm, gd_tm, gd_lr, gd_dv, gd_post):
    ident = CM(IDN)
    ones = CM(ONE)
    WAs = k.sb("WAs", [128, 11, NJ, 128], BF16)
    wstg = k.ring("wstg", 1, [128, NJ, 128])
    gain = prm[:, PR_GAIN:PR_GAIN + 16]
    for t in range(11):
        s = wstg.next()
        k.dma(s.v(lambda h: h[:].rearrange("p j c -> p (j c)")), WA[t, :, :])
        k.tt("dve" if t % 2 == 0 else "pool", WAs[:, t, :, :], s[:],
             V(gain.ap.unsqueeze(2).to_broadcast([128, NJ, 128]), gain.res), ALU.mult)
    omm = k.sb("omm", [128, 7])
    hmu = k.sb("hmu", [128, 7])
    k.ts("dve", omm[:], prm[:, PR_MU:PR_MU + 7], -1.0, 1.0, ALU.mult, ALU.add)
    k.ts("dve", hmu[:], prm[:, PR_MU:PR_MU + 7], 0.5, None, ALU.mult)
    omka = k.sb("omka", [128, 256])
    k.ts("dve", omka[:], rowc[:, RC_KA:RC_KA + 256], -1.0, 1.0, ALU.mult, ALU.add)
    nA = k.sb("nA", [128, 2])
    k.act(nA[:], rowc[:, RC_ALOG:RC_ALOG + 2], AF.Exp)
    k.ts("dve", nA[:], nA[:], -1.0, None, ALU.mult)
    c10 = k.sb("c10", [2, 2])
    k.cp("dve", c10[:], V(ident.ap[0:2, 0:2], ident.res))
    cw = prm[:, PR_CW:PR_CW + 15]

    xt_r = k.ring("xt", 1, [128, NJ, 260])
    sq_r = k.ring("sq", 1, [128, NJ, 260])
    u_r = k.ring("u", 2, [128, NJ, 260], BF16)
    rstd_r = k.ring("rstd", 2, [128, 260])
    pa_r = k.ring("pa", 3, [128, 260])
    s_r = k.ring("shs", 2, [128, 256])
    m1_r = k.ring("shm", 2, [128, 256])
    pm = [k.ring("pm%d" % t, 2, [128, 256]) for t in range(7)]
    cacc = k.ring("cacc", 2, [128, 256])
    qkv = [k.ring("qkv%d" % t, 2, [128, 256]) for t in range(3)]
    sqn = k.ring("sqn", 2, [128, 256])
    rn_r = k.ring("rnq", 2, [128, 256])

    for b in range(NBLK):
        t0 = b * 256
        seq0 = (t0 // SEQ) * SEQ
        lo = t0 - 2
        hi = t0 + 258
        xt = xt_r.next()
        c_lo, c_hi = 0, 260
        if lo < seq0:
            c_lo = 2
            k.memset("pool", xt[:, :, 0:2], 0.0)
        if hi > seq0 + SEQ:
            c_hi = 258
            k.memset("pool", xt[:, :, 258:260], 0.0)
        k.dma_multi([(xt[:, jq * 4:(jq + 1) * 4, c_lo:c_hi],
                      xT.v(lambda h, jq=jq: h[jq * 512:(jq + 1) * 512, lo + c_lo:lo + c_hi].rearrange("(j p) t -> p j t", p=128)))
                     for jq in range(4)])
        sq = sq_r.next()
        k.act(sq[:], xt[:], AF.Square)
        pss = k.ps()
        for j in range(NJ):
            k.mm(pss[:, 0:260], ones, sq[:, j, :], start=(j == 0), stop=(j == NJ - 1))
        rstd = rstd_r.next()
        k.rsqrt(rstd[:], pss[:, 0:260], scale=1.0 / D_MODEL, bias=1e-6)
        u = u_r.next()
        k.tt("dve", u[:], xt[:], V(rstd.h[:].unsqueeze(1).to_broadcast([128, NJ, 260]), rstd.res), ALU.mult)

        pmt = []
        for t in range(10):
            pp = k.ps()
            for j in range(NJ):
                k.mm(pp[:, 0:260], WAs[:, t, j, :], u[:, j, :], start=(j == 0), stop=(j == NJ - 1))
            pa = pa_r.next()
            k.cp("act", pa[:], pp[:, 0:260])
            if t < 7:
                s = s_r.next()
                k.tt("pool", s[:], pa[:, 1:257], pa[:, 3:259], ALU.add)
                m1 = m1_r.next()
                k.act(m1[:], pa[:, 2:258], AF.Copy, scale=omm[:, t:t + 1])
                o = pm[t].next()
                k.stt("dve", o[:], s[:], hmu[:, t:t + 1], m1[:], ALU.mult, ALU.add)
                pmt.append(o)
            else:
                tc_ = t - 7
                acc = cacc.next()
                k.ts("dve", acc[:], pa[:, 0:256], cw[:, tc_ * 5:tc_ * 5 + 1], None, ALU.mult)
                for kk_ in range(1, 5):
                    k.stt("dve", acc[:], pa[:, kk_:kk_ + 256], cw[:, tc_ * 5 + kk_:tc_ * 5 + kk_ + 1], acc[:],
                          ALU.mult, ALU.add)
                o = qkv[tc_].next()
                k.act(o[:], acc[:], AF.Silu)
                pmt.append(o)
        qn = []
        for qi in range(2):
            src = pmt[7 + qi]
            s2 = sqn.next()
            k.tt("pool", s2[:], src[:], src[:], ALU.mult)
            pq = k.ps()
            k.mm(pq[:, 0:256], ones, s2[:])
            rn = rn_r.next()
            k.rsqrt(rn[:], pq[:, 0:256], scale=1.0, bias=1e-6)
            if qi == 0:
                k.stt("dve", src[:], src[:], 128.0 ** -0.5, rn[:], ALU.mult, ALU.mult)
            else:
                k.tt("dve", src[:], src[:], rn[:], ALU.mult)
        for cc in range(2):
            g = b * 2 + cc
            cs = slice(cc * 128, cc * 128 + 128)
            _a1_rwkv_chunk(k, g, cs, pmt, rowc, lw, CM, omka, rw_fm, rw_tm, rw_dv, rw_post)
            _a1_gdn_chunk(k, g, cs, cc, u, WAs, pmt, rowc, CM, nA, c10, gd_fm, gd_tm, gd_lr, gd_dv, gd_post)


_A1R = {}


def _a1_rwkv_chunk(k, g, cs, pmt, rowc, lw, CM, omka, rw_fm, rw_tm, rw_dv, rw_post):
    ident = CM(IDN)
    ones = CM(ONE)
    R = _A1R
    if not R:
        R["th"] = k.ring("a1th", 2, [128, 128])
        R["sg0"] = k.ring("a1sg0", 2, [128, 128])
        R["sg1"] = k.ring("a1sg1", 2, [32, 128])
        R["rkv"] = k.ring("a1rkv", 2, [128, 384])
        R["sw"] = k.ring("a1sw", 2, [128, 256])
        R["aa"] = k.ring("a1aa", 2, [128, 256])
        R["E"] = [k.ring("a1E%d" % i, 1, [128, 256]) for i in range(4)]
        R["kx"] = k.ring("a1kx", 2, [128, 128])
        R["kx2"] = k.ring("a1kx2", 2, [128, 128])
        R["ss"] = k.ring("a1ss", 2, [128, 2])
        R["kk"] = k.ring("a1kk", 2, [128, 128])
        R["kka"] = k.ring("a1kka", 2, [128, 256])
        R["t1"] = k.ring("a1t1", 2, [128, 256])
        R["kd"] = k.ring("a1kd", 2, [128, 256])
        R["q4"] = k.ring("a1q4", 1, [128, 4, 256])
        R["tm"] = k.ring("a1tm", 1, [128, 4, 4, 64])
        R["fm"] = k.ring("a1fm", 2, [64, 512])
        R["post"] = k.ring("a1post", 2, [128, 258])
        R["bs"] = k.ring("a1bs", 2, [128, 128])
        R["dv"] = k.ring("a1dv", 2, [64, 4])
    th = R["th"].next()
    k.act(th[:], pmt[3][:, cs], AF.Tanh)
    sg0 = R["sg0"].next()
    k.act(sg0[:], pmt[5][:, cs], AF.Sigmoid)
    sg1 = R["sg1"].next()
    k.act(sg1[:], pmt[6][0:32, cs], AF.Sigmoid)
    p_aw = k.ps()
    k.mm(p_aw[:, 0:256], th[:], lw[:, 0:256], start=True, stop=False)
    k.mm(p_aw[:, 0:256], V(ones.ap[0:1, :], ones.res), rowc[0:1, RC_W0:RC_W0 + 256], start=False, stop=True)
    p_aa = k.ps()
    k.mm(p_aa[:, 0:256], pmt[4][:, cs], lw[:, 256:512], start=True, stop=False)
    k.mm(p_aa[:, 0:256], V(ones.ap[0:1, :], ones.res), rowc[0:1, RC_A0:RC_A0 + 256], start=False, stop=True)
    p_g = k.ps()
    k.mm(p_g[:, 0:128], sg0[:], lw[:, 512:640], start=True, stop=False)
    k.mm(p_g[:, 0:128], sg1[:], lw[0:32, 640:768], start=False, stop=True)
    p_t = k.ps()
    for i in range(3):
        k.tr(p_t[:, i * 128:(i + 1) * 128], pmt[i][:, cs], ident)
    rkv = R["rkv"].next()
    k.cp("act", rkv[:], p_t[:, 0:384])
    r_tm, k_tm, v_tm = rkv[:, 0:128], rkv[:, 128:256], rkv[:, 256:384]
    post = R["post"].next()
    k.cp("pool", post[:, 0:128], v_tm)
    k.cp("act", post[:, 128:256], p_g[:, 0:128])
    sw = R["sw"].next()
    k.act(sw[:], p_aw[:, 0:256], AF.Sigmoid)
    aa = R["aa"].next()
    k.act(aa[:], p_aa[:, 0:256], AF.Sigmoid)
    pL = [k.ps(), k.ps(), k.ps()]
    for d in range(2):
        dd = DIRS[d]
        for i, key in enumerate(("Ti", "Ts", "Tr")):
            k.mm(pL[i][:, d * 128:(d + 1) * 128], CM(dd[key]), sw[:, d * 128:(d + 1) * 128])
    E = [r.next() for r in R["E"]]
    k.act(E[0][:], pL[0][:, 0:256], AF.Exp, scale=C0)
    k.act(E[1][:], pL[0][:, 0:256], AF.Exp, scale=-C0)
    k.act(E[2][:], pL[1][:, 0:256], AF.Exp, scale=C0)
    k.act(E[3][:], pL[2][:, 0:256], AF.Exp, scale=C0)
    p_dv = k.ps()
    for hd in range(4):
        k.mm(p_dv[0:64, hd:hd + 1], sw[:, hd * 64:(hd + 1) * 64], V(ones.ap[:, 0:1], ones.res))
    dv = R["dv"].next()
    k.act(dv[:], p_dv[0:64, 0:4], AF.Exp, scale=C0)
    k.dma(rw_dv[g, :, :], dv[:])
    kx = R["kx"].next()
    k.tt("dve", kx[:], k_tm, rowc[:, RC_KK:RC_KK + 128], ALU.mult)
    kx2 = R["kx2"].next()
    k.tt("pool", kx2[:], kx[:], kx[:], ALU.mult)
    ss = R["ss"].next()
    k.red("dve", ss[:], kx2.v(lambda h: h[:].rearrange("p (a b) -> p a b", a=2)))
    k.rsqrt(ss[:], ss[:], scale=1.0, bias=1e-6)
    kk = R["kk"].next()
    k.tt("dve", kk.v(lambda h: h[:].rearrange("p (a b) -> p a b", a=2)),
         kx.v(lambda h: h[:].rearrange("p (a b) -> p a b", a=2)),
         V(ss.h[:].unsqueeze(2).to_broadcast([128, 2, 64]), ss.res), ALU.mult)

    def bc2(v):
        return V(v.ap.unsqueeze(1).to_broadcast([128, 2, 128]), v.res)

    def as2(t_):
        return t_.v(lambda h: h[:].rearrange("p (a b) -> p a b", a=2))

    kka = R["kka"].next()
    k.tt("dve", as2(kka), bc2(kk[:]), as2(aa), ALU.mult)
    t1 = R["t1"].next()
    k.tt("pool", t1[:], aa[:], rowc[:, RC_KA:RC_KA + 256], ALU.mult)
    k.tt("pool", t1[:], t1[:], omka[:], ALU.add)
    kd = R["kd"].next()
    k.tt("dve", as2(kd), as2(t1), bc2(k_tm), ALU.mult)
    q4 = R["q4"].next()
    tm = R["tm"].next()

    def q4v(i):
        return q4.v(lambda h: h[:, i, :].rearrange("p (a b) -> p a b", a=2))

    def tmv(i):
        return tm.v(lambda h: h[:, :, i, :])

    def as4(t_):
        return t_.v(lambda h: h[:].rearrange("p (a b) -> p a b", a=4))

    k.stt("dve", q4v(0), bc2(kk[:]), -1.0, as2(E[2]), ALU.mult, ALU.mult)
    k.tt("pool", q4v(1), bc2(r_tm), as2(E[0]), ALU.mult)
    k.tt("dve", q4v(2), as2(kka), as2(E[1]), ALU.mult)
    k.tt("pool", q4v(3), as2(kd), as2(E[1]), ALU.mult)
    k.cp("pool", tmv(0), q4.v(lambda h: h[:, 0, :].rearrange("p (a b) -> p a b", a=4)))
    k.tt("dve", tmv(1), as4(kka), as4(E[3]), ALU.mult)
    k.tt("pool", tmv(2), as4(kd), as4(E[3]), ALU.mult)
    k.cp("pool", tm.v(lambda h: h[:, :, 3, :].rearrange("p (d h) b -> p d h b", d=2)), V(_vdup(v_tm.ap), v_tm.res))
    k.dma(rw_tm[g, :, :], tm.v(lambda h: h[:].rearrange("p a b c -> p (a b c)")))
    bs = R["bs"].next()
    k.tt("pool", bs[:], kd[:, 0:128], kd[:, 128:256], ALU.add)
    k.tt("pool", bs[:], bs[:], r_tm, ALU.mult)
    k.stt("dve", bs[:], bs[:], 0.5, rowc[:, RC_RK:RC_RK + 128], ALU.mult, ALU.mult)
    k.red("dve", post[:, 256:258], bs.v(lambda h: h[:].rearrange("p (a b) -> p a b", a=2)))
    k.dma(rw_post[g, :, :], post[:])
    for hd in range(4):
        pf = k.ps()
        for i in range(4):
            k.tr(pf[0:64, i * 128:(i + 1) * 128], q4[:, i, hd * 64:(hd + 1) * 64], ident)
        fm = R["fm"].next()
        k.cp("act" if hd % 2 == 0 else "dve", fm[:], pf[0:64, 0:512])
        k.dma(rw_fm[g, hd, :, :], fm[:])


def _vdup(ap):
    return ap.rearrange("p (h b) -> p h b", h=2).unsqueeze(1).to_broadcast([128, 2, 2, 64])


_A1G = {}


def _a1_gdn_chunk(k, g, cs, cc, u, WAs, pmt, rowc, CM, nA, c10, gd_fm, gd_tm, gd_lr, gd_dv, gd_post):
    ident = CM(IDN)
    ones = CM(ONE)
    R = _A1G
    if not R:
        R["sz"] = k.ring("g1sz", 2, [128, 128])
        R["kv"] = k.ring("g1kv", 2, [128, 256])
        R["t4"] = k.ring("g1t4", 2, [128, 4])
        R["gb"] = k.ring("g1gb", 2, [128, 6])
        R["gn2"] = k.ring("g1gn2", 2, [128, 4])
        R["bc"] = k.ring("g1bc", 2, [128, 4, 128])
        R["eg"] = k.ring("g1eg", 2, [128, 128])
        R["fm"] = k.ring("g1fm", 2, [128, 512])
        R["tm"] = k.ring("g1tm", 2, [128, 384])
        R["ec"] = k.ring("g1ec", 2, [128, 2])
        R["sc"] = k.ring("g1sc", 2, [128, 1])
        R["lr"] = k.ring("g1lr", 2, [2, 256])
        R["dv"] = k.ring("g1dv", 2, [128, 1])
    pz = k.ps()
    for j in range(NJ):
        k.mm(pz[:, 0:128], u[:, j, 2 + cc * 128:2 + cc * 128 + 128], WAs[:, 10, j, :], start=(j == 0), stop=(j == NJ - 1))
    sz = R["sz"].next()
    k.act(sz[:], pz[:, 0:128], AF.Silu)
    k.dma(gd_post[g, :, :], sz[:])
    pt = k.ps()
    k.tr(pt[:, 0:128], pmt[8][:, cs], ident)
    k.tr(pt[:, 128:256], pmt[9][:, cs], ident)
    kv = R["kv"].next()
    k.cp("act", kv[:], pt[:, 0:256])
    k_tm, v_tm = kv[:, 0:128], kv[:, 128:256]
    p4 = k.ps()
    k.tr(p4[:, 0:4], pmt[6][32:36, cs], V(ident.ap[32:36, 32:36], ident.res))
    t4 = R["t4"].next()
    k.tt("dve", t4[:, 0:2], p4[:, 0:2], rowc[:, RC_DTB:RC_DTB + 2], ALU.add)
    gb = R["gb"].next()
    k.act(t4[:, 0:2], t4[:, 0:2], AF.Exp)
    k.act(t4[:, 0:2], t4[:, 0:2], AF.Ln, bias=1.0)
    k.tt("dve", gb[:, 0:2], t4[:, 0:2], nA[:], ALU.mult)
    k.act(gb[:, 2:4], p4[:, 2:4], AF.Sigmoid)
    gn2 = R["gn2"].next()
    k.cp("pool", gn2.v(lambda h: h[:].rearrange("p (d s) -> p d s", s=2)[:, :, 0]), gb[:, 0:2])
    k.ts("dve", gn2.v(lambda h: h[:].rearrange("p (d s) -> p d s", s=2)[:, :, 1]), gb[:, 0:2], -1.0, None, ALU.mult)
    bc = R["bc"].next()
    k.cp("pool", bc[:], V(gb.h[:, 0:4].unsqueeze(2).to_broadcast([128, 4, 128]), gb.res))
    for d in range(2):
        dd = DIRS[d]
        fm = R["fm"].next()
        tm = R["tm"].next()
        pb = k.ps()
        k.mm(pb[:, 0:128], bc[:, 2 + d, :], ident)
        k.mm(pb[:, 128:256], bc[:, d, :], CM(dd["Ti"]))
        eg = R["eg"].next()
        k.act(eg[:], pb[:, 128:256], AF.Exp)
        k.cp("pool", fm[:, 0:128], pmt[8][:, cs])
        k.cp("pool", fm[:, 128:256], pmt[7][:, cs])
        k.tt("dve", fm[:, 256:384], pmt[8][:, cs], pb[:, 0:128], ALU.mult)
        k.tt("pool", fm[:, 384:512], pmt[7][:, cs], eg[:], ALU.mult)
        k.dma(gd_fm[g, d, :, :], fm[:])
        dv = R["dv"].next()
        k.cp("act", dv[:], eg[:, dd["last"]:dd["last"] + 1])
        k.dma(gd_dv[g, d, :, :], dv[:])
        pc = k.ps()
        k.mm(pc[:, 0:1], CM(dd["Ti"]), gb[:, d:d + 1])
        k.mm(pc[:, 1:2], CM(dd["Tr"]), gb[:, d:d + 1])
        ec = R["ec"].next()
        k.act(ec[:], pc[:, 0:2], AF.Exp)
        sc = R["sc"].next()
        k.tt("dve", sc[:], ec[:, 0:1], gb[:, 2 + d:3 + d], ALU.mult)
        k.ts("dve", tm[:, 0:128], v_tm, gb[:, 2 + d:3 + d], None, ALU.mult)
        k.ts("pool", tm[:, 128:256], k_tm, sc[:, 0:1], None, ALU.mult)
        k.ts("dve", tm[:, 256:384], k_tm, ec[:, 1:2], None, ALU.mult)
        k.dma(gd_tm[g, d, :, :], tm[:])
        pr = k.ps()
        k.mm(pr[0:2, 0:128], gn2[:, 2 * d:2 * d + 2], CM(dd["Ti"]))
        lr = R["lr"].next()
        k.ts("dve", lr[:, 0:128], pr[0:2, 0:128], c10[:, 0:1], c10[:, 1:2], ALU.mult, ALU.add)
        k.ts("dve", lr[:, 128:256], pr[0:2, 0:128], c10[:, 1:2], c10[:, 0:1], ALU.mult, ALU.add)
        k.dma(gd_lr[g, d, :, :], lr[:])


def _invert(k, R, A0, N0, CM, tag):
    ident = CM(IDN)
    TN = [R["TN0"].next(), R["TN1"].next()]
    Ak = [R["Ak0"].next(), R["Ak1"].next()]
    k.tt("pool", TN[0][:, 0:128], N0, ident, ALU.add)
    if DBG.get("lv", 8) == 1:
        return TN[0][:, 0:128]
    p = k.ps()
    iv = DBG.get("iv", 15)
    if iv & 1:
        k.mm(p[:, 0:128], A0, N0)
    if iv & 2:
        k.mm(p[:, 128:256], N0, A0)
    if iv & 4:
        k.cp("act", TN[0][:, 128:256], p[:, 0:128])
    if iv & 8:
        k.cp("dve", Ak[0][:], p[:, 128:256])
    cur = 0
    if DBG.get("lv", 8) == 0:
        return TN[0][:, 0:128]
    for lv in range(2, DBG.get("lv", 8)):
        a_prev = Ak[cur]
        tn_prev = TN[cur]
        tn_new = TN[1 - cur]
        p = k.ps()
        if lv < 7:
            k.mm(p[:, 0:256], a_prev[:], tn_prev[:, 0:256])
            p2 = k.ps()
            k.mm(p2[:, 0:128], tn_prev[:, 128:256], a_prev[:])
            k.tt("dve", tn_new[:, 0:128], tn_prev[:, 0:128], p[:, 0:128], ALU.add)
            k.cp("act", tn_new[:, 128:256], p[:, 128:256])
            k.cp("act", Ak[1 - cur][:], p2[:, 0:128])
        else:
            k.mm(p[:, 0:128], a_prev[:], tn_prev[:, 0:128])
            k.tt("dve", tn_new[:, 0:128], tn_prev[:, 0:128], p[:, 0:128], ALU.add)
        cur = 1 - cur
    return TN[cur][:, 0:128]


def _phase_a2(k, nc, NB, SEQ, NCH, CPS, cst, CM, rw_fm, rw_tm, rw_dv, rw_y, gd_fm, gd_tm, gd_lr, gd_dv, gd_o):
    ident = CM(IDN)
    NRW = NB * 4
    NGD = NB * 2
    Zrw = [[k.sb("zrw%d_%d" % (s, i), [64, 64]) for i in range(2)] for s in range(NRW)]
    Zgd = [[k.sb("zgd%d_%d" % (s, i), [128, 128]) for i in range(2)] for s in range(NGD)]
    for s in range(NRW):
        k.memset("pool", Zrw[s][0][:], 0.0)
    for s in range(NGD):
        k.memset("pool", Zgd[s][0][:], 0.0)
    NR = 3
    R = dict(
        TN0=k.ring("TN0", NR, [128, 256]), TN1=k.ring("TN1", NR, [128, 256]),
        Ak0=k.ring("Ak0", NR, [128, 128]), Ak1=k.ring("Ak1", NR, [128, 128]),
        fm=k.ring("s_fm", NR, [128, 512]), tm=k.ring("s_tm", NR, [128, 384]),
        rtm=k.ring("s_rtm", NR, [128, 256]), rfm=k.ring("s_rfm", NR, [64, 512]),
        dv=k.ring("s_dv", NR, [128, 4]), lr=k.ring("s_lr", NR, [2, 256]),
        A0=k.ring("s_A0", NR, [128, 128]), NB_=k.ring("s_NB", NR, [128, 256]), CK=k.ring("s_CK", NR, [128, 256]),
        D=k.ring("s_D", NR, [128, 384]), m=k.ring("s_m", NR, [128, 256]),
        akv=k.ring("s_akv", NR, [128, 128]), U=k.ring("s_U", NR, [128, 128]), WT=k.ring("s_WT", NR, [128, 128]),
        X=k.ring("s_X", NR, [128, 128]), O=k.ring("s_O", NR, [128, 128]),
    )
    for step in range(CPS):
        for bi in range(NB):
            for d in (range(2) if DBG.get("rw", 1) else []):
                dd = DIRS[d]
                ci = step if d == 0 else CPS - 1 - step
                g = bi * CPS + ci
                rdv = R["dv"].next()
                k.dma(rdv[0:64, 0:4], rw_dv[g, :, :])
                for h in range(2):
                    hd = d * 2 + h
                    s = bi * 4 + hd
                    par = step % 2
                    Zo, Zn = Zrw[s][par], Zrw[s][1 - par]
                    fm = R["rfm"].next()
                    k.dma(fm[:], rw_fm[g, hd, :, :])
                    tm = R["rtm"].next()
                    k.dma(tm[:], rw_tm.v(lambda hh: hh[g, :, hd * 256:(hd + 1) * 256]))
                    aT, rT, bT, kT = fm[:, 0:128], fm[:, 128:256], fm[:, 256:384], fm[:, 384:512]
                    A_tm, Bh, Kh, Vt = tm[:, 0:64], tm[:, 64:128], tm[:, 128:192], tm[:, 192:256]
                    pA = k.ps()
                    k.mm(pA[:, 0:128], aT, bT)
                    pB = k.ps()
                    k.mm(pB[:, 0:256], bT, fm[:, 0:256])
                    pC = k.ps()
                    k.mm(pC[:, 0:256], kT, fm[:, 0:256])
                    A0 = R["A0"].next()
                    k.tt("dve", A0[:], pA[:, 0:128], CM(dd["Ms"]), ALU.mult)
                    msk = V(cst.h[:, dd["MTs"]:dd["MTs"] + 2, :].rearrange("p a b -> p (a b)"), cst.res)
                    NBt = R["NB_"].next()
                    k.tt("dve", NBt[:], pB[:, 0:256], msk, ALU.mult)
                    CK = R["CK"].next()
                    k.tt("dve", CK[:], pC[:, 0:256], msk, ALU.mult)
                    TT = _invert(k, R, A0[:], NBt[:, 0:128], CM, "rw")
                    p1 = k.ps()
                    k.mm(p1[:, 0:64], CK[:, 0:128], Vt)
                    akv = R["akv"].next()
                    k.cp("act", akv[:, 0:64], p1[:, 0:64])
                    p2 = k.ps()
                    k.mm(p2[:, 0:64], TT, akv[:, 0:64])
                    k.mm(p2[0:64, 128:256], A_tm, TT)
                    U = R["U"].next()
                    k.cp("act", U[:, 0:64], p2[:, 0:64])
                    WT = R["WT"].next()
                    k.cp("dve", WT[0:64, :], p2[0:64, 128:256])
                    pX = k.ps()
                    k.mm(pX[:, 0:64], WT[0:64, :], Zo[:])
                    X = R["X"].next()
                    k.tt("dve", X[:, 0:64], pX[:, 0:64], U[:, 0:64], ALU.add)
                    pZ = k.ps()
                    k.mm(pZ[0:64, 0:64], Kh, Vt, start=True, stop=False)
                    k.mm(pZ[0:64, 0:64], Bh, X[:, 0:64], start=False, stop=True)
                    pO = k.ps()
                    k.mm(pO[:, 0:64], rT, Zo[:], start=True, stop=False)
                    k.mm(pO[:, 0:64], NBt[:, 128:256], X[:, 0:64], start=False, stop=False)
                    k.mm(pO[:, 0:64], CK[:, 128:256], Vt, start=False, stop=True)
                    k.stt("dve", Zn[:], Zo[:], rdv[0:64, hd:hd + 1], pZ[0:64, 0:64], ALU.mult, ALU.add)
                    O = R["O"].next()
                    k.cp("act", O[:, 0:64], pO[:, 0:64])
                    k.dma(rw_y.v(lambda hh: hh[g, :, hd * 64:(hd + 1) * 64]), O[:, 0:64])
            for d in (range(2) if DBG.get("gd", 1) else []):
                dd = DIRS[d]
                ci = step if d == 0 else CPS - 1 - step
                g = bi * CPS + ci
                s = bi * 2 + d
                par = step % 2
                Zo, Zn = Zgd[s][par], Zgd[s][1 - par]
                fm = R["fm"].next()
                k.dma(fm[:], gd_fm[g, d, :, :])
                tm = R["tm"].next()
                k.dma(tm[:], gd_tm[g, d, :, :])
                lr = R["lr"].next()
                k.dma(lr[:], gd_lr[g, d, :, :])
                gdv = R["dv"].next()
                k.dma(gdv[:, 0:1], gd_dv[g, d, :, :])
                if DBG.get("cut", 9) <= 0:
                    continue
                kT, qT, kbT, qgT = fm[:, 0:128], fm[:, 128:256], fm[:, 256:384], fm[:, 384:512]
                vb, kbg, kg = tm[:, 0:128], tm[:, 128:256], tm[:, 256:384]
                pt = k.ps()
                k.mm(pt[:, 0:128], lr[:, 0:128], lr[:, 128:256])
                k.mm(pt[:, 128:256], lr[:, 128:256], lr[:, 0:128])
                m = R["m"].next()
                k.stt("dve", m[:, 0:128], pt[:, 0:128], 0.0, CM(dd["Ns"]), ALU.min, ALU.add)
                k.stt("dve", m[:, 128:256], pt[:, 128:256], 0.0, CM(dd["NTs"]), ALU.min, ALU.add)
                D = R["D"].next()
                k.act(D[:, 0:256], m[:, 0:256], AF.Exp)
                k.tt("pool", D[:, 256:384], D[:, 128:256], ident, ALU.add)
                pA = k.ps()
                k.mm(pA[:, 0:128], kbT, kT)
                pB = k.ps()
                k.mm(pB[:, 0:256], kT, fm[:, 128:384])
                A0 = R["A0"].next()
                k.stt("dve", A0[:], pA[:, 0:128], -1.0, D[:, 0:128], ALU.mult, ALU.mult)
                NBt = R["NB_"].next()
                k.stt("dve", NBt[:, 0:128], pB[:, 128:256], -1.0, D[:, 128:256], ALU.mult, ALU.mult)
                k.tt("dve", NBt[:, 128:256], pB[:, 0:128], D[:, 256:384], ALU.mult)
                if DBG.get("cut", 9) <= 1:
                    continue
                TT = _invert(k, R, A0[:], NBt[:, 0:128], CM, "gd") if DBG.get("cut", 9) > 2 else NBt[:, 0:128]
                if DBG.get("cut", 9) <= 3:
                    continue
                p2 = k.ps()
                k.mm(p2[:, 0:128], TT, vb)
                k.mm(p2[:, 128:256], kbg, TT)
                U = R["U"].next()
                k.cp("act", U[:], p2[:, 0:128])
                WT = R["WT"].next()
                k.ts("dve", WT[:], p2[:, 128:256], -1.0, None, ALU.mult)
                pX = k.ps()
                k.mm(pX[:, 0:128], WT[:], Zo[:])
                X = R["X"].next()
                k.tt("dve", X[:], pX[:, 0:128], U[:], ALU.add)
                pZ = k.ps()
                k.mm(pZ[:, 0:128], kg, X[:])
                pO = k.ps()
                k.mm(pO[:, 0:128], qgT, Zo[:], start=True, stop=False)
                k.mm(pO[:, 0:128], NBt[:, 128:256], X[:], start=False, stop=True)
                k.stt("dve", Zn[:], Zo[:], gdv[:, 0:1], pZ[:, 0:128], ALU.mult, ALU.add)
                O = R["O"].next()
                k.cp("act", O[:], pO[:, 0:128])
                k.dma(gd_o[g, d, :, :], O[:])


def _phase_a3(k, nc, NCH, rowc, CM, rw_post, rw_y, gd_post, gd_o, yT):
    ident = CM(IDN)
    yt_r = k.ring("a3y", 2, [128, 256])
    po_r = k.ring("a3po", 2, [128, 258])
    y_r = k.ring("a3ys", 2, [128, 128])
    c_r = k.ring("a3c", 2, [128, 128])
    st_r = k.ring("a3st", 2, [128, 4])
    go_r = k.ring("a3go", 2, [128, 256])
    sz_r = k.ring("a3sz", 2, [128, 128])
    o_r = k.ring("a3o", 2, [128, 128])
    yo_r = k.ring("a3yo", 2, [128, 256])

    def h2(v):
        return V(v.ap.rearrange("p (a b) -> p a b", a=2), v.res)

    def bch(v):
        return V(v.ap.unsqueeze(2).to_broadcast([128, 2, 64]), v.res)

    for g in range(NCH):
        yt = yt_r.next()
        k.dma(yt[:], rw_y[g, :, :])
        po = po_r.next()
        k.dma(po[:], rw_post[g, :, :])
        y = y_r.next()
        k.tt("pool", y[:], yt[:, 0:128], yt[:, 128:256], ALU.add)
        st = st_r.next()
        k.red("dve", st[:, 0:2], h2(y[:]))
        k.ts("dve", st[:, 0:2], st[:, 0:2], 1.0 / 64, None, ALU.mult)
        c = c_r.next()
        k.tt("dve", h2(c[:]), h2(y[:]), bch(st[:, 0:2]), ALU.subtract)
        k.tt("pool", y[:], c[:], c[:], ALU.mult)
        k.red("dve", st[:, 2:4], h2(y[:]))
        k.rsqrt(st[:, 2:4], st[:, 2:4], scale=1.0 / 64, bias=64e-5)
        k.tt("dve", h2(c[:]), h2(c[:]), bch(st[:, 2:4]), ALU.mult)
        k.tt("pool", c[:], c[:], rowc[:, RC_GNW:RC_GNW + 128], ALU.mult)
        k.tt("pool", c[:], c[:], rowc[:, RC_GNB:RC_GNB + 128], ALU.add)
        k.tt("dve", h2(y[:]), h2(po[:, 0:128]), bch(po[:, 256:258]), ALU.mult)
        k.tt("pool", c[:], c[:], y[:], ALU.add)
        k.tt("dve", c[:], c[:], po[:, 128:256], ALU.mult)
        go = go_r.next()
        k.dma(go.v(lambda h: h[:].rearrange("p (d c) -> p d c", d=2)),
              gd_o.v(lambda h: h[g, :, :, :].rearrange("d p c -> p d c")))
        sz = sz_r.next()
        k.dma(sz[:], gd_post[g, :, :])
        o = o_r.next()
        k.tt("pool", o[:], go[:, 0:128], go[:, 128:256], ALU.add)
        o2 = o_r.next()
        k.tt("pool", o2[:], o[:], o[:], ALU.mult)
        st2 = st_r.next()
        k.red("dve", st2[:, 0:1], o2[:])
        k.rsqrt(st2[:, 0:1], st2[:, 0:1], scale=1.0 / 128, bias=1e-6)
        k.ts("dve", o[:], o[:], st2[:, 0:1], None, ALU.mult)
        k.tt("pool", o[:], o[:], rowc[:, RC_GNORM:RC_GNORM + 128], ALU.mult)
        k.tt("dve", o[:], o[:], sz[:], ALU.mult)
        p = k.ps()
        k.tr(p[:, 0:128], c[:], ident)
        k.tr(p[:, 128:256], o[:], ident)
        yo = yo_r.next()
        k.cp("act", yo[:], p[:, 0:256])
        k.dma_multi([(yT[0:128, g * 128:(g + 1) * 128], yo[:, 0:128]),
                     (yT[128:256, g * 128:(g + 1) * 128], yo[:, 128:256])])


def phase_a_inputs(c, inp, consts):
    W = inp["w_in"][0]
    rc = np.arange(128 * c, 128 * c + 128)
    qh = c // 2
    cols = [rc, 1024 + rc, 2048 + rc, np.arange(3072, 3200), np.arange(3200, 3328), np.arange(3328, 3456),
            np.concatenate([np.arange(3456, 3488), G0 + 3072 + np.array([c, 8 + c, 16 + c, 24 + c])]),
            G0 + qh * 128 + np.arange(128), G0 + 512 + qh * 128 + np.arange(128),
            G0 + 1024 + c * 128 + np.arange(128), G0 + 2048 + c * 128 + np.arange(128)]
    WA = np.zeros((11, 128, NJ, 128), np.float32)
    for t, cl in enumerate(cols):
        WA[t, :, :, :len(cl)] = W[:, cl].reshape(NJ, 128, len(cl)).transpose(1, 0, 2)
    prm = np.zeros((128, NPRM), np.float32)
    prm[:, PR_GAIN:PR_GAIN + 16] = inp["norm_pre_mix"][0].reshape(NJ, 128).T
    mu = inp["rw_shift_mu"][0]
    for t in range(7):
        cl = cols[t]
        n = min(len(cl), 128)
        if t == 6:
            prm[:32, PR_MU + t] = mu[cl[:32]]
        else:
            prm[:n, PR_MU + t] = mu[cl]
    cwh = inp["gdn_conv_w"][0]
    for t, base in enumerate([qh * 128, 512 + qh * 128, 1024 + c * 128]):
        prm[:, PR_CW + t * 5:PR_CW + t * 5 + 5] = cwh[:, base:base + 128].T
    rowc = np.zeros((128, NROWC), np.float32)

    def row(v):
        return np.broadcast_to(np.asarray(v, np.float32)[None, :], (128, len(v)))

    rowc[:, RC_W0:RC_W0 + 256] = row(np.concatenate([inp["rw_w0_f"][0][rc], inp["rw_w0_b"][0][rc]]))
    rowc[:, RC_A0:RC_A0 + 256] = row(np.concatenate([inp["rw_a0_f"][0][rc], inp["rw_a0_b"][0][rc]]))
    rowc[:, RC_KK:RC_KK + 128] = row(inp["rw_k_k"][0][rc])
    rowc[:, RC_KA:RC_KA + 256] = row(np.concatenate([inp["rw_k_a"][0][rc]] * 2))
    rowc[:, RC_RK:RC_RK + 128] = row(inp["rw_r_k"][0].reshape(-1)[rc])
    rowc[:, RC_GNW:RC_GNW + 128] = row(inp["rw_gn_w"][0][rc])
    rowc[:, RC_GNB:RC_GNB + 128] = row(inp["rw_gn_b"][0][rc])
    rowc[:, RC_DTB:RC_DTB + 2] = row(np.array([inp["gdn_dt_bias_f"][0][c], inp["gdn_dt_bias_b"][0][c]]))
    rowc[:, RC_ALOG:RC_ALOG + 2] = row(np.array([inp["gdn_a_log_f"][0][c], inp["gdn_a_log_b"][0][c]]))
    rowc[:, RC_GNORM:RC_GNORM + 128] = row(inp["gdn_norm_w"][0])
    lw = np.zeros((128, 768), np.float32)
    lw[0:64, 0:128] = inp["rw_w2_f"][0][:, rc]
    lw[64:128, 128:256] = inp["rw_w2_b"][0][:, rc]
    lw[0:64, 256:384] = inp["rw_a2_f"][0][:, rc]
    lw[64:128, 384:512] = inp["rw_a2_b"][0][:, rc]
    lw[:, 512:640] = inp["rw_g2"][0][0:128, rc]
    lw[0:32, 640:768] = inp["rw_g2"][0][128:160, rc]
    return dict(WA=WA.reshape(11, 128, NJ * 128), prm=prm, rowc=rowc, lw=lw, cst=consts.reshape(128, NCST * 128))


def build_phase_b(TB):
    N = min(512, TB)
    NBK = TB // N
    nc = bass.Bass("TRN2", target_bir_lowering=False)
    k = KB(nc)
    with k.st:
        xTs = k.dram("xTs", [D_MODEL, TB], kind="ExternalInput")
        yTs = k.dram("yTs", [D_MODEL, TB], kind="ExternalInput")
        WG = k.dram("WG", [32, 128, 2048], kind="ExternalInput")
        WP = k.dram("WP", [16, 128, 2048], kind="ExternalInput")
        WO = k.dram("WO", [16, 128, 2048], kind="ExternalInput")
        WFG = k.dram("WFG", [44, 128, 2048], kind="ExternalInput")
        WFU = k.dram("WFU", [44, 128, 2048], kind="ExternalInput")
        WD = k.dram("WD", [16, 128, 44 * 128], kind="ExternalInput")
        gn_d = k.dram("gn", [128, 64], kind="ExternalInput")
        cst_d = k.dram("cst", [128, NCST * 128], kind="ExternalInput")
        outT = k.dram("outT", [D_MODEL, TB], kind="ExternalOutput")
        _phase_b(k, nc, TB, N, NBK, xTs, yTs, WG, WP, WO, WFG, WFU, WD, gn_d, cst_d, outT)
        k.S.emit(final_waits=[outT.res])
    return nc


def _phase_b(k, nc, TB, N, NBK, xTs, yTs, WG, WP, WO, WFG, WFU, WD, gn_d, cst_d, outT):
    k.psr = [T(k.st.enter_context(nc.psum_tensor("psb%d" % i, [128, 512], F32)), "psb%d" % i, True) for i in range(7)]
    k.psi = 0
    psn = T(k.st.enter_context(nc.psum_tensor("psn", [128, 512], F32)), "psn", True)
    ones = k.sb("b_ones", [128, 128])
    k.dma(ones[:], cst_d[:, ONE * 128:(ONE + 1) * 128])
    gn = k.sb("b_gn", [128, 64])
    k.dma(gn[:], gn_d[:, :])
    xt = k.sb("b_xt", [128, NJ, N])
    u = k.sb("b_u", [128, NJ, N], BF16)
    ybf = k.sb("b_ybf", [128, NJ, N], BF16)
    mg = k.sb("b_mg", [128, NJ, N], BF16)
    o = k.sb("b_o", [128, NJ, N])
    f = k.sb("b_f", [128, 44, N], BF16)
    ystg = k.ring("b_ystg", 2, [128, N])
    sqt = k.ring("b_sqt", 2, [128, N])
    rstd = k.sb("b_rstd", [128, N])
    sg = k.ring("b_sg", 4, [128, N])
    mt = k.ring("b_mt", 2, [128, N])
    wstg = k.ring("b_wstg", 2, [128, NJ, 128])
    wb = k.ring("b_wb", 3, [128, NJ, 128], BF16)
    cnt = [0]

    def unit(src_v, nj, gain_col=None):
        s = wstg.next()
        k.dma(s.v(lambda h: h[:, 0:nj, :].rearrange("p j c -> p (j c)")), src_v)
        w = wb.next()
        eng = "pool" if cnt[0] % 2 == 0 else "dve"
        cnt[0] += 1
        if gain_col is None:
            k.cp(eng, w[:, 0:nj, :], s[:, 0:nj, :])
        else:
            gcol = gn[:, gain_col:gain_col + nj]
            k.tt(eng, w[:, 0:nj, :], s[:, 0:nj, :],
                 V(gcol.ap.unsqueeze(2).to_broadcast([128, nj, 128]), gcol.res), ALU.mult)
        return w

    def bc_j(t_):
        return V(t_.h[:].unsqueeze(1).to_broadcast([128, NJ, N]), t_.res)

    def norm_stats(src_fn, nrow):
        for r in range(nrow):
            s2 = sqt.next()
            k.act(s2[:], src_fn(r), AF.Square)
            k.mm(psn[:, 0:N], ones[:], s2[:], start=(r == 0), stop=(r == nrow - 1))
        k.rsqrt(rstd[:], psn[:, 0:N], scale=1.0 / D_MODEL, bias=1e-6)

    for bk in range(NBK):
        ts_ = slice(bk * N, (bk + 1) * N)
        k.dma_multi([(xt[:, jq * 4:(jq + 1) * 4, :],
                      xTs.v(lambda h, jq=jq: h[jq * 512:(jq + 1) * 512, ts_].rearrange("(j p) t -> p j t", p=128)))
                     for jq in range(4)])
        norm_stats(lambda r: xt[:, r, :], NJ)
        k.tt("dve", u[:], xt[:], bc_j(rstd), ALU.mult)
        for j in range(NJ):
            ys = ystg.next()
            k.dma(ys[:], yTs[j * 128:(j + 1) * 128, ts_])
            k.cp("pool", ybf[:, j, :], ys[:])
        for r in range(NJ):
            sgs = []
            for gi in range(2):
                w = unit(WG[gi * 16 + r, :, :], NJ, gain_col=0)
                p = k.ps()
                for j in range(NJ):
                    k.mm(p[:, 0:N], w[:, j, :], u[:, j, :], start=(j == 0), stop=(j == NJ - 1))
                s_ = sg.next()
                k.act(s_[:], p[:, 0:N], AF.Sigmoid)
                sgs.append(s_)
            w = unit(WP[r, :, :], NJ)
            pa = k.ps()
            pb = k.ps()
            for j in range(8):
                k.mm(pa[:, 0:N], w[:, j, :], ybf[:, j, :], start=(j == 0), stop=(j == 7))
            for j in range(8, 16):
                k.mm(pb[:, 0:N], w[:, j, :], ybf[:, j, :], start=(j == 8), stop=(j == 15))
            m1 = mt.next()
            k.tt("dve", m1[:], pa[:, 0:N], sgs[0][:], ALU.mult)
            m2 = mt.next()
            k.tt("dve", m2[:], pb[:, 0:N], sgs[1][:], ALU.mult)
            k.tt("pool", mg[:, r, :], m1[:], m2[:], ALU.add)
        for r in range(NJ):
            w = unit(WO[r, :, :], NJ)
            p = k.ps()
            for j in range(NJ):
                k.mm(p[:, 0:N], w[:, j, :], mg[:, j, :], start=(j == 0), stop=(j == NJ - 1))
            k.cp("act", o[:, r, :], p[:, 0:N])
        norm_stats(lambda r: o[:, r, :], NJ)
        for r in range(NJ):
            m1 = mt.next()
            k.tt("dve", m1[:], o[:, r, :], rstd[:], ALU.mult)
            k.stt("dve", xt[:, r, :], m1[:], gn[:, 16 + r:17 + r], xt[:, r, :], ALU.mult, ALU.add)
        norm_stats(lambda r: xt[:, r, :], NJ)
        k.tt("dve", u[:], xt[:], bc_j(rstd), ALU.mult)
        for r in range(44):
            w = unit(WFG[r, :, :], NJ, gain_col=32)
            pg = k.ps()
            for j in range(NJ):
                k.mm(pg[:, 0:N], w[:, j, :], u[:, j, :], start=(j == 0), stop=(j == NJ - 1))
            w2 = unit(WFU[r, :, :], NJ, gain_col=32)
            pu = k.ps()
            for j in range(NJ):
                k.mm(pu[:, 0:N], w2[:, j, :], u[:, j, :], start=(j == 0), stop=(j == NJ - 1))
            s_ = sg.next()
            k.act(s_[:], pg[:, 0:N], AF.Silu)
            k.tt("dve", f[:, r, :], s_[:], pu[:, 0:N], ALU.mult)
        for r in range(NJ):
            p = k.ps()
            j0 = 0
            for piece in (16, 16, 12):
                w = unit(WD.v(lambda h: h[r, :, j0 * 128:(j0 + piece) * 128]), piece)
                for jj in range(piece):
                    j = j0 + jj
                    k.mm(p[:, 0:N], w[:, jj, :], f[:, j, :], start=(j == 0), stop=(j == 43))
                j0 += piece
            k.cp("act", o[:, r, :], p[:, 0:N])
        norm_stats(lambda r: o[:, r, :], NJ)
        for r in range(NJ):
            m1 = mt.next()
            k.tt("dve", m1[:], o[:, r, :], rstd[:], ALU.mult)
            k.stt("dve", xt[:, r, :], m1[:], gn[:, 48 + r:49 + r], xt[:, r, :], ALU.mult, ALU.add)
        k.dma_multi([(outT.v(lambda h, jq=jq: h[jq * 512:(jq + 1) * 512, ts_].rearrange("(j p) t -> p j t", p=128)),
                      xt[:, jq * 4:(jq + 1) * 4, :]) for jq in range(4)])


def _tiles(W, ktiles):
    K_, R_ = W.shape
    return np.ascontiguousarray(W.reshape(ktiles, 128, R_ // 128, 128).transpose(2, 1, 0, 3)).reshape(R_ // 128, 128, ktiles * 128)


def phase_b_weights(inp, consts):
    W = inp["w_in"][0]
    d = {}
    d["WG"] = _tiles(W[:, GATE0:GATE0 + 4096], 16)
    wp = np.concatenate([inp["w_branch_rw"][0], inp["w_branch_gdn"][0]], axis=0)
    d["WP"] = _tiles(wp, 16)
    d["WO"] = _tiles(inp["w_out"][0], 16)
    d["WFG"] = _tiles(inp["w_ffn_gate"][0], 16)
    d["WFU"] = _tiles(inp["w_ffn_up"][0], 16)
    d["WD"] = _tiles(inp["w_ffn_down"][0], 44)
    gn = np.zeros((128, 64), np.float32)
    for i, nm in enumerate(["norm_pre_mix", "norm_post_mix", "norm_pre_ffn", "norm_post_ffn"]):
        gn[:, i * 16:(i + 1) * 16] = inp[nm][0].reshape(NJ, 128).T
    d["gn"] = gn
    d["cst"] = consts.reshape(128, NCST * 128)
    return d


def kernel(**inp):
    inp = {kk: np.asarray(v) for kk, v in inp.items()}
    x = inp["x"]
    NB, SEQ, _ = x.shape
    NTOK = NB * SEQ
    n = 8
    TB = NTOK // n
    consts = make_consts()
    xT = np.ascontiguousarray(x.reshape(NTOK, D_MODEL).T)
    nca = build_phase_a(NB, SEQ)
    in_a = []
    for c in range(n):
        d = phase_a_inputs(c, inp, consts)
        d["xT"] = xT
        in_a.append(d)
    ra = run_bass_kernel_spmd(nca, in_a, core_ids=list(range(n)))
    yT = np.zeros((D_MODEL, NTOK), np.float32)
    for c in range(n):
        y = np.asarray(ra.results[c]["yT"])
        yT[128 * c:128 * c + 128] = y[0:128]
        yT[1024 + 128 * c:1024 + 128 * c + 128] = y[128:256]
    ncb = build_phase_b(TB)
    wts = phase_b_weights(inp, consts)
    in_b = []
    for c in range(n):
        d = dict(wts)
        d["xTs"] = np.ascontiguousarray(xT[:, c * TB:(c + 1) * TB])
        d["yTs"] = np.ascontiguousarray(yT[:, c * TB:(c + 1) * TB])
        in_b.append(d)
    rb = run_bass_kernel_spmd(ncb, in_b, core_ids=list(range(n)))
    out = np.zeros((NTOK, D_MODEL), np.float32)
    for c in range(n):
        out[c * TB:(c + 1) * TB] = np.asarray(rb.results[c]["outT"]).T
    return out.reshape(NB, SEQ, D_MODEL)
```

```python
import contextlib
import numpy as np
import concourse.bass as bass
import concourse.mybir as mybir
from concourse.bass_utils import run_bass_kernel_spmd

F32 = mybir.dt.float32
BF16 = mybir.dt.bfloat16
AF = mybir.ActivationFunctionType
ALU = mybir.AluOpType
AX = mybir.AxisListType

D_MODEL = 2048
NJ = 16
RW_IN = 3488
G0 = 3488
GATE0 = 3488 + 3104
FFN = 5632
C0 = -0.6065306597126334
NEG = -30000.0
DBG = {}
PIDC = {}


class Res:
    __slots__ = ("name", "w", "rs", "excl")

    def __init__(self, name="", excl=False):
        self.name = name
        self.w = None
        self.rs = []
        self.excl = excl


class V:
    __slots__ = ("ap", "res")

    def __init__(self, ap, res):
        self.ap = ap
        self.res = res

    def __getitem__(self, idx):
        return V(self.ap[idx], self.res)


class T:
    def __init__(self, h, name, excl=False):
        self.h = h
        self.res = Res(name, excl)

    def __getitem__(self, idx):
        return V(self.h[idx], self.res)

    def v(self, fn):
        return V(fn(self.h), self.res)


class Sched:
    ENG = ("pe", "dve", "act", "pool", "sp")

    def __init__(self, nc, n_dma_sems=16):
        self.nc = nc
        self.ops = {e: [] for e in self.ENG}
        self.cnt = {e: 0 for e in self.ENG}
        self.seen = {e: {} for e in self.ENG}
        self.n_dma_sems = n_dma_sems
        self.dma_use = [0] * n_dma_sems
        self.dma_rr = 0
        self.total = 0
        self.pending = {e: {} for e in self.ENG}
        self.ncc = 0
        self.last_ev = None
        self.last_cev = None
        self.last_g = None
        self.last_dev = None

    def barrier(self):
        allev = {}
        for e in self.ENG:
            if self.cnt[e] > 0 and e != "sp":
                allev[e] = self.cnt[e]
        for i in range(self.n_dma_sems):
            if self.dma_use[i] > 0:
                allev[("dma", i)] = 16 * self.dma_use[i]
        if self.ncc > 0:
            allev["cc"] = self.ncc
        for e in self.ENG:
            for kk, vv in allev.items():
                if self.pending[e].get(kk, 0) < vv:
                    self.pending[e][kk] = vv

    @staticmethod
    def _add(deps, ev):
        if ev is None:
            return
        k, v = ev
        if deps.get(k, 0) < v:
            deps[k] = v

    def op(self, eng, fn, reads=(), writes=(), dma=False, ndma=1, cc=False):
        if eng == "pool" and DBG.get("nopool", 0):
            eng = "dve"
        deps = {}
        for r in reads:
            self._add(deps, r.w)
            if r.excl:
                for ev in r.rs:
                    self._add(deps, ev)
        for w in writes:
            self._add(deps, w.w)
            for ev in w.rs:
                self._add(deps, ev)
        if cc:
            self.ncc += 1
            ev = ("cc", self.ncc)
        elif dma:
            i = self.dma_rr
            self.dma_rr = (self.dma_rr + 1) % self.n_dma_sems
            if self.dma_use[i] > 0:
                self._add(deps, (("dma", i), 16 * self.dma_use[i]))
            self.dma_use[i] += ndma
            ev = (("dma", i), 16 * self.dma_use[i])
        else:
            self.cnt[eng] += 1
            ev = (eng, self.cnt[eng])
        if eng == "pe":
            deps.pop("pe", None)
        sm = DBG.get("serial", 0)
        if sm == 1 and self.last_ev is not None:
            self._add(deps, self.last_ev)
        if sm == 2 and not dma and self.last_cev is not None:
            self._add(deps, self.last_cev)
        grp = {5: ("act", "dve", "pool"), 6: ("pe", "act"), 7: ("pe", "dve"), 8: ("act", "dve")}.get(sm)
        if grp and not dma and eng in grp and self.last_g is not None:
            self._add(deps, self.last_g)
        if sm == 3 and dma and self.last_ev is not None:
            self._add(deps, self.last_ev)
        if sm == 3 and not dma and self.last_dev is not None:
            self._add(deps, self.last_dev)
        if self.pending[eng]:
            for kk, vv in self.pending[eng].items():
                if deps.get(kk, 0) < vv:
                    deps[kk] = vv
            self.pending[eng] = {}
        seen = self.seen[eng]
        waits = []
        for k, v in deps.items():
            if seen.get(k, 0) < v:
                seen[k] = v
                waits.append((k, v))
        self.ops[eng].append((fn, waits, ev))
        self.last_ev = ev
        grp = {5: ("act", "dve", "pool"), 6: ("pe", "act"), 7: ("pe", "dve"), 8: ("act", "dve")}.get(DBG.get("serial", 0))
        if grp and not dma and eng in grp:
            self.last_g = ev
        if dma:
            self.last_dev = ev
        else:
            self.last_cev = ev
        for r in reads:
            r.rs.append(ev)
        for w in writes:
            w.w = ev
            w.rs = []
        self.total += 1
        return ev

    def emit(self, final_waits=()):
        nc = self.nc
        sems = {}
        with contextlib.ExitStack() as st:
            for e in self.ENG:
                sems[e] = st.enter_context(nc.semaphore("s_" + e))
            for i in range(self.n_dma_sems):
                sems[("dma", i)] = st.enter_context(nc.semaphore("s_dma%d" % i))
            sems["cc"] = st.enter_context(nc.semaphore("s_cc"))
            deps = {}
            for r in final_waits:
                self._add(deps, r.w)
            fw = list(deps.items())
            block = st.enter_context(nc.Block())

            def mk(ename):
                def body(eng):
                    for fn, waits, ev in self.ops[ename]:
                        for k, v in waits:
                            eng.wait_ge(sems[k], v)
                        ins = fn(eng)
                        k, v = ev
                        if k == "cc":
                            ins.then_inc(sems[k])
                        elif isinstance(ins, list):
                            for i_ in ins:
                                i_.then_inc(sems[k], 16)
                        else:
                            ins.then_inc(sems[k], 16 if isinstance(k, tuple) else 1)
                    if ename == "sp":
                        for k, v in fw:
                            eng.wait_ge(sems[k], v)
                        for i in range(self.n_dma_sems):
                            if self.dma_use[i] > 0:
                                eng.wait_ge(sems[("dma", i)], 16 * self.dma_use[i])
                return body

            block.tensor(mk("pe"))
            block.vector(mk("dve"))
            block.scalar(mk("act"))
            block.gpsimd(mk("pool"))
            block.sync(mk("sp"))


class KB:
    def __init__(self, nc):
        self.nc = nc
        self.S = Sched(nc)
        self.st = contextlib.ExitStack()
        self.cur = self.st
        self.nps = 0
        self.psr = []
        self.psi = 0

    def sb(self, name, shape, dt=F32):
        return T(self.cur.enter_context(self.nc.sbuf_tensor("sb_" + name, list(shape), dt)), name)

    @contextlib.contextmanager
    def phase(self):
        old = self.cur
        with contextlib.ExitStack() as sub:
            self.cur = sub
            yield
            self.cur = old
        self.S.barrier()

    def ring(self, name, n, shape, dt=F32):
        return Ring([self.sb("%s%d" % (name, i), shape, dt) for i in range(n)])

    def init_psum(self, n=8):
        self.psr = [T(self.st.enter_context(self.nc.psum_tensor("ps%d" % i, [128, 512], F32)), "ps%d" % i, True)
                    for i in range(n)]

    def ps(self):
        t = self.psr[self.psi]
        self.psi = (self.psi + 1) % len(self.psr)
        return t

    def dram(self, name, shape, dt=F32, kind="Internal"):
        return T(self.nc.dram_tensor(name, list(shape), dt, kind=kind).ap(), name)

    @staticmethod
    def _rs(*vs):
        return [x.res for x in vs if isinstance(x, V)]

    @staticmethod
    def _a(x):
        return x.ap if isinstance(x, V) else x

    def mm(self, out, lhsT, rhs, start=True, stop=True):
        self.S.op("pe", lambda e: e.matmul(out.ap, lhsT=lhsT.ap, rhs=rhs.ap, start=start, stop=stop),
                  reads=[lhsT.res, rhs.res], writes=[out.res])

    def tr(self, out, in_, ident):
        self.S.op("pe", lambda e: e.transpose(out.ap, in_.ap, ident.ap),
                  reads=[in_.res, ident.res], writes=[out.res])

    def act(self, out, in_, func, scale=1.0, bias=0.0, eng="act"):
        a = self._a
        self.S.op(eng, lambda e: e.activation(out=out.ap, in_=in_.ap, func=func, bias=a(bias), scale=a(scale)),
                  reads=self._rs(in_, scale, bias), writes=[out.res])

    def tt(self, eng, out, a, b, op):
        self.S.op(eng, lambda e: e.tensor_tensor(out=out.ap, in0=a.ap, in1=b.ap, op=op),
                  reads=[a.res, b.res], writes=[out.res])

    def ts(self, eng, out, a, s1, s2, op0, op1=None):
        g = self._a
        if op1 is None:
            self.S.op(eng, lambda e: e.tensor_scalar(out=out.ap, in0=a.ap, scalar1=g(s1), scalar2=None, op0=op0),
                      reads=self._rs(a, s1), writes=[out.res])
        else:
            self.S.op(eng, lambda e: e.tensor_scalar(out=out.ap, in0=a.ap, scalar1=g(s1), scalar2=g(s2),
                                                     op0=op0, op1=op1),
                      reads=self._rs(a, s1, s2), writes=[out.res])

    def stt(self, eng, out, a, s, b, op0, op1):
        g = self._a
        self.S.op(eng, lambda e: e.scalar_tensor_tensor(out=out.ap, in0=a.ap, scalar=g(s), in1=b.ap,
                                                        op0=op0, op1=op1),
                  reads=self._rs(a, s, b), writes=[out.res])

    def cp(self, eng, out, a):
        if eng == "act":
            self.S.op(eng, lambda e: e.copy(out=out.ap, in_=a.ap), reads=[a.res], writes=[out.res])
        else:
            self.S.op(eng, lambda e: e.tensor_copy(out=out.ap, in_=a.ap), reads=[a.res], writes=[out.res])

    def red(self, eng, out, a, op=None):
        self.S.op(eng, lambda e: e.tensor_reduce(out=out.ap, in_=a.ap, axis=AX.X, op=op or ALU.add),
                  reads=[a.res], writes=[out.res])

    def memset(self, eng, out, val):
        self.S.op(eng, lambda e: e.memset(out.ap, val), writes=[out.res])

    def dma(self, out, in_, eng="sp"):
        self.S.op(eng, lambda e: e.dma_start(out=out.ap, in_=in_.ap), reads=[in_.res], writes=[out.res], dma=True)

    def dma_multi(self, pairs, eng="sp"):
        self.S.op(eng, lambda e: [e.dma_start(out=o.ap, in_=i.ap) for o, i in pairs],
                  reads=[i.res for o, i in pairs], writes=[o.res for o, i in pairs], dma=True, ndma=len(pairs))

    def rsqrt(self, out, in_, scale=1.0, bias=0.0):
        self.act(out, in_, AF.Ln, scale=scale, bias=bias)
        self.act(out, out, AF.Exp, scale=-0.5)


class Ring:
    def __init__(self, tiles):
        self.tiles = tiles
        self.i = 0

    def next(self):
        t = self.tiles[self.i]
        self.i = (self.i + 1) % len(self.tiles)
        return t


IDN, ONE, LS, LI, US, UI, NLS, NUS = range(8)
NCST = 8
DIRS = {
    0: dict(Ms=LS, MTs=US, MTi=UI, Ti=UI, Ts=US, Tr=LS, Ns=NLS, NTs=NUS, last=127),
    1: dict(Ms=US, MTs=LS, MTi=LI, Ti=LI, Ts=LS, Tr=US, Ns=NUS, NTs=NLS, last=0),
}


def make_consts():
    i = np.arange(128)
    ls = (i[:, None] > i[None, :]).astype(np.float32)
    li = (i[:, None] >= i[None, :]).astype(np.float32)
    us = ls.T.copy()
    ui = li.T.copy()
    c = np.zeros((128, NCST, 128), np.float32)
    c[:, IDN] = np.eye(128)
    c[:, ONE] = 1.0
    c[:, LS] = ls
    c[:, LI] = li
    c[:, US] = us
    c[:, UI] = ui
    c[:, NLS] = (ls - 1.0) * (-NEG)
    c[:, NUS] = (us - 1.0) * (-NEG)
    return c


RC_W0, RC_A0, RC_KK, RC_KA, RC_RK, RC_GNW, RC_GNB, RC_DTB, RC_ALOG, RC_GNORM = 0, 256, 512, 640, 896, 1024, 1152, 1280, 1282, 1284
NROWC = 1284 + 128
PR_GAIN, PR_MU, PR_CW = 0, 16, 23
NPRM = 23 + 15


def build_phase_a(NB, SEQ, upto='a3'):
    NTOK = NB * SEQ
    NCH = NTOK // 128
    NBLK = NTOK // 256
    CPS = SEQ // 128
    nc = bass.Bass("TRN2", target_bir_lowering=False)
    k = KB(nc)
    with k.st:
        xT = k.dram("xT", [D_MODEL, NTOK], kind="ExternalInput")
        WA = k.dram("WA", [11, 128, NJ * 128], kind="ExternalInput")
        prm_d = k.dram("prm", [128, NPRM], kind="ExternalInput")
        rowc_d = k.dram("rowc", [128, NROWC], kind="ExternalInput")
        lw_d = k.dram("lw", [128, 768], kind="ExternalInput")
        cst_d = k.dram("cst", [128, NCST * 128], kind="ExternalInput")
        yT = k.dram("yT", [256, NTOK], kind="ExternalOutput")
        rw_fm = k.dram("rw_fm", [NCH, 4, 64, 512])
        rw_tm = k.dram("rw_tm", [NCH, 128, 4 * 256])
        rw_dv = k.dram("rw_dv", [NCH, 64, 4])
        rw_post = k.dram("rw_post", [NCH, 128, 258])
        rw_y = k.dram("rw_y", [NCH, 128, 256])
        gd_fm = k.dram("gd_fm", [NCH, 2, 128, 512])
        gd_tm = k.dram("gd_tm", [NCH, 2, 128, 384])
        gd_lr = k.dram("gd_lr", [NCH, 2, 2, 256])
        gd_dv = k.dram("gd_dv", [NCH, 2, 128, 1])
        gd_post = k.dram("gd_post", [NCH, 128, 128])
        gd_o = k.dram("gd_o", [NCH, 2, 128, 128])

        k.init_psum(8)
        cst = k.sb("cst", [128, NCST, 128])
        k.dma(cst.v(lambda h: h[:].rearrange("p a b -> p (a b)")), cst_d[:, :])
        prm = k.sb("prm", [128, NPRM])
        k.dma(prm[:], prm_d[:, :])
        rowc = k.sb("rowc", [128, NROWC])
        k.dma(rowc[:], rowc_d[:, :])
        lw = k.sb("lw", [128, 768])
        k.dma(lw[:], lw_d[:, :])

        def CM(i):
            return cst[:, i, :]

        ident = CM(IDN)
        ones = CM(ONE)

        with k.phase():
            _A1R.clear()
            _A1G.clear()
            if upto != 'a0':
                _phase_a1(k, nc, NB, SEQ, NTOK, NCH, NBLK, xT, WA, prm, rowc, lw, cst, CM,
                          rw_fm, rw_tm, rw_dv, rw_post, gd_fm, gd_tm, gd_lr, gd_dv, gd_post)
        if upto in ('a2', 'a3'):
            with k.phase():
                _phase_a2(k, nc, NB, SEQ, NCH, CPS, cst, CM, rw_fm, rw_tm, rw_dv, rw_y, gd_fm, gd_tm, gd_lr, gd_dv, gd_o)
        if upto == 'a3':
            with k.phase():
                _phase_a3(k, nc, NCH, rowc, CM, rw_post, rw_y, gd_post, gd_o, yT)
        else:
            k.dma(yT[0:128, 0:128], cst[:, 0, :])
        k.S.emit(final_waits=[yT.res])
    return nc


def _phase_a1(k, nc, NB, SEQ, NTOK, NCH, NBLK, xT, WA, prm, rowc, lw, cst, CM,
              rw_fm, rw_tm, rw_dv, rw_post, gd_fm, gd_tm, gd_lr, gd_dv, gd_post):
    ident = CM(IDN)
    ones = CM(ONE)
    WAs = k.sb("WAs", [128, 11, NJ, 128], BF16)
    wstg = k.ring("wstg", 1, [128, NJ, 128])
    gain = prm[:, PR_GAIN:PR_GAIN + 16]
    for t in range(11):
        s = wstg.next()
        k.dma(s.v(lambda h: h[:].rearrange("p j c -> p (j c)")), WA[t, :, :])
        k.tt("dve" if t % 2 == 0 else "pool", WAs[:, t, :, :], s[:],
             V(gain.ap.unsqueeze(2).to_broadcast([128, NJ, 128]), gain.res), ALU.mult)
    omm = k.sb("omm", [128, 7])
    hmu = k.sb("hmu", [128, 7])
    k.ts("dve", omm[:], prm[:, PR_MU:PR_MU + 7], -1.0, 1.0, ALU.mult, ALU.add)
    k.ts("dve", hmu[:], prm[:, PR_MU:PR_MU + 7], 0.5, None, ALU.mult)
    omka = k.sb("omka", [128, 256])
    k.ts("dve", omka[:], rowc[:, RC_KA:RC_KA + 256], -1.0, 1.0, ALU.mult, ALU.add)
    nA = k.sb("nA", [128, 2])
    k.act(nA[:], rowc[:, RC_ALOG:RC_ALOG + 2], AF.Exp)
    k.ts("dve", nA[:], nA[:], -1.0, None, ALU.mult)
    c10 = k.sb("c10", [2, 2])
    k.cp("dve", c10[:], V(ident.ap[0:2, 0:2], ident.res))
    cw = prm[:, PR_CW:PR_CW + 15]

    xt_r = k.ring("xt", 1, [128, NJ, 260])
    sq_r = k.ring("sq", 1, [128, NJ, 260])
    u_r = k.ring("u", 2, [128, NJ, 260], BF16)
    rstd_r = k.ring("rstd", 2, [128, 260])
    pa_r = k.ring("pa", 3, [128, 260])
    s_r = k.ring("shs", 2, [128, 256])
    m1_r = k.ring("shm", 2, [128, 256])
    pm = [k.ring("pm%d" % t, 2, [128, 256]) for t in range(7)]
    cacc = k.ring("cacc", 2, [128, 256])
    qkv = [k.ring("qkv%d" % t, 2, [128, 256]) for t in range(3)]
    sqn = k.ring("sqn", 2, [128, 256])
    rn_r = k.ring("rnq", 2, [128, 256])

    for b in range(NBLK):
        t0 = b * 256
        seq0 = (t0 // SEQ) * SEQ
        lo = t0 - 2
        hi = t0 + 258
        xt = xt_r.next()
        c_lo, c_hi = 0, 260
        if lo < seq0:
            c_lo = 2
            k.memset("pool", xt[:, :, 0:2], 0.0)
        if hi > seq0 + SEQ:
            c_hi = 258
            k.memset("pool", xt[:, :, 258:260], 0.0)
        k.dma_multi([(xt[:, jq * 4:(jq + 1) * 4, c_lo:c_hi],
                      xT.v(lambda h, jq=jq: h[jq * 512:(jq + 1) * 512, lo + c_lo:lo + c_hi].rearrange("(j p) t -> p j t", p=128)))
                     for jq in range(4)])
        sq = sq_r.next()
        k.act(sq[:], xt[:], AF.Square)
        pss = k.ps()
        for j in range(NJ):
            k.mm(pss[:, 0:260], ones, sq[:, j, :], start=(j == 0), stop=(j == NJ - 1))
        rstd = rstd_r.next()
        k.rsqrt(rstd[:], pss[:, 0:260], scale=1.0 / D_MODEL, bias=1e-6)
        u = u_r.next()
        k.tt("dve", u[:], xt[:], V(rstd.h[:].unsqueeze(1).to_broadcast([128, NJ, 260]), rstd.res), ALU.mult)

        pmt = []
        for t in range(10):
            pp = k.ps()
            for j in range(NJ):
                k.mm(pp[:, 0:260], WAs[:, t, j, :], u[:, j, :], start=(j == 0), stop=(j == NJ - 1))
            pa = pa_r.next()
            k.cp("act", pa[:], pp[:, 0:260])
            if t < 7:
                s = s_r.next()
                k.tt("pool", s[:], pa[:, 1:257], pa[:, 3:259], ALU.add)
                m1 = m1_r.next()
                k.act(m1[:], pa[:, 2:258], AF.Copy, scale=omm[:, t:t + 1])
                o = pm[t].next()
                k.stt("dve", o[:], s[:], hmu[:, t:t + 1], m1[:], ALU.mult, ALU.add)
                pmt.append(o)
            else:
                tc_ = t - 7
                acc = cacc.next()
                k.ts("dve", acc[:], pa[:, 0:256], cw[:, tc_ * 5:tc_ * 5 + 1], None, ALU.mult)
                for kk_ in range(1, 5):
                    k.stt("dve", acc[:], pa[:, kk_:kk_ + 256], cw[:, tc_ * 5 + kk_:tc_ * 5 + kk_ + 1], acc[:],
                          ALU.mult, ALU.add)
                o = qkv[tc_].next()
                k.act(o[:], acc[:], AF.Silu)
                pmt.append(o)
        qn = []
        for qi in range(2):
            src = pmt[7 + qi]
            s2 = sqn.next()
            k.tt("pool", s2[:], src[:], src[:], ALU.mult)
            pq = k.ps()
            k.mm(pq[:, 0:256], ones, s2[:])
            rn = rn_r.next()
            k.rsqrt(rn[:], pq[:, 0:256], scale=1.0, bias=1e-6)
            if qi == 0:
                k.stt("dve", src[:], src[:], 128.0 ** -0.5, rn[:], ALU.mult, ALU.mult)
            else:
                k.tt("dve", src[:], src[:], rn[:], ALU.mult)
        for cc in range(2):
            g = b * 2 + cc
            cs = slice(cc * 128, cc * 128 + 128)
            _a1_rwkv_chunk(k, g, cs, pmt, rowc, lw, CM, omka, rw_fm, rw_tm, rw_dv, rw_post)
            _a1_gdn_chunk(k, g, cs, cc, u, WAs, pmt, rowc, CM, nA, c10, gd_fm, gd_tm, gd_lr, gd_dv, gd_post)


_A1R = {}


def _a1_rwkv_chunk(k, g, cs, pmt, rowc, lw, CM, omka, rw_fm, rw_tm, rw_dv, rw_post):
    ident = CM(IDN)
    ones = CM(ONE)
    R = _A1R
    if not R:
        R["th"] = k.ring("a1th", 2, [128, 128])
        R["sg0"] = k.ring("a1sg0", 2, [128, 128])
        R["sg1"] = k.ring("a1sg1", 2, [32, 128])
        R["rkv"] = k.ring("a1rkv", 2, [128, 384])
        R["sw"] = k.ring("a1sw", 2, [128, 256])
        R["aa"] = k.ring("a1aa", 2, [128, 256])
        R["E"] = [k.ring("a1E%d" % i, 1, [128, 256]) for i in range(4)]
        R["kx"] = k.ring("a1kx", 2, [128, 128])
        R["kx2"] = k.ring("a1kx2", 2, [128, 128])
        R["ss"] = k.ring("a1ss", 2, [128, 2])
        R["kk"] = k.ring("a1kk", 2, [128, 128])
        R["kka"] = k.ring("a1kka", 2, [128, 256])
        R["t1"] = k.ring("a1t1", 2, [128, 256])
        R["kd"] = k.ring("a1kd", 2, [128, 256])
        R["q4"] = k.ring("a1q4", 1, [128, 4, 256])
        R["tm"] = k.ring("a1tm", 1, [128, 4, 4, 64])
        R["fm"] = k.ring("a1fm", 2, [64, 512])
        R["post"] = k.ring("a1post", 2, [128, 258])
        R["bs"] = k.ring("a1bs", 2, [128, 128])
        R["dv"] = k.ring("a1dv", 2, [64, 4])
    th = R["th"].next()
    k.act(th[:], pmt[3][:, cs], AF.Tanh)
    sg0 = R["sg0"].next()
    k.act(sg0[:], pmt[5][:, cs], AF.Sigmoid)
    sg1 = R["sg1"].next()
    k.act(sg1[:], pmt[6][0:32, cs], AF.Sigmoid)
    p_aw = k.ps()
    k.mm(p_aw[:, 0:256], th[:], lw[:, 0:256], start=True, stop=False)
    k.mm(p_aw[:, 0:256], V(ones.ap[0:1, :], ones.res), rowc[0:1, RC_W0:RC_W0 + 256], start=False, stop=True)
    p_aa = k.ps()
    k.mm(p_aa[:, 0:256], pmt[4][:, cs], lw[:, 256:512], start=True, stop=False)
    k.mm(p_aa[:, 0:256], V(ones.ap[0:1, :], ones.res), rowc[0:1, RC_A0:RC_A0 + 256], start=False, stop=True)
    p_g = k.ps()
    k.mm(p_g[:, 0:128], sg0[:], lw[:, 512:640], start=True, stop=False)
    k.mm(p_g[:, 0:128], sg1[:], lw[0:32, 640:768], start=False, stop=True)
    p_t = k.ps()
    for i in range(3):
        k.tr(p_t[:, i * 128:(i + 1) * 128], pmt[i][:, cs], ident)
    rkv = R["rkv"].next()
    k.cp("act", rkv[:], p_t[:, 0:384])
    r_tm, k_tm, v_tm = rkv[:, 0:128], rkv[:, 128:256], rkv[:, 256:384]
    post = R["post"].next()
    k.cp("pool", post[:, 0:128], v_tm)
    k.cp("act", post[:, 128:256], p_g[:, 0:128])
    sw = R["sw"].next()
    k.act(sw[:], p_aw[:, 0:256], AF.Sigmoid)
    aa = R["aa"].next()
    k.act(aa[:], p_aa[:, 0:256], AF.Sigmoid)
    pL = [k.ps(), k.ps(), k.ps()]
    for d in range(2):
        dd = DIRS[d]
        for i, key in enumerate(("Ti", "Ts", "Tr")):
            k.mm(pL[i][:, d * 128:(d + 1) * 128], CM(dd[key]), sw[:, d * 128:(d + 1) * 128])
    E = [r.next() for r in R["E"]]
    k.act(E[0][:], pL[0][:, 0:256], AF.Exp, scale=C0)
    k.act(E[1][:], pL[0][:, 0:256], AF.Exp, scale=-C0)
    k.act(E[2][:], pL[1][:, 0:256], AF.Exp, scale=C0)
    k.act(E[3][:], pL[2][:, 0:256], AF.Exp, scale=C0)
    p_dv = k.ps()
    for hd in range(4):
        k.mm(p_dv[0:64, hd:hd + 1], sw[:, hd * 64:(hd + 1) * 64], V(ones.ap[:, 0:1], ones.res))
    dv = R["dv"].next()
    k.act(dv[:], p_dv[0:64, 0:4], AF.Exp, scale=C0)
    k.dma(rw_dv[g, :, :], dv[:])
    kx = R["kx"].next()
    k.tt("dve", kx[:], k_tm, rowc[:, RC_KK:RC_KK + 128], ALU.mult)
    kx2 = R["kx2"].next()
    k.tt("pool", kx2[:], kx[:], kx[:], ALU.mult)
    ss = R["ss"].next()
    k.red("dve", ss[:], kx2.v(lambda h: h[:].rearrange("p (a b) -> p a b", a=2)))
    k.rsqrt(ss[:], ss[:], scale=1.0, bias=1e-6)
    kk = R["kk"].next()
    k.tt("dve", kk.v(lambda h: h[:].rearrange("p (a b) -> p a b", a=2)),
         kx.v(lambda h: h[:].rearrange("p (a b) -> p a b", a=2)),
         V(ss.h[:].unsqueeze(2).to_broadcast([128, 2, 64]), ss.res), ALU.mult)

    def bc2(v):
        return V(v.ap.unsqueeze(1).to_broadcast([128, 2, 128]), v.res)

    def as2(t_):
        return t_.v(lambda h: h[:].rearrange("p (a b) -> p a b", a=2))

    kka = R["kka"].next()
    k.tt("dve", as2(kka), bc2(kk[:]), as2(aa), ALU.mult)
    t1 = R["t1"].next()
    k.tt("pool", t1[:], aa[:], rowc[:, RC_KA:RC_KA + 256], ALU.mult)
    k.tt("pool", t1[:], t1[:], omka[:], ALU.add)
    kd = R["kd"].next()
    k.tt("dve", as2(kd), as2(t1), bc2(k_tm), ALU.mult)
    q4 = R["q4"].next()
    tm = R["tm"].next()

    def q4v(i):
        return q4.v(lambda h: h[:, i, :].rearrange("p (a b) -> p a b", a=2))

    def tmv(i):
        return tm.v(lambda h: h[:, :, i, :])

    def as4(t_):
        return t_.v(lambda h: h[:].rearrange("p (a b) -> p a b", a=4))

    k.stt("dve", q4v(0), bc2(kk[:]), -1.0, as2(E[2]), ALU.mult, ALU.mult)
    k.tt("pool", q4v(1), bc2(r_tm), as2(E[0]), ALU.mult)
    k.tt("dve", q4v(2), as2(kka), as2(E[1]), ALU.mult)
    k.tt("pool", q4v(3), as2(kd), as2(E[1]), ALU.mult)
    k.cp("pool", tmv(0), q4.v(lambda h: h[:, 0, :].rearrange("p (a b) -> p a b", a=4)))
    k.tt("dve", tmv(1), as4(kka), as4(E[3]), ALU.mult)
    k.tt("pool", tmv(2), as4(kd), as4(E[3]), ALU.mult)
    k.cp("pool", tm.v(lambda h: h[:, :, 3, :].rearrange("p (d h) b -> p d h b", d=2)), V(_vdup(v_tm.ap), v_tm.res))
    k.dma(rw_tm[g, :, :], tm.v(lambda h: h[:].rearrange("p a b c -> p (a b c)")))
    bs = R["bs"].next()
    k.tt("pool", bs[:], kd[:, 0:128], kd[:, 128:256], ALU.add)
    k.tt("pool", bs[:], bs[:], r_tm, ALU.mult)
    k.stt("dve", bs[:], bs[:], 0.5, rowc[:, RC_RK:RC_RK + 128], ALU.mult, ALU.mult)
    k.red("dve", post[:, 256:258], bs.v(lambda h: h[:].rearrange("p (a b) -> p a b", a=2)))
    k.dma(rw_post[g, :, :], post[:])
    for hd in range(4):
        pf = k.ps()
        for i in range(4):
            k.tr(pf[0:64, i * 128:(i + 1) * 128], q4[:, i, hd * 64:(hd + 1) * 64], ident)
        fm = R["fm"].next()
        k.cp("act" if hd % 2 == 0 else "dve", fm[:], pf[0:64, 0:512])
        k.dma(rw_fm[g, hd, :, :], fm[:])


def _vdup(ap):
    return ap.rearrange("p (h b) -> p h b", h=2).unsqueeze(1).to_broadcast([128, 2, 2, 64])


_A1G = {}


def _a1_gdn_chunk(k, g, cs, cc, u, WAs, pmt, rowc, CM, nA, c10, gd_fm, gd_tm, gd_lr, gd_dv, gd_post):
    ident = CM(IDN)
    ones = CM(ONE)
    R = _A1G
    if not R:
        R["sz"] = k.ring("g1sz", 2, [128, 128])
        R["kv"] = k.ring("g1kv", 2, [128, 256])
        R["t4"] = k.ring("g1t4", 2, [128, 4])
        R["gb"] = k.ring("g1gb", 2, [128, 6])
        R["gn2"] = k.ring("g1gn2", 2, [128, 4])
        R["bc"] = k.ring("g1bc", 2, [128, 4, 128])
        R["eg"] = k.ring("g1eg", 2, [128, 128])
        R["fm"] = k.ring("g1fm", 2, [128, 512])
        R["tm"] = k.ring("g1tm", 2, [128, 384])
        R["ec"] = k.ring("g1ec", 2, [128, 2])
        R["sc"] = k.ring("g1sc", 2, [128, 1])
        R["lr"] = k.ring("g1lr", 2, [2, 256])
        R["dv"] = k.ring("g1dv", 2, [128, 1])
    pz = k.ps()
    for j in range(NJ):
        k.mm(pz[:, 0:128], u[:, j, 2 + cc * 128:2 + cc * 128 + 128], WAs[:, 10, j, :], start=(j == 0), stop=(j == NJ - 1))
    sz = R["sz"].next()
    k.act(sz[:], pz[:, 0:128], AF.Silu)
    k.dma(gd_post[g, :, :], sz[:])
    pt = k.ps()
    k.tr(pt[:, 0:128], pmt[8][:, cs], ident)
    k.tr(pt[:, 128:256], pmt[9][:, cs], ident)
    kv = R["kv"].next()
    k.cp("act", kv[:], pt[:, 0:256])
    k_tm, v_tm = kv[:, 0:128], kv[:, 128:256]
    p4 = k.ps()
    k.tr(p4[:, 0:4], pmt[6][32:36, cs], V(ident.ap[32:36, 32:36], ident.res))
    t4 = R["t4"].next()
    k.tt("dve", t4[:, 0:2], p4[:, 0:2], rowc[:, RC_DTB:RC_DTB + 2], ALU.add)
    gb = R["gb"].next()
    k.act(t4[:, 0:2], t4[:, 0:2], AF.Exp)
    k.act(t4[:, 0:2], t4[:, 0:2], AF.Ln, bias=1.0)
    k.tt("dve", gb[:, 0:2], t4[:, 0:2], nA[:], ALU.mult)
    k.act(gb[:, 2:4], p4[:, 2:4], AF.Sigmoid)
    gn2 = R["gn2"].next()
    k.cp("pool", gn2.v(lambda h: h[:].rearrange("p (d s) -> p d s", s=2)[:, :, 0]), gb[:, 0:2])
    k.ts("dve", gn2.v(lambda h: h[:].rearrange("p (d s) -> p d s", s=2)[:, :, 1]), gb[:, 0:2], -1.0, None, ALU.mult)
    bc = R["bc"].next()
    k.cp("pool", bc[:], V(gb.h[:, 0:4].unsqueeze(2).to_broadcast([128, 4, 128]), gb.res))
    for d in range(2):
        dd = DIRS[d]
        fm = R["fm"].next()
        tm = R["tm"].next()
        pb = k.ps()
        k.mm(pb[:, 0:128], bc[:, 2 + d, :], ident)
        k.mm(pb[:, 128:256], bc[:, d, :], CM(dd["Ti"]))
        eg = R["eg"].next()
        k.act(eg[:], pb[:, 128:256], AF.Exp)
        k.cp("pool", fm[:, 0:128], pmt[8][:, cs])
        k.cp("pool", fm[:, 128:256], pmt[7][:, cs])
        k.tt("dve", fm[:, 256:384], pmt[8][:, cs], pb[:, 0:128], ALU.mult)
        k.tt("pool", fm[:, 384:512], pmt[7][:, cs], eg[:], ALU.mult)
        k.dma(gd_fm[g, d, :, :], fm[:])
        dv = R["dv"].next()
        k.cp("act", dv[:], eg[:, dd["last"]:dd["last"] + 1])
        k.dma(gd_dv[g, d, :, :], dv[:])
        pc = k.ps()
        k.mm(pc[:, 0:1], CM(dd["Ti"]), gb[:, d:d + 1])
        k.mm(pc[:, 1:2], CM(dd["Tr"]), gb[:, d:d + 1])
        ec = R["ec"].next()
        k.act(ec[:], pc[:, 0:2], AF.Exp)
        sc = R["sc"].next()
        k.tt("dve", sc[:], ec[:, 0:1], gb[:, 2 + d:3 + d], ALU.mult)
        k.ts("dve", tm[:, 0:128], v_tm, gb[:, 2 + d:3 + d], None, ALU.mult)
        k.ts("pool", tm[:, 128:256], k_tm, sc[:, 0:1], None, ALU.mult)
        k.ts("dve", tm[:, 256:384], k_tm, ec[:, 1:2], None, ALU.mult)
        k.dma(gd_tm[g, d, :, :], tm[:])
        pr = k.ps()
        k.mm(pr[0:2, 0:128], gn2[:, 2 * d:2 * d + 2], CM(dd["Ti"]))
        lr = R["lr"].next()
        k.ts("dve", lr[:, 0:128], pr[0:2, 0:128], c10[:, 0:1], c10[:, 1:2], ALU.mult, ALU.add)
        k.ts("dve", lr[:, 128:256], pr[0:2, 0:128], c10[:, 1:2], c10[:, 0:1], ALU.mult, ALU.add)
        k.dma(gd_lr[g, d, :, :], lr[:])


def _invert(k, R, A0, N0, CM, tag):
    ident = CM(IDN)
    TN = [R["TN0"].next(), R["TN1"].next()]
    Ak = [R["Ak0"].next(), R["Ak1"].next()]
    k.tt("pool", TN[0][:, 0:128], N0, ident, ALU.add)
    if DBG.get("lv", 8) == 1:
        return TN[0][:, 0:128]
    p = k.ps()
    iv = DBG.get("iv", 15)
    if iv & 1:
        k.mm(p[:, 0:128], A0, N0)
    if iv & 2:
        k.mm(p[:, 128:256], N0, A0)
    if iv & 4:
        k.cp("act", TN[0][:, 128:256], p[:, 0:128])
    if iv & 8:
        k.cp("dve", Ak[0][:], p[:, 128:256])
    cur = 0
    if DBG.get("lv", 8) == 0:
        return TN[0][:, 0:128]
    for lv in range(2, DBG.get("lv", 8)):
        a_prev = Ak[cur]
        tn_prev = TN[cur]
        tn_new = TN[1 - cur]
        p = k.ps()
        if lv < 7:
            k.mm(p[:, 0:256], a_prev[:], tn_prev[:, 0:256])
            p2 = k.ps()
            k.mm(p2[:, 0:128], tn_prev[:, 128:256], a_prev[:])
            k.tt("dve", tn_new[:, 0:128], tn_prev[:, 0:128], p[:, 0:128], ALU.add)
            k.cp("act", tn_new[:, 128:256], p[:, 128:256])
            k.cp("act", Ak[1 - cur][:], p2[:, 0:128])
        else:
            k.mm(p[:, 0:128], a_prev[:], tn_prev[:, 0:128])
            k.tt("dve", tn_new[:, 0:128], tn_prev[:, 0:128], p[:, 0:128], ALU.add)
        cur = 1 - cur
    return TN[cur][:, 0:128]


def _phase_a2(k, nc, NB, SEQ, NCH, CPS, cst, CM, rw_fm, rw_tm, rw_dv, rw_y, gd_fm, gd_tm, gd_lr, gd_dv, gd_o):
    ident = CM(IDN)
    NRW = NB * 4
    NGD = NB * 2
    Zrw = [[k.sb("zrw%d_%d" % (s, i), [64, 64]) for i in range(2)] for s in range(NRW)]
    Zgd = [[k.sb("zgd%d_%d" % (s, i), [128, 128]) for i in range(2)] for s in range(NGD)]
    for s in range(NRW):
        k.memset("pool", Zrw[s][0][:], 0.0)
    for s in range(NGD):
        k.memset("pool", Zgd[s][0][:], 0.0)
    NR = 3
    R = dict(
        TN0=k.ring("TN0", NR, [128, 256]), TN1=k.ring("TN1", NR, [128, 256]),
        Ak0=k.ring("Ak0", NR, [128, 128]), Ak1=k.ring("Ak1", NR, [128, 128]),
        fm=k.ring("s_fm", NR, [128, 512]), tm=k.ring("s_tm", NR, [128, 384]),
        rtm=k.ring("s_rtm", NR, [128, 256]), rfm=k.ring("s_rfm", NR, [64, 512]),
        dv=k.ring("s_dv", NR, [128, 4]), lr=k.ring("s_lr", NR, [2, 256]),
        A0=k.ring("s_A0", NR, [128, 128]), NB_=k.ring("s_NB", NR, [128, 256]), CK=k.ring("s_CK", NR, [128, 256]),
        D=k.ring("s_D", NR, [128, 384]), m=k.ring("s_m", NR, [128, 256]),
        akv=k.ring("s_akv", NR, [128, 128]), U=k.ring("s_U", NR, [128, 128]), WT=k.ring("s_WT", NR, [128, 128]),
        X=k.ring("s_X", NR, [128, 128]), O=k.ring("s_O", NR, [128, 128]),
    )
    for step in range(CPS):
        for bi in range(NB):
            for d in (range(2) if DBG.get("rw", 1) else []):
                dd = DIRS[d]
                ci = step if d == 0 else CPS - 1 - step
                g = bi * CPS + ci
                rdv = R["dv"].next()
                k.dma(rdv[0:64, 0:4], rw_dv[g, :, :])
                for h in range(2):
                    hd = d * 2 + h
                    s = bi * 4 + hd
                    par = step % 2
                    Zo, Zn = Zrw[s][par], Zrw[s][1 - par]
                    fm = R["rfm"].next()
                    k.dma(fm[:], rw_fm[g, hd, :, :])
                    tm = R["rtm"].next()
                    k.dma(tm[:], rw_tm.v(lambda hh: hh[g, :, hd * 256:(hd + 1) * 256]))
                    aT, rT, bT, kT = fm[:, 0:128], fm[:, 128:256], fm[:, 256:384], fm[:, 384:512]
                    A_tm, Bh, Kh, Vt = tm[:, 0:64], tm[:, 64:128], tm[:, 128:192], tm[:, 192:256]
                    pA = k.ps()
                    k.mm(pA[:, 0:128], aT, bT)
                    pB = k.ps()
                    k.mm(pB[:, 0:256], bT, fm[:, 0:256])
                    pC = k.ps()
                    k.mm(pC[:, 0:256], kT, fm[:, 0:256])
                    A0 = R["A0"].next()
                    k.tt("dve", A0[:], pA[:, 0:128], CM(dd["Ms"]), ALU.mult)
                    msk = V(cst.h[:, dd["MTs"]:dd["MTs"] + 2, :].rearrange("p a b -> p (a b)"), cst.res)
                    NBt = R["NB_"].next()
                    k.tt("dve", NBt[:], pB[:, 0:256], msk, ALU.mult)
                    CK = R["CK"].next()
                    k.tt("dve", CK[:], pC[:, 0:256], msk, ALU.mult)
                    TT = _invert(k, R, A0[:], NBt[:, 0:128], CM, "rw")
                    p1 = k.ps()
                    k.mm(p1[:, 0:64], CK[:, 0:128], Vt)
                    akv = R["akv"].next()
                    k.cp("act", akv[:, 0:64], p1[:, 0:64])
                    p2 = k.ps()
                    k.mm(p2[:, 0:64], TT, akv[:, 0:64])
                    k.mm(p2[0:64, 128:256], A_tm, TT)
                    U = R["U"].next()
                    k.cp("act", U[:, 0:64], p2[:, 0:64])
                    WT = R["WT"].next()
                    k.cp("dve", WT[0:64, :], p2[0:64, 128:256])
                    pX = k.ps()
                    k.mm(pX[:, 0:64], WT[0:64, :], Zo[:])
                    X = R["X"].next()
                    k.tt("dve", X[:, 0:64], pX[:, 0:64], U[:, 0:64], ALU.add)
                    pZ = k.ps()
                    k.mm(pZ[0:64, 0:64], Kh, Vt, start=True, stop=False)
                    k.mm(pZ[0:64, 0:64], Bh, X[:, 0:64], start=False, stop=True)
                    pO = k.ps()
                    k.mm(pO[:, 0:64], rT, Zo[:], start=True, stop=False)
                    k.mm(pO[:, 0:64], NBt[:, 128:256], X[:, 0:64], start=False, stop=False)
                    k.mm(pO[:, 0:64], CK[:, 128:256], Vt, start=False, stop=True)
                    k.stt("dve", Zn[:], Zo[:], rdv[0:64, hd:hd + 1], pZ[0:64, 0:64], ALU.mult, ALU.add)
                    O = R["O"].next()
                    k.cp("act", O[:, 0:64], pO[:, 0:64])
                    k.dma(rw_y.v(lambda hh: hh[g, :, hd * 64:(hd + 1) * 64]), O[:, 0:64])
            for d in (range(2) if DBG.get("gd", 1) else []):
                dd = DIRS[d]
                ci = step if d == 0 else CPS - 1 - step
                g = bi * CPS + ci
                s = bi * 2 + d
                par = step % 2
                Zo, Zn = Zgd[s][par], Zgd[s][1 - par]
                fm = R["fm"].next()
                k.dma(fm[:], gd_fm[g, d, :, :])
                tm = R["tm"].next()
                k.dma(tm[:], gd_tm[g, d, :, :])
                lr = R["lr"].next()
                k.dma(lr[:], gd_lr[g, d, :, :])
                gdv = R["dv"].next()
                k.dma(gdv[:, 0:1], gd_dv[g, d, :, :])
                if DBG.get("cut", 9) <= 0:
                    continue
                kT, qT, kbT, qgT = fm[:, 0:128], fm[:, 128:256], fm[:, 256:384], fm[:, 384:512]
                vb, kbg, kg = tm[:, 0:128], tm[:, 128:256], tm[:, 256:384]
                pt = k.ps()
                k.mm(pt[:, 0:128], lr[:, 0:128], lr[:, 128:256])
                k.mm(pt[:, 128:256], lr[:, 128:256], lr[:, 0:128])
                m = R["m"].next()
                k.stt("dve", m[:, 0:128], pt[:, 0:128], 0.0, CM(dd["Ns"]), ALU.min, ALU.add)
                k.stt("dve", m[:, 128:256], pt[:, 128:256], 0.0, CM(dd["NTs"]), ALU.min, ALU.add)
                D = R["D"].next()
                k.act(D[:, 0:256], m[:, 0:256], AF.Exp)
                k.tt("pool", D[:, 256:384], D[:, 128:256], ident, ALU.add)
                pA = k.ps()
                k.mm(pA[:, 0:128], kbT, kT)
                pB = k.ps()
                k.mm(pB[:, 0:256], kT, fm[:, 128:384])
                A0 = R["A0"].next()
                k.stt("dve", A0[:], pA[:, 0:128], -1.0, D[:, 0:128], ALU.mult, ALU.mult)
                NBt = R["NB_"].next()
                k.stt("dve", NBt[:, 0:128], pB[:, 128:256], -1.0, D[:, 128:256], ALU.mult, ALU.mult)
                k.tt("dve", NBt[:, 128:256], pB[:, 0:128], D[:, 256:384], ALU.mult)
                if DBG.get("cut", 9) <= 1:
                    continue
                TT = _invert(k, R, A0[:], NBt[:, 0:128], CM, "gd") if DBG.get("cut", 9) > 2 else NBt[:, 0:128]
                if DBG.get("cut", 9) <= 3:
                    continue
                p2 = k.ps()
                k.mm(p2[:, 0:128], TT, vb)
                k.mm(p2[:, 128:256], kbg, TT)
                U = R["U"].next()
                k.cp("act", U[:], p2[:, 0:128])
                WT = R["WT"].next()
                k.ts("dve", WT[:], p2[:, 128:256], -1.0, None, ALU.mult)
                pX = k.ps()
                k.mm(pX[:, 0:128], WT[:], Zo[:])
                X = R["X"].next()
                k.tt("dve", X[:], pX[:, 0:128], U[:], ALU.add)
                pZ = k.ps()
                k.mm(pZ[:, 0:128], kg, X[:])
                pO = k.ps()
                k.mm(pO[:, 0:128], qgT, Zo[:], start=True, stop=False)
                k.mm(pO[:, 0:128], NBt[:, 128:256], X[:], start=False, stop=True)
                k.stt("dve", Zn[:], Zo[:], gdv[:, 0:1], pZ[:, 0:128], ALU.mult, ALU.add)
                O = R["O"].next()
                k.cp("act", O[:], pO[:, 0:128])
                k.dma(gd_o[g, d, :, :], O[:])


def _phase_a3(k, nc, NCH, rowc, CM, rw_post, rw_y, gd_post, gd_o, yT, ydt=F32):
    ident = CM(IDN)
    yt_r = k.ring("a3y", 2, [128, 256])
    po_r = k.ring("a3po", 2, [128, 258])
    y_r = k.ring("a3ys", 2, [128, 128])
    c_r = k.ring("a3c", 2, [128, 128])
    st_r = k.ring("a3st", 2, [128, 4])
    go_r = k.ring("a3go", 2, [128, 256])
    sz_r = k.ring("a3sz", 2, [128, 128])
    o_r = k.ring("a3o", 2, [128, 128])
    yo_r = k.ring("a3yo", 2, [128, 256], ydt)

    def h2(v):
        return V(v.ap.rearrange("p (a b) -> p a b", a=2), v.res)

    def bch(v):
        return V(v.ap.unsqueeze(2).to_broadcast([128, 2, 64]), v.res)

    for g in range(NCH):
        yt = yt_r.next()
        k.dma(yt[:], rw_y[g, :, :])
        po = po_r.next()
        k.dma(po[:], rw_post[g, :, :])
        y = y_r.next()
        k.tt("pool", y[:], yt[:, 0:128], yt[:, 128:256], ALU.add)
        st = st_r.next()
        k.red("dve", st[:, 0:2], h2(y[:]))
        k.ts("dve", st[:, 0:2], st[:, 0:2], 1.0 / 64, None, ALU.mult)
        c = c_r.next()
        k.tt("dve", h2(c[:]), h2(y[:]), bch(st[:, 0:2]), ALU.subtract)
        k.tt("pool", y[:], c[:], c[:], ALU.mult)
        k.red("dve", st[:, 2:4], h2(y[:]))
        k.rsqrt(st[:, 2:4], st[:, 2:4], scale=1.0 / 64, bias=64e-5)
        k.tt("dve", h2(c[:]), h2(c[:]), bch(st[:, 2:4]), ALU.mult)
        k.tt("pool", c[:], c[:], rowc[:, RC_GNW:RC_GNW + 128], ALU.mult)
        k.tt("pool", c[:], c[:], rowc[:, RC_GNB:RC_GNB + 128], ALU.add)
        k.tt("dve", h2(y[:]), h2(po[:, 0:128]), bch(po[:, 256:258]), ALU.mult)
        k.tt("pool", c[:], c[:], y[:], ALU.add)
        k.tt("dve", c[:], c[:], po[:, 128:256], ALU.mult)
        go = go_r.next()
        k.dma(go.v(lambda h: h[:].rearrange("p (d c) -> p d c", d=2)),
              gd_o.v(lambda h: h[g, :, :, :].rearrange("d p c -> p d c")))
        sz = sz_r.next()
        k.dma(sz[:], gd_post[g, :, :])
        o = o_r.next()
        k.tt("pool", o[:], go[:, 0:128], go[:, 128:256], ALU.add)
        o2 = o_r.next()
        k.tt("pool", o2[:], o[:], o[:], ALU.mult)
        st2 = st_r.next()
        k.red("dve", st2[:, 0:1], o2[:])
        k.rsqrt(st2[:, 0:1], st2[:, 0:1], scale=1.0 / 128, bias=1e-6)
        k.ts("dve", o[:], o[:], st2[:, 0:1], None, ALU.mult)
        k.tt("pool", o[:], o[:], rowc[:, RC_GNORM:RC_GNORM + 128], ALU.mult)
        k.tt("dve", o[:], o[:], sz[:], ALU.mult)
        p = k.ps()
        k.tr(p[:, 0:128], c[:], ident)
        k.tr(p[:, 128:256], o[:], ident)
        yo = yo_r.next()
        k.cp("act", yo[:], p[:, 0:256])
        k.dma_multi([(yT[0:128, g * 128:(g + 1) * 128], yo[:, 0:128]),
                     (yT[128:256, g * 128:(g + 1) * 128], yo[:, 128:256])])


def phase_a_inputs(c, inp, consts):
    W = inp["w_in"][0]
    rc = np.arange(128 * c, 128 * c + 128)
    qh = c // 2
    cols = [rc, 1024 + rc, 2048 + rc, np.arange(3072, 3200), np.arange(3200, 3328), np.arange(3328, 3456),
            np.concatenate([np.arange(3456, 3488), G0 + 3072 + np.array([c, 8 + c, 16 + c, 24 + c])]),
            G0 + qh * 128 + np.arange(128), G0 + 512 + qh * 128 + np.arange(128),
            G0 + 1024 + c * 128 + np.arange(128), G0 + 2048 + c * 128 + np.arange(128)]
    WA = np.zeros((11, 128, NJ, 128), np.float32)
    for t, cl in enumerate(cols):
        WA[t, :, :, :len(cl)] = W[:, cl].reshape(NJ, 128, len(cl)).transpose(1, 0, 2)
    prm = np.zeros((128, NPRM), np.float32)
    prm[:, PR_GAIN:PR_GAIN + 16] = inp["norm_pre_mix"][0].reshape(NJ, 128).T
    mu = inp["rw_shift_mu"][0]
    for t in range(7):
        cl = cols[t]
        n = min(len(cl), 128)
        if t == 6:
            prm[:32, PR_MU + t] = mu[cl[:32]]
        else:
            prm[:n, PR_MU + t] = mu[cl]
    cwh = inp["gdn_conv_w"][0]
    for t, base in enumerate([qh * 128, 512 + qh * 128, 1024 + c * 128]):
        prm[:, PR_CW + t * 5:PR_CW + t * 5 + 5] = cwh[:, base:base + 128].T
    rowc = np.zeros((128, NROWC), np.float32)

    def row(v):
        return np.broadcast_to(np.asarray(v, np.float32)[None, :], (128, len(v)))

    rowc[:, RC_W0:RC_W0 + 256] = row(np.concatenate([inp["rw_w0_f"][0][rc], inp["rw_w0_b"][0][rc]]))
    rowc[:, RC_A0:RC_A0 + 256] = row(np.concatenate([inp["rw_a0_f"][0][rc], inp["rw_a0_b"][0][rc]]))
    rowc[:, RC_KK:RC_KK + 128] = row(inp["rw_k_k"][0][rc])
    rowc[:, RC_KA:RC_KA + 256] = row(np.concatenate([inp["rw_k_a"][0][rc]] * 2))
    rowc[:, RC_RK:RC_RK + 128] = row(inp["rw_r_k"][0].reshape(-1)[rc])
    rowc[:, RC_GNW:RC_GNW + 128] = row(inp["rw_gn_w"][0][rc])
    rowc[:, RC_GNB:RC_GNB + 128] = row(inp["rw_gn_b"][0][rc])
    rowc[:, RC_DTB:RC_DTB + 2] = row(np.array([inp["gdn_dt_bias_f"][0][c], inp["gdn_dt_bias_b"][0][c]]))
    rowc[:, RC_ALOG:RC_ALOG + 2] = row(np.array([inp["gdn_a_log_f"][0][c], inp["gdn_a_log_b"][0][c]]))
    rowc[:, RC_GNORM:RC_GNORM + 128] = row(inp["gdn_norm_w"][0])
    lw = np.zeros((128, 768), np.float32)
    lw[0:64, 0:128] = inp["rw_w2_f"][0][:, rc]
    lw[64:128, 128:256] = inp["rw_w2_b"][0][:, rc]
    lw[0:64, 256:384] = inp["rw_a2_f"][0][:, rc]
    lw[64:128, 384:512] = inp["rw_a2_b"][0][:, rc]
    lw[:, 512:640] = inp["rw_g2"][0][0:128, rc]
    lw[0:32, 640:768] = inp["rw_g2"][0][128:160, rc]
    return dict(WA=WA.reshape(11, 128, NJ * 128), prm=prm, rowc=rowc, lw=lw, cst=consts.reshape(128, NCST * 128))


def build_phase_b(TB):
    N = min(512, TB)
    NBK = TB // N
    nc = bass.Bass("TRN2", target_bir_lowering=False)
    k = KB(nc)
    with k.st:
        xTs = k.dram("xTs", [D_MODEL, TB], kind="ExternalInput")
        yTs = k.dram("yTs", [D_MODEL, TB], kind="ExternalInput")
        WG = k.dram("WG", [32, 128, 2048], kind="ExternalInput")
        WP = k.dram("WP", [16, 128, 2048], kind="ExternalInput")
        WO = k.dram("WO", [16, 128, 2048], kind="ExternalInput")
        WFG = k.dram("WFG", [44, 128, 2048], kind="ExternalInput")
        WFU = k.dram("WFU", [44, 128, 2048], kind="ExternalInput")
        WD = k.dram("WD", [16, 128, 44 * 128], kind="ExternalInput")
        gn_d = k.dram("gn", [128, 64], kind="ExternalInput")
        cst_d = k.dram("cst", [128, NCST * 128], kind="ExternalInput")
        outT = k.dram("outT", [D_MODEL, TB], kind="ExternalOutput")
        _phase_b(k, nc, TB, N, NBK, xTs, yTs, WG, WP, WO, WFG, WFU, WD, gn_d, cst_d, outT)
        k.S.emit(final_waits=[outT.res])
    return nc


def _phase_b(k, nc, TB, N, NBK, xTs, yTs, WG, WP, WO, WFG, WFU, WD, gn_d, cst_d, outT, yg=None):
    if k.psr:
        psn = k.psr[7]
        k.psr = k.psr[0:7]
    else:
        k.psr = [T(k.st.enter_context(nc.psum_tensor("psb%d" % i, [128, 512], F32)), "psb%d" % i, True) for i in range(7)]
        psn = T(k.st.enter_context(nc.psum_tensor("psn", [128, 512], F32)), "psn", True)
    k.psi = 0
    ones = k.sb("b_ones", [128, 128])
    k.dma(ones[:], cst_d[:, ONE * 128:(ONE + 1) * 128])
    gn = k.sb("b_gn", [128, 64])
    k.dma(gn[:], gn_d[:, :])
    xt = k.sb("b_xt", [128, NJ, N])
    u = k.sb("b_u", [128, NJ, N], BF16)
    ybf = k.sb("b_ybf", [128, NJ, N], BF16)
    mg = k.sb("b_mg", [128, NJ, N], BF16)
    o = k.sb("b_o", [128, NJ, N])
    f = k.sb("b_f", [128, 44, N], BF16)
    ystg = k.ring("b_ystg", 2, [128, N])
    sqt = k.ring("b_sqt", 2, [128, N])
    rstd = k.sb("b_rstd", [128, N])
    sg = k.ring("b_sg", 4, [128, N])
    mt = k.ring("b_mt", 2, [128, N])
    wstg = k.ring("b_wstg", 2, [128, NJ, 128])
    wb = k.ring("b_wb", 3, [128, NJ, 128], BF16)
    cnt = [0]

    def unit(src_v, nj, gain_col=None):
        s = wstg.next()
        k.dma(s.v(lambda h: h[:, 0:nj, :].rearrange("p j c -> p (j c)")), src_v)
        w = wb.next()
        eng = "pool" if cnt[0] % 2 == 0 else "dve"
        cnt[0] += 1
        if gain_col is None:
            k.cp(eng, w[:, 0:nj, :], s[:, 0:nj, :])
        else:
            gcol = gn[:, gain_col:gain_col + nj]
            k.tt(eng, w[:, 0:nj, :], s[:, 0:nj, :],
                 V(gcol.ap.unsqueeze(2).to_broadcast([128, nj, 128]), gcol.res), ALU.mult)
        return w

    def bc_j(t_):
        return V(t_.h[:].unsqueeze(1).to_broadcast([128, NJ, N]), t_.res)

    def norm_stats(src_fn, nrow):
        for r in range(nrow):
            s2 = sqt.next()
            k.act(s2[:], src_fn(r), AF.Square)
            k.mm(psn[:, 0:N], ones[:], s2[:], start=(r == 0), stop=(r == nrow - 1))
        k.rsqrt(rstd[:], psn[:, 0:N], scale=1.0 / D_MODEL, bias=1e-6)

    for bk in range(NBK):
        ts_ = slice(bk * N, (bk + 1) * N)
        k.dma_multi([(xt[:, jq * 4:(jq + 1) * 4, :],
                      xTs.v(lambda h, jq=jq: h[jq * 512:(jq + 1) * 512, ts_].rearrange("(j p) t -> p j t", p=128)))
                     for jq in range(4)])
        norm_stats(lambda r: xt[:, r, :], NJ)
        k.tt("dve", u[:], xt[:], bc_j(rstd), ALU.mult)
        if yg is None:
            for j in range(NJ):
                ys = ystg.next()
                k.dma(ys[:], yTs[j * 128:(j + 1) * 128, ts_])
                k.cp("pool", ybf[:, j, :], ys[:])
        else:
            def ld(e, bk=bk):
                if "pid" not in PIDC:
                    PIDC["pid"] = e.partition_id()
                pid = PIDC["pid"]
                off = e.snap(pid * (TB // N) + bk)
                ygv = yg.h.rearrange("(r h p) t -> h p r t", h=2, p=128)
                return [e.dma_start(out=ybf.h[:, hh * 8:(hh + 1) * 8, :], in_=ygv[hh, :, :, bass.ts(off, N)])
                        for hh in range(2)]
            k.S.op("sp", ld, reads=[yg.res], writes=[ybf.res], dma=True, ndma=2)
        for r in range(NJ):
            sgs = []
            for gi in range(2):
                w = unit(WG[gi * 16 + r, :, :], NJ, gain_col=0)
                p = k.ps()
                for j in range(NJ):
                    k.mm(p[:, 0:N], w[:, j, :], u[:, j, :], start=(j == 0), stop=(j == NJ - 1))
                s_ = sg.next()
                k.act(s_[:], p[:, 0:N], AF.Sigmoid)
                sgs.append(s_)
            w = unit(WP[r, :, :], NJ)
            pa = k.ps()
            pb = k.ps()
            for j in range(8):
                k.mm(pa[:, 0:N], w[:, j, :], ybf[:, j, :], start=(j == 0), stop=(j == 7))
            for j in range(8, 16):
                k.mm(pb[:, 0:N], w[:, j, :], ybf[:, j, :], start=(j == 8), stop=(j == 15))
            m1 = mt.next()
            k.tt("dve", m1[:], pa[:, 0:N], sgs[0][:], ALU.mult)
            m2 = mt.next()
            k.tt("dve", m2[:], pb[:, 0:N], sgs[1][:], ALU.mult)
            k.tt("pool", mg[:, r, :], m1[:], m2[:], ALU.add)
        for r in range(NJ):
            w = unit(WO[r, :, :], NJ)
            p = k.ps()
            for j in range(NJ):
                k.mm(p[:, 0:N], w[:, j, :], mg[:, j, :], start=(j == 0), stop=(j == NJ - 1))
            k.cp("act", o[:, r, :], p[:, 0:N])
        norm_stats(lambda r: o[:, r, :], NJ)
        for r in range(NJ):
            m1 = mt.next()
            k.tt("dve", m1[:], o[:, r, :], rstd[:], ALU.mult)
            k.stt("dve", xt[:, r, :], m1[:], gn[:, 16 + r:17 + r], xt[:, r, :], ALU.mult, ALU.add)
        norm_stats(lambda r: xt[:, r, :], NJ)
        k.tt("dve", u[:], xt[:], bc_j(rstd), ALU.mult)
        for r in range(44):
            w = unit(WFG[r, :, :], NJ, gain_col=32)
            pg = k.ps()
            for j in range(NJ):
                k.mm(pg[:, 0:N], w[:, j, :], u[:, j, :], start=(j == 0), stop=(j == NJ - 1))
            w2 = unit(WFU[r, :, :], NJ, gain_col=32)
            pu = k.ps()
            for j in range(NJ):
                k.mm(pu[:, 0:N], w2[:, j, :], u[:, j, :], start=(j == 0), stop=(j == NJ - 1))
            s_ = sg.next()
            k.act(s_[:], pg[:, 0:N], AF.Silu)
            k.tt("dve", f[:, r, :], s_[:], pu[:, 0:N], ALU.mult)
        for r in range(NJ):
            p = k.ps()
            j0 = 0
            for piece in (16, 16, 12):
                w = unit(WD.v(lambda h: h[r, :, j0 * 128:(j0 + piece) * 128]), piece)
                for jj in range(piece):
                    j = j0 + jj
                    k.mm(p[:, 0:N], w[:, jj, :], f[:, j, :], start=(j == 0), stop=(j == 43))
                j0 += piece
            k.cp("act", o[:, r, :], p[:, 0:N])
        norm_stats(lambda r: o[:, r, :], NJ)
        for r in range(NJ):
            m1 = mt.next()
            k.tt("dve", m1[:], o[:, r, :], rstd[:], ALU.mult)
            k.stt("dve", xt[:, r, :], m1[:], gn[:, 48 + r:49 + r], xt[:, r, :], ALU.mult, ALU.add)
        k.dma_multi([(outT.v(lambda h, jq=jq: h[jq * 512:(jq + 1) * 512, ts_].rearrange("(j p) t -> p j t", p=128)),
                      xt[:, jq * 4:(jq + 1) * 4, :]) for jq in range(4)])


def _tiles(W, ktiles):
    K_, R_ = W.shape
    return np.ascontiguousarray(W.reshape(ktiles, 128, R_ // 128, 128).transpose(2, 1, 0, 3)).reshape(R_ // 128, 128, ktiles * 128)


def phase_b_weights(inp, consts):
    W = inp["w_in"][0]
    d = {}
    d["WG"] = _tiles(W[:, GATE0:GATE0 + 4096], 16)
    wp = np.concatenate([inp["w_branch_rw"][0], inp["w_branch_gdn"][0]], axis=0)
    d["WP"] = _tiles(wp, 16)
    d["WO"] = _tiles(inp["w_out"][0], 16)
    d["WFG"] = _tiles(inp["w_ffn_gate"][0], 16)
    d["WFU"] = _tiles(inp["w_ffn_up"][0], 16)
    d["WD"] = _tiles(inp["w_ffn_down"][0], 44)
    gn = np.zeros((128, 64), np.float32)
    for i, nm in enumerate(["norm_pre_mix", "norm_post_mix", "norm_pre_ffn", "norm_post_ffn"]):
        gn[:, i * 16:(i + 1) * 16] = inp[nm][0].reshape(NJ, 128).T
    d["gn"] = gn
    d["cst"] = consts.reshape(128, NCST * 128)
    return d


def build_fused(NB, SEQ, n=8):
    NTOK = NB * SEQ
    NCH = NTOK // 128
    NBLK = NTOK // 256
    CPS = SEQ // 128
    TB = NTOK // n
    N = min(512, TB)
    NBK = TB // N
    nc = bass.Bass("TRN2", target_bir_lowering=False)
    PIDC.clear()
    k = KB(nc)
    rg = [list(range(n))]
    with k.st:
        xT = k.dram("xT", [D_MODEL, NTOK], kind="ExternalInput")
        WA = k.dram("WA", [11, 128, NJ * 128], kind="ExternalInput")
        prm_d = k.dram("prm", [128, NPRM], kind="ExternalInput")
        rowc_d = k.dram("rowc", [128, NROWC], kind="ExternalInput")
        lw_d = k.dram("lw", [128, 768], kind="ExternalInput")
        cst_d = k.dram("cst", [128, NCST * 128], kind="ExternalInput")
        xTs = k.dram("xTs", [D_MODEL, TB], kind="ExternalInput")
        WG = k.dram("WG", [32, 128, 2048], kind="ExternalInput")
        WP = k.dram("WP", [16, 128, 2048], kind="ExternalInput")
        WO = k.dram("WO", [16, 128, 2048], kind="ExternalInput")
        WFG = k.dram("WFG", [44, 128, 2048], kind="ExternalInput")
        WFU = k.dram("WFU", [44, 128, 2048], kind="ExternalInput")
        WD = k.dram("WD", [16, 128, 44 * 128], kind="ExternalInput")
        gn_d = k.dram("gn", [128, 64], kind="ExternalInput")
        outT = k.dram("outT", [D_MODEL, TB], kind="ExternalOutput")
        yT = k.dram("yT_loc", [256, NTOK], BF16)
        yg = T(nc.dram_tensor("yg", [n * 256, NTOK], BF16, addr_space="Shared").ap(), "yg")
        fin = k.dram("fence_in", [1, 64])
        fout = k.dram("fence_out", [1, 64])
        rw_fm = k.dram("rw_fm", [NCH, 4, 64, 512])
        rw_tm = k.dram("rw_tm", [NCH, 128, 4 * 256])
        rw_dv = k.dram("rw_dv", [NCH, 64, 4])
        rw_post = k.dram("rw_post", [NCH, 128, 258])
        rw_y = k.dram("rw_y", [NCH, 128, 256])
        gd_fm = k.dram("gd_fm", [NCH, 2, 128, 512])
        gd_tm = k.dram("gd_tm", [NCH, 2, 128, 384])
        gd_lr = k.dram("gd_lr", [NCH, 2, 2, 256])
        gd_dv = k.dram("gd_dv", [NCH, 2, 128, 1])
        gd_post = k.dram("gd_post", [NCH, 128, 128])
        gd_o = k.dram("gd_o", [NCH, 2, 128, 128])

        k.init_psum(8)
        with k.phase():
            cst = k.sb("cst", [128, NCST, 128])
            k.dma(cst.v(lambda h: h[:].rearrange("p a b -> p (a b)")), cst_d[:, :])
            prm = k.sb("prm", [128, NPRM])
            k.dma(prm[:], prm_d[:, :])
            rowc = k.sb("rowc", [128, NROWC])
            k.dma(rowc[:], rowc_d[:, :])
            lw = k.sb("lw", [128, 768])
            k.dma(lw[:], lw_d[:, :])

            def CM(i):
                return cst[:, i, :]

            k.dma(fin[:, :], cst[0:1, ONE, 0:64])
            with k.phase():
                _A1R.clear()
                _A1G.clear()
                _phase_a1(k, nc, NB, SEQ, NTOK, NCH, NBLK, xT, WA, prm, rowc, lw, cst, CM,
                          rw_fm, rw_tm, rw_dv, rw_post, gd_fm, gd_tm, gd_lr, gd_dv, gd_post)
            with k.phase():
                _phase_a2(k, nc, NB, SEQ, NCH, CPS, cst, CM, rw_fm, rw_tm, rw_dv, rw_y, gd_fm, gd_tm, gd_lr, gd_dv, gd_o)
            with k.phase():
                _phase_a3(k, nc, NCH, rowc, CM, rw_post, rw_y, gd_post, gd_o, yT, ydt=BF16)
        yin, yout = yT.h.opt(), yg.h.opt()
        k.S.op("pool", lambda e: e.collective_compute("AllGather", ALU.bypass, replica_groups=rg, ins=[yin], outs=[yout]),
               reads=[yT.res], writes=[yg.res], cc=True)
        fi, fo = fin.h.opt(), fout.h.opt()
        k.S.op("pool", lambda e: e.collective_compute("AllReduce", ALU.add, replica_groups=rg, ins=[fi], outs=[fo]),
               reads=[fin.res, yg.res], writes=[fout.res, yg.res], cc=True)
        k.S.barrier()
        _phase_b(k, nc, TB, N, NBK, xTs, None, WG, WP, WO, WFG, WFU, WD, gn_d, cst_d, outT, yg=yg)
        k.S.emit(final_waits=[outT.res])
    return nc


def kernel_unfused(**inp):
    return _kernel_impl(False, **inp)


def kernel(**inp):
    return _kernel_impl(True, **inp)


def _kernel_impl(fused, **inp):
    inp = {kk: np.asarray(v) for kk, v in inp.items()}
    x = inp["x"]
    NB, SEQ, _ = x.shape
    NTOK = NB * SEQ
    n = 8
    TB = NTOK // n
    consts = make_consts()
    xT = np.ascontiguousarray(x.reshape(NTOK, D_MODEL).T)
    wts = phase_b_weights(inp, consts)
    if fused:
        nc = build_fused(NB, SEQ, n)
        ims = []
        for c in range(n):
            d = phase_a_inputs(c, inp, consts)
            d.update(wts)
            d["xT"] = xT
            d["xTs"] = np.ascontiguousarray(xT[:, c * TB:(c + 1) * TB])
            ims.append(d)
        rb = run_bass_kernel_spmd(nc, ims, core_ids=list(range(n)))
    else:
        nca = build_phase_a(NB, SEQ)
        in_a = []
        for c in range(n):
            d = phase_a_inputs(c, inp, consts)
            d["xT"] = xT
            in_a.append(d)
        ra = run_bass_kernel_spmd(nca, in_a, core_ids=list(range(n)))
        yT = np.zeros((D_MODEL, NTOK), np.float32)
        for c in range(n):
            y = np.asarray(ra.results[c]["yT"])
            yT[128 * c:128 * c + 128] = y[0:128]
            yT[1024 + 128 * c:1024 + 128 * c + 128] = y[128:256]
        ncb = build_phase_b(TB)
        in_b = []
        for c in range(n):
            d = dict(wts)
            d["xTs"] = np.ascontiguousarray(xT[:, c * TB:(c + 1) * TB])
            d["yTs"] = np.ascontiguousarray(yT[:, c * TB:(c + 1) * TB])
            in_b.append(d)
        rb = run_bass_kernel_spmd(ncb, in_b, core_ids=list(range(n)))
    out = np.zeros((NTOK, D_MODEL), np.float32)
    for c in range(n):
        out[c * TB:(c + 1) * TB] = np.asarray(rb.results[c]["outT"]).T
    return out.reshape(NB, SEQ, D_MODEL)


def _invert_g(k, R, B, A0, N0, CM):
    ident = CM(IDN)
    TN = [R["TN0"], R["TN1"]]
    Ak = [R["Ak0"], R["Ak1"]]
    k.tt("pool", TN[0][:, 0:128], N0, ident, ALU.add)
    k.mm(B[:, 0:128], A0, N0)
    k.mm(B[:, 128:256], N0, A0)
    yield
    k.cp("act", TN[0][:, 128:256], B[:, 0:128])
    k.cp("dve", Ak[0][:], B[:, 128:256])
    yield
    cur = 0
    for lv in range(2, 8):
        a_prev, tn_prev, tn_new = Ak[cur], TN[cur], TN[1 - cur]
        if lv < 7:
            k.mm(B[:, 0:256], a_prev[:], tn_prev[:, 0:256])
            k.mm(B[:, 256:384], tn_prev[:, 128:256], a_prev[:])
            yield
            k.tt("dve", tn_new[:, 0:128], tn_prev[:, 0:128], B[:, 0:128], ALU.add)
            k.cp("act", tn_new[:, 128:256], B[:, 128:256])
            k.cp("act" if lv % 2 else "dve", Ak[1 - cur][:], B[:, 256:384])
            yield
        else:
            k.mm(B[:, 0:128], a_prev[:], tn_prev[:, 0:128])
            yield
            k.tt("dve", tn_new[:, 0:128], tn_prev[:, 0:128], B[:, 0:128], ALU.add)
            yield
        cur = 1 - cur
    return TN[cur][:, 0:128]


def _rw_step_g(k, R, B, cst, CM, dd, g, hd, Zo, Zn, rw_fm, rw_tm, rw_dv, rw_y):
    fm = R["rfm"].next()
    k.dma(fm[:], rw_fm[g, hd, :, :])
    tm = R["rtm"].next()
    k.dma(tm[:], rw_tm.v(lambda hh: hh[g, :, hd * 256:(hd + 1) * 256]))
    rdv = R["dv"].next()
    k.dma(rdv[0:64, 0:4], rw_dv[g, :, :])
    aT, rT, bT, kT = fm[:, 0:128], fm[:, 128:256], fm[:, 256:384], fm[:, 384:512]
    A_tm, Bh, Kh, Vt = tm[:, 0:64], tm[:, 64:128], tm[:, 128:192], tm[:, 192:256]
    msk = V(cst.h[:, dd["MTs"]:dd["MTs"] + 2, :].rearrange("p a b -> p (a b)"), cst.res)
    k.mm(B[:, 0:128], aT, bT)
    k.mm(B[:, 128:384], bT, fm[:, 0:256])
    yield
    A0, NBt, CK = R["A0"], R["NB_"], R["CK"]
    k.tt("dve", A0[:], B[:, 0:128], CM(dd["Ms"]), ALU.mult)
    k.tt("dve", NBt[:], B[:, 128:384], msk, ALU.mult)
    yield
    k.mm(B[:, 0:256], kT, fm[:, 0:256])
    yield
    k.tt("dve", CK[:], B[:, 0:256], msk, ALU.mult)
    yield
    TT = yield from _invert_g(k, R, B, A0[:], NBt[:, 0:128], CM)
    k.mm(B[:, 0:64], CK[:, 0:128], Vt)
    yield
    akv, U, WT, X, O = R["akv"], R["U"], R["WT"], R["X"], R["O"]
    k.cp("act", akv[:, 0:64], B[:, 0:64])
    yield
    k.mm(B[:, 0:64], TT, akv[:, 0:64])
    k.mm(B[0:64, 128:256], A_tm, TT)
    yield
    k.cp("act", U[:, 0:64], B[:, 0:64])
    k.cp("dve", WT[0:64, :], B[0:64, 128:256])
    yield
    k.mm(B[:, 0:64], WT[0:64, :], Zo[:])
    yield
    k.tt("dve", X[:, 0:64], B[:, 0:64], U[:, 0:64], ALU.add)
    yield
    k.mm(B[0:64, 64:128], Kh, Vt, start=True, stop=False)
    k.mm(B[0:64, 64:128], Bh, X[:, 0:64], start=False, stop=True)
    k.mm(B[:, 128:192], rT, Zo[:], start=True, stop=False)
    k.mm(B[:, 128:192], NBt[:, 128:256], X[:, 0:64], start=False, stop=False)
    k.mm(B[:, 128:192], CK[:, 128:256], Vt, start=False, stop=True)
    yield
    k.stt("dve", Zn[:], Zo[:], rdv[0:64, hd:hd + 1], B[0:64, 64:128], ALU.mult, ALU.add)
    k.cp("act", O[:, 0:64], B[:, 128:192])
    k.dma(rw_y.v(lambda hh: hh[g, :, hd * 64:(hd + 1) * 64]), O[:, 0:64])


def _gd_step_g(k, R, B, cst, CM, dd, g, d, Zo, Zn, gd_fm, gd_tm, gd_lr, gd_dv, gd_o):
    ident = CM(IDN)
    fm = R["fm"].next()
    k.dma(fm[:], gd_fm[g, d, :, :])
    tm = R["tm"].next()
    k.dma(tm[:], gd_tm[g, d, :, :])
    lr = R["lr"].next()
    k.dma(lr[:], gd_lr[g, d, :, :])
    gdv = R["dv"].next()
    k.dma(gdv[:, 0:1], gd_dv[g, d, :, :])
    kT, qT, kbT, qgT = fm[:, 0:128], fm[:, 128:256], fm[:, 256:384], fm[:, 384:512]
    vb, kbg, kg = tm[:, 0:128], tm[:, 128:256], tm[:, 256:384]
    k.mm(B[:, 0:128], lr[:, 0:128], lr[:, 128:256])
    k.mm(B[:, 128:256], lr[:, 128:256], lr[:, 0:128])
    yield
    m, D, A0, NBt = R["m"], R["D"], R["A0"], R["NB_"]
    k.stt("dve", m[:, 0:128], B[:, 0:128], 0.0, CM(dd["Ns"]), ALU.min, ALU.add)
    k.stt("dve", m[:, 128:256], B[:, 128:256], 0.0, CM(dd["NTs"]), ALU.min, ALU.add)
    yield
    k.act(D[:, 0:256], m[:, 0:256], AF.Exp)
    k.mm(B[:, 0:128], kbT, kT)
    k.mm(B[:, 128:384], kT, fm[:, 128:384])
    yield
    k.tt("pool", D[:, 256:384], D[:, 128:256], ident, ALU.add)
    k.stt("dve", A0[:], B[:, 0:128], -1.0, D[:, 0:128], ALU.mult, ALU.mult)
    k.stt("dve", NBt[:, 0:128], B[:, 256:384], -1.0, D[:, 128:256], ALU.mult, ALU.mult)
    yield
    k.tt("dve", NBt[:, 128:256], B[:, 128:256], D[:, 256:384], ALU.mult)
    yield
    TT = yield from _invert_g(k, R, B, A0[:], NBt[:, 0:128], CM)
    k.mm(B[:, 0:128], TT, vb)
    k.mm(B[:, 128:256], kbg, TT)
    yield
    U, WT, X, O = R["U"], R["WT"], R["X"], R["O"]
    k.cp("act", U[:], B[:, 0:128])
    k.ts("dve", WT[:], B[:, 128:256], -1.0, None, ALU.mult)
    yield
    k.mm(B[:, 0:128], WT[:], Zo[:])
    yield
    k.tt("dve", X[:], B[:, 0:128], U[:], ALU.add)
    yield
    k.mm(B[:, 128:256], kg, X[:])
    k.mm(B[:, 256:384], qgT, Zo[:], start=True, stop=False)
    k.mm(B[:, 256:384], NBt[:, 128:256], X[:], start=False, stop=True)
    yield
    k.stt("dve", Zn[:], Zo[:], gdv[:, 0:1], B[:, 128:256], ALU.mult, ALU.add)
    k.cp("act", O[:], B[:, 256:384])
    k.dma(gd_o[g, d, :, :], O[:])


def _phase_a2(k, nc, NB, SEQ, NCH, CPS, cst, CM, rw_fm, rw_tm, rw_dv, rw_y, gd_fm, gd_tm, gd_lr, gd_dv, gd_o):
    NRW = NB * 4
    NGD = NB * 2
    Zrw = [[k.sb("zrw%d_%d" % (s, i), [64, 64]) for i in range(2)] for s in range(NRW)]
    Zgd = [[k.sb("zgd%d_%d" % (s, i), [128, 128]) for i in range(2)] for s in range(NGD)]
    for s in range(NRW):
        k.memset("pool", Zrw[s][0][:], 0.0)
    for s in range(NGD):
        k.memset("pool", Zgd[s][0][:], 0.0)
    slots = []
    for sl in range(6):
        t = "s%d_" % sl
        R = dict(TN0=k.sb(t + "TN0", [128, 256]), TN1=k.sb(t + "TN1", [128, 256]),
                 Ak0=k.sb(t + "Ak0", [128, 128]), Ak1=k.sb(t + "Ak1", [128, 128]),
                 A0=k.sb(t + "A0", [128, 128]), NB_=k.sb(t + "NB", [128, 256]),
                 U=k.sb(t + "U", [128, 128]), WT=k.sb(t + "WT", [128, 128]),
                 X=k.sb(t + "X", [128, 128]), O=k.sb(t + "O", [128, 128]),
                 dv=k.ring(t + "dv", 2, [128, 4]))
        if sl < 4:
            R.update(CK=k.sb(t + "CK", [128, 256]), akv=k.sb(t + "akv", [128, 64]),
                     rfm=k.ring(t + "rfm", 2, [64, 512]), rtm=k.ring(t + "rtm", 2, [128, 256]))
        else:
            R.update(D=k.sb(t + "D", [128, 384]), m=k.sb(t + "m", [128, 256]),
                     fm=k.ring(t + "fm", 2, [128, 512]), tm=k.ring(t + "tm", 2, [128, 384]),
                     lr=k.ring(t + "lr", 2, [2, 256]))
        slots.append(R)
    banks = k.psr[0:6]
    for step in range(CPS):
        par = step % 2
        for bi in range(NB):
            gens = []
            for d in range(2):
                dd = DIRS[d]
                ci = step if d == 0 else CPS - 1 - step
                g = bi * CPS + ci
                for h in range(2):
                    hd = d * 2 + h
                    s = bi * 4 + hd
                    gens.append(_rw_step_g(k, slots[hd], banks[hd], cst, CM, dd, g, hd,
                                           Zrw[s][par], Zrw[s][1 - par], rw_fm, rw_tm, rw_dv, rw_y))
                s = bi * 2 + d
                gens.append(_gd_step_g(k, slots[4 + d], banks[4 + d], cst, CM, dd, g, d,
                                       Zgd[s][par], Zgd[s][1 - par], gd_fm, gd_tm, gd_lr, gd_dv, gd_o))
            act_ = list(gens)
            while act_:
                for gg in list(act_):
                    try:
                        next(gg)
                    except StopIteration:
                        act_.remove(gg)
```

```python
import contextlib
import numpy as np
import concourse.bass as bass
import concourse.mybir as mybir
from concourse.bass_utils import run_bass_kernel_spmd

F32 = mybir.dt.float32
BF16 = mybir.dt.bfloat16
AF = mybir.ActivationFunctionType
ALU = mybir.AluOpType
AX = mybir.AxisListType

D_MODEL = 2048
NJ = 16
RW_IN = 3488
G0 = 3488
GATE0 = 3488 + 3104
FFN = 5632
C0 = -0.6065306597126334
NEG = -30000.0
DBG = {}
PIDC = {}


class Res:
    __slots__ = ("name", "w", "rs", "excl")

    def __init__(self, name="", excl=False):
        self.name = name
        self.w = None
        self.rs = []
        self.excl = excl


class V:
    __slots__ = ("ap", "res")

    def __init__(self, ap, res):
        self.ap = ap
        self.res = res

    def __getitem__(self, idx):
        return V(self.ap[idx], self.res)


class T:
    def __init__(self, h, name, excl=False):
        self.h = h
        self.res = Res(name, excl)

    def __getitem__(self, idx):
        return V(self.h[idx], self.res)

    def v(self, fn):
        return V(fn(self.h), self.res)


class Sched:
    ENG = ("pe", "dve", "act", "pool", "sp")

    def __init__(self, nc, n_dma_sems=16):
        self.nc = nc
        self.ops = {e: [] for e in self.ENG}
        self.cnt = {e: 0 for e in self.ENG}
        self.seen = {e: {} for e in self.ENG}
        self.n_dma_sems = n_dma_sems
        self.dma_use = [0] * n_dma_sems
        self.dma_rr = 0
        self.total = 0
        self.pending = {e: {} for e in self.ENG}
        self.ncc = 0
        self.last_ev = None
        self.last_cev = None
        self.last_g = None
        self.last_dev = None

    def barrier(self):
        allev = {}
        for e in self.ENG:
            if self.cnt[e] > 0 and e != "sp":
                allev[e] = self.cnt[e]
        for i in range(self.n_dma_sems):
            if self.dma_use[i] > 0:
                allev[("dma", i)] = 16 * self.dma_use[i]
        if self.ncc > 0:
            allev["cc"] = self.ncc
        for e in self.ENG:
            for kk, vv in allev.items():
                if self.pending[e].get(kk, 0) < vv:
                    self.pending[e][kk] = vv

    @staticmethod
    def _add(deps, ev):
        if ev is None:
            return
        k, v = ev
        if deps.get(k, 0) < v:
            deps[k] = v

    def op(self, eng, fn, reads=(), writes=(), dma=False, ndma=1, cc=False):
        if eng == "pool" and DBG.get("nopool", 0):
            eng = "dve"
        deps = {}
        for r in reads:
            self._add(deps, r.w)
            if r.excl:
                for ev in r.rs:
                    self._add(deps, ev)
        for w in writes:
            self._add(deps, w.w)
            for ev in w.rs:
                self._add(deps, ev)
        if cc:
            self.ncc += 1
            ev = ("cc", self.ncc)
        elif dma:
            i = self.dma_rr
            self.dma_rr = (self.dma_rr + 1) % self.n_dma_sems
            if self.dma_use[i] > 0:
                self._add(deps, (("dma", i), 16 * self.dma_use[i]))
            self.dma_use[i] += ndma
            ev = (("dma", i), 16 * self.dma_use[i])
        else:
            self.cnt[eng] += 1
            ev = (eng, self.cnt[eng])
        if eng == "pe":
            deps.pop("pe", None)
        sm = DBG.get("serial", 0)
        if sm == 1 and self.last_ev is not None:
            self._add(deps, self.last_ev)
        if sm == 2 and not dma and self.last_cev is not None:
            self._add(deps, self.last_cev)
        grp = {5: ("act", "dve", "pool"), 6: ("pe", "act"), 7: ("pe", "dve"), 8: ("act", "dve")}.get(sm)
        if grp and not dma and eng in grp and self.last_g is not None:
            self._add(deps, self.last_g)
        if sm == 3 and dma and self.last_ev is not None:
            self._add(deps, self.last_ev)
        if sm == 3 and not dma and self.last_dev is not None:
            self._add(deps, self.last_dev)
        if self.pending[eng]:
            for kk, vv in self.pending[eng].items():
                if deps.get(kk, 0) < vv:
                    deps[kk] = vv
            self.pending[eng] = {}
        seen = self.seen[eng]
        waits = []
        for k, v in deps.items():
            if seen.get(k, 0) < v:
                seen[k] = v
                waits.append((k, v))
        self.ops[eng].append((fn, waits, ev))
        self.last_ev = ev
        grp = {5: ("act", "dve", "pool"), 6: ("pe", "act"), 7: ("pe", "dve"), 8: ("act", "dve")}.get(DBG.get("serial", 0))
        if grp and not dma and eng in grp:
            self.last_g = ev
        if dma:
            self.last_dev = ev
        else:
            self.last_cev = ev
        for r in reads:
            r.rs.append(ev)
        for w in writes:
            w.w = ev
            w.rs = []
        self.total += 1
        return ev

    def emit(self, final_waits=()):
        nc = self.nc
        sems = {}
        with contextlib.ExitStack() as st:
            for e in self.ENG:
                sems[e] = st.enter_context(nc.semaphore("s_" + e))
            for i in range(self.n_dma_sems):
                sems[("dma", i)] = st.enter_context(nc.semaphore("s_dma%d" % i))
            sems["cc"] = st.enter_context(nc.semaphore("s_cc"))
            deps = {}
            for r in final_waits:
                self._add(deps, r.w)
            fw = list(deps.items())
            block = st.enter_context(nc.Block())

            def mk(ename):
                def body(eng):
                    for fn, waits, ev in self.ops[ename]:
                        for k, v in waits:
                            eng.wait_ge(sems[k], v)
                        ins = fn(eng)
                        k, v = ev
                        if k == "cc":
                            ins.then_inc(sems[k])
                        elif isinstance(ins, list):
                            for i_ in ins:
                                i_.then_inc(sems[k], 16)
                        else:
                            ins.then_inc(sems[k], 16 if isinstance(k, tuple) else 1)
                    if ename == "sp":
                        for k, v in fw:
                            eng.wait_ge(sems[k], v)
                        for i in range(self.n_dma_sems):
                            if self.dma_use[i] > 0:
                                eng.wait_ge(sems[("dma", i)], 16 * self.dma_use[i])
                return body

            block.tensor(mk("pe"))
            block.vector(mk("dve"))
            block.scalar(mk("act"))
            block.gpsimd(mk("pool"))
            block.sync(mk("sp"))


class KB:
    def __init__(self, nc):
        self.nc = nc
        self.S = Sched(nc)
        self.st = contextlib.ExitStack()
        self.cur = self.st
        self.nps = 0
        self.psr = []
        self.psi = 0

    def sb(self, name, shape, dt=F32):
        return T(self.cur.enter_context(self.nc.sbuf_tensor("sb_" + name, list(shape), dt)), name)

    @contextlib.contextmanager
    def phase(self):
        old = self.cur
        with contextlib.ExitStack() as sub:
            self.cur = sub
            yield
            self.cur = old
        self.S.barrier()

    def ring(self, name, n, shape, dt=F32):
        return Ring([self.sb("%s%d" % (name, i), shape, dt) for i in range(n)])

    def init_psum(self, n=8):
        self.psr = [T(self.st.enter_context(self.nc.psum_tensor("ps%d" % i, [128, 512], F32)), "ps%d" % i, True)
                    for i in range(n)]

    def ps(self):
        t = self.psr[self.psi]
        self.psi = (self.psi + 1) % len(self.psr)
        return t

    def dram(self, name, shape, dt=F32, kind="Internal"):
        return T(self.nc.dram_tensor(name, list(shape), dt, kind=kind).ap(), name)

    @staticmethod
    def _rs(*vs):
        return [x.res for x in vs if isinstance(x, V)]

    @staticmethod
    def _a(x):
        return x.ap if isinstance(x, V) else x

    def mm(self, out, lhsT, rhs, start=True, stop=True):
        self.S.op("pe", lambda e: e.matmul(out.ap, lhsT=lhsT.ap, rhs=rhs.ap, start=start, stop=stop),
                  reads=[lhsT.res, rhs.res], writes=[out.res])

    def tr(self, out, in_, ident):
        self.S.op("pe", lambda e: e.transpose(out.ap, in_.ap, ident.ap),
                  reads=[in_.res, ident.res], writes=[out.res])

    def act(self, out, in_, func, scale=1.0, bias=0.0, eng="act"):
        a = self._a
        self.S.op(eng, lambda e: e.activation(out=out.ap, in_=in_.ap, func=func, bias=a(bias), scale=a(scale)),
                  reads=self._rs(in_, scale, bias), writes=[out.res])

    def tt(self, eng, out, a, b, op):
        self.S.op(eng, lambda e: e.tensor_tensor(out=out.ap, in0=a.ap, in1=b.ap, op=op),
                  reads=[a.res, b.res], writes=[out.res])

    def ts(self, eng, out, a, s1, s2, op0, op1=None):
        g = self._a
        if op1 is None:
            self.S.op(eng, lambda e: e.tensor_scalar(out=out.ap, in0=a.ap, scalar1=g(s1), scalar2=None, op0=op0),
                      reads=self._rs(a, s1), writes=[out.res])
        else:
            self.S.op(eng, lambda e: e.tensor_scalar(out=out.ap, in0=a.ap, scalar1=g(s1), scalar2=g(s2),
                                                     op0=op0, op1=op1),
                      reads=self._rs(a, s1, s2), writes=[out.res])

    def stt(self, eng, out, a, s, b, op0, op1):
        g = self._a
        self.S.op(eng, lambda e: e.scalar_tensor_tensor(out=out.ap, in0=a.ap, scalar=g(s), in1=b.ap,
                                                        op0=op0, op1=op1),
                  reads=self._rs(a, s, b), writes=[out.res])

    def cp(self, eng, out, a):
        if eng == "act":
            self.S.op(eng, lambda e: e.copy(out=out.ap, in_=a.ap), reads=[a.res], writes=[out.res])
        else:
            self.S.op(eng, lambda e: e.tensor_copy(out=out.ap, in_=a.ap), reads=[a.res], writes=[out.res])

    def red(self, eng, out, a, op=None):
        self.S.op(eng, lambda e: e.tensor_reduce(out=out.ap, in_=a.ap, axis=AX.X, op=op or ALU.add),
                  reads=[a.res], writes=[out.res])

    def memset(self, eng, out, val):
        self.S.op(eng, lambda e: e.memset(out.ap, val), writes=[out.res])

    def dma(self, out, in_, eng="sp"):
        self.S.op(eng, lambda e: e.dma_start(out=out.ap, in_=in_.ap), reads=[in_.res], writes=[out.res], dma=True)

    def dma_multi(self, pairs, eng="sp"):
        self.S.op(eng, lambda e: [e.dma_start(out=o.ap, in_=i.ap) for o, i in pairs],
                  reads=[i.res for o, i in pairs], writes=[o.res for o, i in pairs], dma=True, ndma=len(pairs))

    def rsqrt(self, out, in_, scale=1.0, bias=0.0):
        self.act(out, in_, AF.Ln, scale=scale, bias=bias)
        self.act(out, out, AF.Exp, scale=-0.5)


class Ring:
    def __init__(self, tiles):
        self.tiles = tiles
        self.i = 0

    def next(self):
        t = self.tiles[self.i]
        self.i = (self.i + 1) % len(self.tiles)
        return t


IDN, ONE, LS, LI, US, UI, NLS, NUS = range(8)
NCST = 8
DIRS = {
    0: dict(Ms=LS, MTs=US, MTi=UI, Ti=UI, Ts=US, Tr=LS, Ns=NLS, NTs=NUS, last=127),
    1: dict(Ms=US, MTs=LS, MTi=LI, Ti=LI, Ts=LS, Tr=US, Ns=NUS, NTs=NLS, last=0),
}


def make_consts():
    i = np.arange(128)
    ls = (i[:, None] > i[None, :]).astype(np.float32)
    li = (i[:, None] >= i[None, :]).astype(np.float32)
    us = ls.T.copy()
    ui = li.T.copy()
    c = np.zeros((128, NCST, 128), np.float32)
    c[:, IDN] = np.eye(128)
    c[:, ONE] = 1.0
    c[:, LS] = ls
    c[:, LI] = li
    c[:, US] = us
    c[:, UI] = ui
    c[:, NLS] = (ls - 1.0) * (-NEG)
    c[:, NUS] = (us - 1.0) * (-NEG)
    return c


RC_W0, RC_A0, RC_KK, RC_KA, RC_RK, RC_GNW, RC_GNB, RC_DTB, RC_ALOG, RC_GNORM = 0, 256, 512, 640, 896, 1024, 1152, 1280, 1282, 1284
NROWC = 1284 + 128
PR_GAIN, PR_MU, PR_CW = 0, 16, 23
NPRM = 23 + 15


def build_phase_a(NB, SEQ, upto='a3'):
    NTOK = NB * SEQ
    NCH = NTOK // 128
    NBLK = NTOK // 256
    CPS = SEQ // 128
    nc = bass.Bass("TRN2", target_bir_lowering=False)
    k = KB(nc)
    with k.st:
        xT = k.dram("xT", [D_MODEL, NTOK], kind="ExternalInput")
        WA = k.dram("WA", [11, 128, NJ * 128], kind="ExternalInput")
        prm_d = k.dram("prm", [128, NPRM], kind="ExternalInput")
        rowc_d = k.dram("rowc", [128, NROWC], kind="ExternalInput")
        lw_d = k.dram("lw", [128, 768], kind="ExternalInput")
        cst_d = k.dram("cst", [128, NCST * 128], kind="ExternalInput")
        yT = k.dram("yT", [256, NTOK], kind="ExternalOutput")
        rw_fm = k.dram("rw_fm", [NCH, 4, 64, 512])
        rw_tm = k.dram("rw_tm", [NCH, 128, 4 * 256])
        rw_dv = k.dram("rw_dv", [NCH, 64, 4])
        rw_post = k.dram("rw_post", [NCH, 128, 258])
        rw_y = k.dram("rw_y", [NCH, 128, 256])
        gd_fm = k.dram("gd_fm", [NCH, 2, 128, 512])
        gd_tm = k.dram("gd_tm", [NCH, 2, 128, 384])
        gd_lr = k.dram("gd_lr", [NCH, 2, 2, 256])
        gd_dv = k.dram("gd_dv", [NCH, 2, 128, 1])
        gd_post = k.dram("gd_post", [NCH, 128, 128])
        gd_o = k.dram("gd_o", [NCH, 2, 128, 128])

        k.init_psum(8)
        cst = k.sb("cst", [128, NCST, 128])
        k.dma(cst.v(lambda h: h[:].rearrange("p a b -> p (a b)")), cst_d[:, :])
        prm = k.sb("prm", [128, NPRM])
        k.dma(prm[:], prm_d[:, :])
        rowc = k.sb("rowc", [128, NROWC])
        k.dma(rowc[:], rowc_d[:, :])
        lw = k.sb("lw", [128, 768])
        k.dma(lw[:], lw_d[:, :])

        def CM(i):
            return cst[:, i, :]

        ident = CM(IDN)
        ones = CM(ONE)

        with k.phase():
            _A1R.clear()
            _A1G.clear()
            if upto != 'a0':
                _phase_a1(k, nc, NB, SEQ, NTOK, NCH, NBLK, xT, WA, prm, rowc, lw, cst, CM,
                          rw_fm, rw_tm, rw_dv, rw_post, gd_fm, gd_tm, gd_lr, gd_dv, gd_post)
        if upto in ('a2', 'a3'):
            with k.phase():
                _phase_a2(k, nc, NB, SEQ, NCH, CPS, cst, CM, rw_fm, rw_tm, rw_dv, rw_y, gd_fm, gd_tm, gd_lr, gd_dv, gd_o)
        if upto == 'a3':
            with k.phase():
                _phase_a3(k, nc, NCH, rowc, CM, rw_post, rw_y, gd_post, gd_o, yT)
        else:
            k.dma(yT[0:128, 0:128], cst[:, 0, :])
        k.S.emit(final_waits=[yT.res])
    return nc


def _phase_a1(k, nc, NB, SEQ, NTOK, NCH, NBLK, xT, WA, prm, rowc, lw, cst, CM,
              rw_fm, rw_tm, rw_dv, rw_post, gd_fm, gd_tm, gd_lr, gd_dv, gd_post):
    ident = CM(IDN)
    ones = CM(ONE)
    WAs = k.sb("WAs", [128, 11, NJ, 128], BF16)
    wstg = k.ring("wstg", 1, [128, NJ, 128])
    gain = prm[:, PR_GAIN:PR_GAIN + 16]
    for t in range(11):
        s = wstg.next()
        k.dma(s.v(lambda h: h[:].rearrange("p j c -> p (j c)")), WA[t, :, :])
        k.tt("dve" if t % 2 == 0 else "pool", WAs[:, t, :, :], s[:],
             V(gain.ap.unsqueeze(2).to_broadcast([128, NJ, 128]), gain.res), ALU.mult)
    omm = k.sb("omm", [128, 7])
    hmu = k.sb("hmu", [128, 7])
    k.ts("dve", omm[:], prm[:, PR_MU:PR_MU + 7], -1.0, 1.0, ALU.mult, ALU.add)
    k.ts("dve", hmu[:], prm[:, PR_MU:PR_MU + 7], 0.5, None, ALU.mult)
    omka = k.sb("omka", [128, 256])
    k.ts("dve", omka[:], rowc[:, RC_KA:RC_KA + 256], -1.0, 1.0, ALU.mult, ALU.add)
    nA = k.sb("nA", [128, 2])
    k.act(nA[:], rowc[:, RC_ALOG:RC_ALOG + 2], AF.Exp)
    k.ts("dve", nA[:], nA[:], -1.0, None, ALU.mult)
    c10 = k.sb("c10", [2, 2])
    k.cp("dve", c10[:], V(ident.ap[0:2, 0:2], ident.res))
    cw = prm[:, PR_CW:PR_CW + 15]

    xt_r = k.ring("xt", 1, [128, NJ, 260])
    sq_r = k.ring("sq", 1, [128, NJ, 260])
    u_r = k.ring("u", 2, [128, NJ, 260], BF16)
    rstd_r = k.ring("rstd", 2, [128, 260])
    pa_r = k.ring("pa", 3, [128, 260])
    s_r = k.ring("shs", 2, [128, 256])
    m1_r = k.ring("shm", 2, [128, 256])
    pm = [k.ring("pm%d" % t, 2, [128, 256]) for t in range(7)]
    cacc = k.ring("cacc", 2, [128, 256])
    qkv = [k.ring("qkv%d" % t, 2, [128, 256]) for t in range(3)]
    sqn = k.ring("sqn", 2, [128, 256])
    rn_r = k.ring("rnq", 2, [128, 256])

    for b in range(NBLK):
        t0 = b * 256
        seq0 = (t0 // SEQ) * SEQ
        lo = t0 - 2
        hi = t0 + 258
        xt = xt_r.next()
        c_lo, c_hi = 0, 260
        if lo < seq0:
            c_lo = 2
            k.memset("pool", xt[:, :, 0:2], 0.0)
        if hi > seq0 + SEQ:
            c_hi = 258
            k.memset("pool", xt[:, :, 258:260], 0.0)
        k.dma_multi([(xt[:, jq * 4:(jq + 1) * 4, c_lo:c_hi],
                      xT.v(lambda h, jq=jq: h[jq * 512:(jq + 1) * 512, lo + c_lo:lo + c_hi].rearrange("(j p) t -> p j t", p=128)))
                     for jq in range(4)])
        sq = sq_r.next()
        k.act(sq[:], xt[:], AF.Square)
        pss = k.ps()
        for j in range(NJ):
            k.mm(pss[:, 0:260], ones, sq[:, j, :], start=(j == 0), stop=(j == NJ - 1))
        rstd = rstd_r.next()
        k.rsqrt(rstd[:], pss[:, 0:260], scale=1.0 / D_MODEL, bias=1e-6)
        u = u_r.next()
        k.tt("dve", u[:], xt[:], V(rstd.h[:].unsqueeze(1).to_broadcast([128, NJ, 260]), rstd.res), ALU.mult)

        pmt = []
        for t in range(10):
            pp = k.ps()
            for j in range(NJ):
                k.mm(pp[:, 0:260], WAs[:, t, j, :], u[:, j, :], start=(j == 0), stop=(j == NJ - 1))
            pa = pa_r.next()
            k.cp("act", pa[:], pp[:, 0:260])
            if t < 7:
                s = s_r.next()
                k.tt("pool", s[:], pa[:, 1:257], pa[:, 3:259], ALU.add)
                m1 = m1_r.next()
                k.act(m1[:], pa[:, 2:258], AF.Copy, scale=omm[:, t:t + 1])
                o = pm[t].next()
                k.stt("dve", o[:], s[:], hmu[:, t:t + 1], m1[:], ALU.mult, ALU.add)
                pmt.append(o)
            else:
                tc_ = t - 7
                acc = cacc.next()
                k.ts("dve", acc[:], pa[:, 0:256], cw[:, tc_ * 5:tc_ * 5 + 1], None, ALU.mult)
                for kk_ in range(1, 5):
                    k.stt("dve", acc[:], pa[:, kk_:kk_ + 256], cw[:, tc_ * 5 + kk_:tc_ * 5 + kk_ + 1], acc[:],
                          ALU.mult, ALU.add)
                o = qkv[tc_].next()
                k.act(o[:], acc[:], AF.Silu)
                pmt.append(o)
        qn = []
        for qi in range(2):
            src = pmt[7 + qi]
            s2 = sqn.next()
            k.tt("pool", s2[:], src[:], src[:], ALU.mult)
            pq = k.ps()
            k.mm(pq[:, 0:256], ones, s2[:])
            rn = rn_r.next()
            k.rsqrt(rn[:], pq[:, 0:256], scale=1.0, bias=1e-6)
            if qi == 0:
                k.stt("dve", src[:], src[:], 128.0 ** -0.5, rn[:], ALU.mult, ALU.mult)
            else:
                k.tt("dve", src[:], src[:], rn[:], ALU.mult)
        for cc in range(2):
            g = b * 2 + cc
            cs = slice(cc * 128, cc * 128 + 128)
            _a1_rwkv_chunk(k, g, cs, pmt, rowc, lw, CM, omka, rw_fm, rw_tm, rw_dv, rw_post)
            _a1_gdn_chunk(k, g, cs, cc, u, WAs, pmt, rowc, CM, nA, c10, gd_fm, gd_tm, gd_lr, gd_dv, gd_post)


_A1R = {}


def _a1_rwkv_chunk(k, g, cs, pmt, rowc, lw, CM, omka, rw_fm, rw_tm, rw_dv, rw_post):
    ident = CM(IDN)
    ones = CM(ONE)
    R = _A1R
    if not R:
        R["th"] = k.ring("a1th", 2, [128, 128])
        R["sg0"] = k.ring("a1sg0", 2, [128, 128])
        R["sg1"] = k.ring("a1sg1", 2, [32, 128])
        R["rkv"] = k.ring("a1rkv", 2, [128, 384])
        R["sw"] = k.ring("a1sw", 2, [128, 256])
        R["aa"] = k.ring("a1aa", 2, [128, 256])
        R["E"] = [k.ring("a1E%d" % i, 1, [128, 256]) for i in range(4)]
        R["kx"] = k.ring("a1kx", 2, [128, 128])
        R["kx2"] = k.ring("a1kx2", 2, [128, 128])
        R["ss"] = k.ring("a1ss", 2, [128, 2])
        R["kk"] = k.ring("a1kk", 2, [128, 128])
        R["kka"] = k.ring("a1kka", 2, [128, 256])
        R["t1"] = k.ring("a1t1", 2, [128, 256])
        R["kd"] = k.ring("a1kd", 2, [128, 256])
        R["q4"] = k.ring("a1q4", 1, [128, 4, 256])
        R["tm"] = k.ring("a1tm", 1, [128, 4, 4, 64])
        R["fm"] = k.ring("a1fm", 2, [64, 512])
        R["post"] = k.ring("a1post", 2, [128, 258])
        R["bs"] = k.ring("a1bs", 2, [128, 128])
        R["dv"] = k.ring("a1dv", 2, [64, 4])
    th = R["th"].next()
    k.act(th[:], pmt[3][:, cs], AF.Tanh)
    sg0 = R["sg0"].next()
    k.act(sg0[:], pmt[5][:, cs], AF.Sigmoid)
    sg1 = R["sg1"].next()
    k.act(sg1[:], pmt[6][0:32, cs], AF.Sigmoid)
    p_aw = k.ps()
    k.mm(p_aw[:, 0:256], th[:], lw[:, 0:256], start=True, stop=False)
    k.mm(p_aw[:, 0:256], V(ones.ap[0:1, :], ones.res), rowc[0:1, RC_W0:RC_W0 + 256], start=False, stop=True)
    p_aa = k.ps()
    k.mm(p_aa[:, 0:256], pmt[4][:, cs], lw[:, 256:512], start=True, stop=False)
    k.mm(p_aa[:, 0:256], V(ones.ap[0:1, :], ones.res), rowc[0:1, RC_A0:RC_A0 + 256], start=False, stop=True)
    p_g = k.ps()
    k.mm(p_g[:, 0:128], sg0[:], lw[:, 512:640], start=True, stop=False)
    k.mm(p_g[:, 0:128], sg1[:], lw[0:32, 640:768], start=False, stop=True)
    p_t = k.ps()
    for i in range(3):
        k.tr(p_t[:, i * 128:(i + 1) * 128], pmt[i][:, cs], ident)
    rkv = R["rkv"].next()
    k.cp("act", rkv[:], p_t[:, 0:384])
    r_tm, k_tm, v_tm = rkv[:, 0:128], rkv[:, 128:256], rkv[:, 256:384]
    post = R["post"].next()
    k.cp("pool", post[:, 0:128], v_tm)
    k.cp("act", post[:, 128:256], p_g[:, 0:128])
    sw = R["sw"].next()
    k.act(sw[:], p_aw[:, 0:256], AF.Sigmoid)
    aa = R["aa"].next()
    k.act(aa[:], p_aa[:, 0:256], AF.Sigmoid)
    pL = [k.ps(), k.ps(), k.ps()]
    for d in range(2):
        dd = DIRS[d]
        for i, key in enumerate(("Ti", "Ts", "Tr")):
            k.mm(pL[i][:, d * 128:(d + 1) * 128], CM(dd[key]), sw[:, d * 128:(d + 1) * 128])
    E = [r.next() for r in R["E"]]
    k.act(E[0][:], pL[0][:, 0:256], AF.Exp, scale=C0)
    k.act(E[1][:], pL[0][:, 0:256], AF.Exp, scale=-C0)
    k.act(E[2][:], pL[1][:, 0:256], AF.Exp, scale=C0)
    k.act(E[3][:], pL[2][:, 0:256], AF.Exp, scale=C0)
    p_dv = k.ps()
    for hd in range(4):
        k.mm(p_dv[0:64, hd:hd + 1], sw[:, hd * 64:(hd + 1) * 64], V(ones.ap[:, 0:1], ones.res))
    dv = R["dv"].next()
    k.act(dv[:], p_dv[0:64, 0:4], AF.Exp, scale=C0)
    k.dma(rw_dv[g, :, :], dv[:])
    kx = R["kx"].next()
    k.tt("dve", kx[:], k_tm, rowc[:, RC_KK:RC_KK + 128], ALU.mult)
    kx2 = R["kx2"].next()
    k.tt("pool", kx2[:], kx[:], kx[:], ALU.mult)
    ss = R["ss"].next()
    k.red("dve", ss[:], kx2.v(lambda h: h[:].rearrange("p (a b) -> p a b", a=2)))
    k.rsqrt(ss[:], ss[:], scale=1.0, bias=1e-6)
    kk = R["kk"].next()
    k.tt("dve", kk.v(lambda h: h[:].rearrange("p (a b) -> p a b", a=2)),
         kx.v(lambda h: h[:].rearrange("p (a b) -> p a b", a=2)),
         V(ss.h[:].unsqueeze(2).to_broadcast([128, 2, 64]), ss.res), ALU.mult)

    def bc2(v):
        return V(v.ap.unsqueeze(1).to_broadcast([128, 2, 128]), v.res)

    def as2(t_):
        return t_.v(lambda h: h[:].rearrange("p (a b) -> p a b", a=2))

    kka = R["kka"].next()
    k.tt("dve", as2(kka), bc2(kk[:]), as2(aa), ALU.mult)
    t1 = R["t1"].next()
    k.tt("pool", t1[:], aa[:], rowc[:, RC_KA:RC_KA + 256], ALU.mult)
    k.tt("pool", t1[:], t1[:], omka[:], ALU.add)
    kd = R["kd"].next()
    k.tt("dve", as2(kd), as2(t1), bc2(k_tm), ALU.mult)
    q4 = R["q4"].next()
    tm = R["tm"].next()

    def q4v(i):
        return q4.v(lambda h: h[:, i, :].rearrange("p (a b) -> p a b", a=2))

    def tmv(i):
        return tm.v(lambda h: h[:, :, i, :])

    def as4(t_):
        return t_.v(lambda h: h[:].rearrange("p (a b) -> p a b", a=4))

    k.stt("dve", q4v(0), bc2(kk[:]), -1.0, as2(E[2]), ALU.mult, ALU.mult)
    k.tt("pool", q4v(1), bc2(r_tm), as2(E[0]), ALU.mult)
    k.tt("dve", q4v(2), as2(kka), as2(E[1]), ALU.mult)
    k.tt("pool", q4v(3), as2(kd), as2(E[1]), ALU.mult)
    k.cp("pool", tmv(0), q4.v(lambda h: h[:, 0, :].rearrange("p (a b) -> p a b", a=4)))
    k.tt("dve", tmv(1), as4(kka), as4(E[3]), ALU.mult)
    k.tt("pool", tmv(2), as4(kd), as4(E[3]), ALU.mult)
    k.cp("pool", tm.v(lambda h: h[:, :, 3, :].rearrange("p (d h) b -> p d h b", d=2)), V(_vdup(v_tm.ap), v_tm.res))
    k.dma(rw_tm[g, :, :], tm.v(lambda h: h[:].rearrange("p a b c -> p (a b c)")))
    bs = R["bs"].next()
    k.tt("pool", bs[:], kd[:, 0:128], kd[:, 128:256], ALU.add)
    k.tt("pool", bs[:], bs[:], r_tm, ALU.mult)
    k.stt("dve", bs[:], bs[:], 0.5, rowc[:, RC_RK:RC_RK + 128], ALU.mult, ALU.mult)
    k.red("dve", post[:, 256:258], bs.v(lambda h: h[:].rearrange("p (a b) -> p a b", a=2)))
    k.dma(rw_post[g, :, :], post[:])
    for hd in range(4):
        pf = k.ps()
        for i in range(4):
            k.tr(pf[0:64, i * 128:(i + 1) * 128], q4[:, i, hd * 64:(hd + 1) * 64], ident)
        fm = R["fm"].next()
        k.cp("act" if hd % 2 == 0 else "dve", fm[:], pf[0:64, 0:512])
        k.dma(rw_fm[g, hd, :, :], fm[:])


def _vdup(ap):
    return ap.rearrange("p (h b) -> p h b", h=2).unsqueeze(1).to_broadcast([128, 2, 2, 64])


_A1G = {}


def _a1_gdn_chunk(k, g, cs, cc, u, WAs, pmt, rowc, CM, nA, c10, gd_fm, gd_tm, gd_lr, gd_dv, gd_post):
    ident = CM(IDN)
    ones = CM(ONE)
    R = _A1G
    if not R:
        R["sz"] = k.ring("g1sz", 2, [128, 128])
        R["kv"] = k.ring("g1kv", 2, [128, 256])
        R["t4"] = k.ring("g1t4", 2, [128, 4])
        R["gb"] = k.ring("g1gb", 2, [128, 6])
        R["gn2"] = k.ring("g1gn2", 2, [128, 4])
        R["bc"] = k.ring("g1bc", 2, [128, 4, 128])
        R["eg"] = k.ring("g1eg", 2, [128, 128])
        R["fm"] = k.ring("g1fm", 2, [128, 512])
        R["tm"] = k.ring("g1tm", 2, [128, 384])
        R["ec"] = k.ring("g1ec", 2, [128, 2])
        R["sc"] = k.ring("g1sc", 2, [128, 1])
        R["lr"] = k.ring("g1lr", 2, [2, 256])
        R["dv"] = k.ring("g1dv", 2, [128, 1])
    pz = k.ps()
    for j in range(NJ):
        k.mm(pz[:, 0:128], u[:, j, 2 + cc * 128:2 + cc * 128 + 128], WAs[:, 10, j, :], start=(j == 0), stop=(j == NJ - 1))
    sz = R["sz"].next()
    k.act(sz[:], pz[:, 0:128], AF.Silu)
    k.dma(gd_post[g, :, :], sz[:])
    pt = k.ps()
    k.tr(pt[:, 0:128], pmt[8][:, cs], ident)
    k.tr(pt[:, 128:256], pmt[9][:, cs], ident)
    kv = R["kv"].next()
    k.cp("act", kv[:], pt[:, 0:256])
    k_tm, v_tm = kv[:, 0:128], kv[:, 128:256]
    p4 = k.ps()
    k.tr(p4[:, 0:4], pmt[6][32:36, cs], V(ident.ap[32:36, 32:36], ident.res))
    t4 = R["t4"].next()
    k.tt("dve", t4[:, 0:2], p4[:, 0:2], rowc[:, RC_DTB:RC_DTB + 2], ALU.add)
    gb = R["gb"].next()
    k.act(t4[:, 0:2], t4[:, 0:2], AF.Exp)
    k.act(t4[:, 0:2], t4[:, 0:2], AF.Ln, bias=1.0)
    k.tt("dve", gb[:, 0:2], t4[:, 0:2], nA[:], ALU.mult)
    k.act(gb[:, 2:4], p4[:, 2:4], AF.Sigmoid)
    gn2 = R["gn2"].next()
    k.cp("pool", gn2.v(lambda h: h[:].rearrange("p (d s) -> p d s", s=2)[:, :, 0]), gb[:, 0:2])
    k.ts("dve", gn2.v(lambda h: h[:].rearrange("p (d s) -> p d s", s=2)[:, :, 1]), gb[:, 0:2], -1.0, None, ALU.mult)
    bc = R["bc"].next()
    k.cp("pool", bc[:], V(gb.h[:, 0:4].unsqueeze(2).to_broadcast([128, 4, 128]), gb.res))
    for d in range(2):
        dd = DIRS[d]
        fm = R["fm"].next()
        tm = R["tm"].next()
        pb = k.ps()
        k.mm(pb[:, 0:128], bc[:, 2 + d, :], ident)
        k.mm(pb[:, 128:256], bc[:, d, :], CM(dd["Ti"]))
        eg = R["eg"].next()
        k.act(eg[:], pb[:, 128:256], AF.Exp)
        k.cp("pool", fm[:, 0:128], pmt[8][:, cs])
        k.cp("pool", fm[:, 128:256], pmt[7][:, cs])
        k.tt("dve", fm[:, 256:384], pmt[8][:, cs], pb[:, 0:128], ALU.mult)
        k.tt("pool", fm[:, 384:512], pmt[7][:, cs], eg[:], ALU.mult)
        k.dma(gd_fm[g, d, :, :], fm[:])
        dv = R["dv"].next()
        k.cp("act", dv[:], eg[:, dd["last"]:dd["last"] + 1])
        k.dma(gd_dv[g, d, :, :], dv[:])
        pc = k.ps()
        k.mm(pc[:, 0:1], CM(dd["Ti"]), gb[:, d:d + 1])
        k.mm(pc[:, 1:2], CM(dd["Tr"]), gb[:, d:d + 1])
        ec = R["ec"].next()
        k.act(ec[:], pc[:, 0:2], AF.Exp)
        sc = R["sc"].next()
        k.tt("dve", sc[:], ec[:, 0:1], gb[:, 2 + d:3 + d], ALU.mult)
        k.ts("dve", tm[:, 0:128], v_tm, gb[:, 2 + d:3 + d], None, ALU.mult)
        k.ts("pool", tm[:, 128:256], k_tm, sc[:, 0:1], None, ALU.mult)
        k.ts("dve", tm[:, 256:384], k_tm, ec[:, 1:2], None, ALU.mult)
        k.dma(gd_tm[g, d, :, :], tm[:])
        pr = k.ps()
        k.mm(pr[0:2, 0:128], gn2[:, 2 * d:2 * d + 2], CM(dd["Ti"]))
        lr = R["lr"].next()
        k.ts("dve", lr[:, 0:128], pr[0:2, 0:128], c10[:, 0:1], c10[:, 1:2], ALU.mult, ALU.add)
        k.ts("dve", lr[:, 128:256], pr[0:2, 0:128], c10[:, 1:2], c10[:, 0:1], ALU.mult, ALU.add)
        k.dma(gd_lr[g, d, :, :], lr[:])


def _invert(k, R, A0, N0, CM, tag):
    ident = CM(IDN)
    TN = [R["TN0"].next(), R["TN1"].next()]
    Ak = [R["Ak0"].next(), R["Ak1"].next()]
    k.tt("pool", TN[0][:, 0:128], N0, ident, ALU.add)
    if DBG.get("lv", 8) == 1:
        return TN[0][:, 0:128]
    p = k.ps()
    iv = DBG.get("iv", 15)
    if iv & 1:
        k.mm(p[:, 0:128], A0, N0)
    if iv & 2:
        k.mm(p[:, 128:256], N0, A0)
    if iv & 4:
        k.cp("act", TN[0][:, 128:256], p[:, 0:128])
    if iv & 8:
        k.cp("dve", Ak[0][:], p[:, 128:256])
    cur = 0
    if DBG.get("lv", 8) == 0:
        return TN[0][:, 0:128]
    for lv in range(2, DBG.get("lv", 8)):
        a_prev = Ak[cur]
        tn_prev = TN[cur]
        tn_new = TN[1 - cur]
        p = k.ps()
        if lv < 7:
            k.mm(p[:, 0:256], a_prev[:], tn_prev[:, 0:256])
            p2 = k.ps()
            k.mm(p2[:, 0:128], tn_prev[:, 128:256], a_prev[:])
            k.tt("dve", tn_new[:, 0:128], tn_prev[:, 0:128], p[:, 0:128], ALU.add)
            k.cp("act", tn_new[:, 128:256], p[:, 128:256])
            k.cp("act", Ak[1 - cur][:], p2[:, 0:128])
        else:
            k.mm(p[:, 0:128], a_prev[:], tn_prev[:, 0:128])
            k.tt("dve", tn_new[:, 0:128], tn_prev[:, 0:128], p[:, 0:128], ALU.add)
        cur = 1 - cur
    return TN[cur][:, 0:128]


def _phase_a2(k, nc, NB, SEQ, NCH, CPS, cst, CM, rw_fm, rw_tm, rw_dv, rw_y, gd_fm, gd_tm, gd_lr, gd_dv, gd_o):
    ident = CM(IDN)
    NRW = NB * 4
    NGD = NB * 2
    Zrw = [[k.sb("zrw%d_%d" % (s, i), [64, 64]) for i in range(2)] for s in range(NRW)]
    Zgd = [[k.sb("zgd%d_%d" % (s, i), [128, 128]) for i in range(2)] for s in range(NGD)]
    for s in range(NRW):
        k.memset("pool", Zrw[s][0][:], 0.0)
    for s in range(NGD):
        k.memset("pool", Zgd[s][0][:], 0.0)
    NR = 3
    R = dict(
        TN0=k.ring("TN0", NR, [128, 256]), TN1=k.ring("TN1", NR, [128, 256]),
        Ak0=k.ring("Ak0", NR, [128, 128]), Ak1=k.ring("Ak1", NR, [128, 128]),
        fm=k.ring("s_fm", NR, [128, 512]), tm=k.ring("s_tm", NR, [128, 384]),
        rtm=k.ring("s_rtm", NR, [128, 256]), rfm=k.ring("s_rfm", NR, [64, 512]),
        dv=k.ring("s_dv", NR, [128, 4]), lr=k.ring("s_lr", NR, [2, 256]),
        A0=k.ring("s_A0", NR, [128, 128]), NB_=k.ring("s_NB", NR, [128, 256]), CK=k.ring("s_CK", NR, [128, 256]),
        D=k.ring("s_D", NR, [128, 384]), m=k.ring("s_m", NR, [128, 256]),
        akv=k.ring("s_akv", NR, [128, 128]), U=k.ring("s_U", NR, [128, 128]), WT=k.ring("s_WT", NR, [128, 128]),
        X=k.ring("s_X", NR, [128, 128]), O=k.ring("s_O", NR, [128, 128]),
    )
    for step in range(CPS):
        for bi in range(NB):
            for d in (range(2) if DBG.get("rw", 1) else []):
                dd = DIRS[d]
                ci = step if d == 0 else CPS - 1 - step
                g = bi * CPS + ci
                rdv = R["dv"].next()
                k.dma(rdv[0:64, 0:4], rw_dv[g, :, :])
                for h in range(2):
                    hd = d * 2 + h
                    s = bi * 4 + hd
                    par = step % 2
                    Zo, Zn = Zrw[s][par], Zrw[s][1 - par]
                    fm = R["rfm"].next()
                    k.dma(fm[:], rw_fm[g, hd, :, :])
                    tm = R["rtm"].next()
                    k.dma(tm[:], rw_tm.v(lambda hh: hh[g, :, hd * 256:(hd + 1) * 256]))
                    aT, rT, bT, kT = fm[:, 0:128], fm[:, 128:256], fm[:, 256:384], fm[:, 384:512]
                    A_tm, Bh, Kh, Vt = tm[:, 0:64], tm[:, 64:128], tm[:, 128:192], tm[:, 192:256]
                    pA = k.ps()
                    k.mm(pA[:, 0:128], aT, bT)
                    pB = k.ps()
                    k.mm(pB[:, 0:256], bT, fm[:, 0:256])
                    pC = k.ps()
                    k.mm(pC[:, 0:256], kT, fm[:, 0:256])
                    A0 = R["A0"].next()
                    k.tt("dve", A0[:], pA[:, 0:128], CM(dd["Ms"]), ALU.mult)
                    msk = V(cst.h[:, dd["MTs"]:dd["MTs"] + 2, :].rearrange("p a b -> p (a b)"), cst.res)
                    NBt = R["NB_"].next()
                    k.tt("dve", NBt[:], pB[:, 0:256], msk, ALU.mult)
                    CK = R["CK"].next()
                    k.tt("dve", CK[:], pC[:, 0:256], msk, ALU.mult)
                    TT = _invert(k, R, A0[:], NBt[:, 0:128], CM, "rw")
                    p1 = k.ps()
                    k.mm(p1[:, 0:64], CK[:, 0:128], Vt)
                    akv = R["akv"].next()
                    k.cp("act", akv[:, 0:64], p1[:, 0:64])
                    p2 = k.ps()
                    k.mm(p2[:, 0:64], TT, akv[:, 0:64])
                    k.mm(p2[0:64, 128:256], A_tm, TT)
                    U = R["U"].next()
                    k.cp("act", U[:, 0:64], p2[:, 0:64])
                    WT = R["WT"].next()
                    k.cp("dve", WT[0:64, :], p2[0:64, 128:256])
                    pX = k.ps()
                    k.mm(pX[:, 0:64], WT[0:64, :], Zo[:])
                    X = R["X"].next()
                    k.tt("dve", X[:, 0:64], pX[:, 0:64], U[:, 0:64], ALU.add)
                    pZ = k.ps()
                    k.mm(pZ[0:64, 0:64], Kh, Vt, start=True, stop=False)
                    k.mm(pZ[0:64, 0:64], Bh, X[:, 0:64], start=False, stop=True)
                    pO = k.ps()
                    k.mm(pO[:, 0:64], rT, Zo[:], start=True, stop=False)
                    k.mm(pO[:, 0:64], NBt[:, 128:256], X[:, 0:64], start=False, stop=False)
                    k.mm(pO[:, 0:64], CK[:, 128:256], Vt, start=False, stop=True)
                    k.stt("dve", Zn[:], Zo[:], rdv[0:64, hd:hd + 1], pZ[0:64, 0:64], ALU.mult, ALU.add)
                    O = R["O"].next()
                    k.cp("act", O[:, 0:64], pO[:, 0:64])
                    k.dma(rw_y.v(lambda hh: hh[g, :, hd * 64:(hd + 1) * 64]), O[:, 0:64])
            for d in (range(2) if DBG.get("gd", 1) else []):
                dd = DIRS[d]
                ci = step if d == 0 else CPS - 1 - step
                g = bi * CPS + ci
                s = bi * 2 + d
                par = step % 2
                Zo, Zn = Zgd[s][par], Zgd[s][1 - par]
                fm = R["fm"].next()
                k.dma(fm[:], gd_fm[g, d, :, :])
                tm = R["tm"].next()
                k.dma(tm[:], gd_tm[g, d, :, :])
                lr = R["lr"].next()
                k.dma(lr[:], gd_lr[g, d, :, :])
                gdv = R["dv"].next()
                k.dma(gdv[:, 0:1], gd_dv[g, d, :, :])
                if DBG.get("cut", 9) <= 0:
                    continue
                kT, qT, kbT, qgT = fm[:, 0:128], fm[:, 128:256], fm[:, 256:384], fm[:, 384:512]
                vb, kbg, kg = tm[:, 0:128], tm[:, 128:256], tm[:, 256:384]
                pt = k.ps()
                k.mm(pt[:, 0:128], lr[:, 0:128], lr[:, 128:256])
                k.mm(pt[:, 128:256], lr[:, 128:256], lr[:, 0:128])
                m = R["m"].next()
                k.stt("dve", m[:, 0:128], pt[:, 0:128], 0.0, CM(dd["Ns"]), ALU.min, ALU.add)
                k.stt("dve", m[:, 128:256], pt[:, 128:256], 0.0, CM(dd["NTs"]), ALU.min, ALU.add)
                D = R["D"].next()
                k.act(D[:, 0:256], m[:, 0:256], AF.Exp)
                k.tt("pool", D[:, 256:384], D[:, 128:256], ident, ALU.add)
                pA = k.ps()
                k.mm(pA[:, 0:128], kbT, kT)
                pB = k.ps()
                k.mm(pB[:, 0:256], kT, fm[:, 128:384])
                A0 = R["A0"].next()
                k.stt("dve", A0[:], pA[:, 0:128], -1.0, D[:, 0:128], ALU.mult, ALU.mult)
                NBt = R["NB_"].next()
                k.stt("dve", NBt[:, 0:128], pB[:, 128:256], -1.0, D[:, 128:256], ALU.mult, ALU.mult)
                k.tt("dve", NBt[:, 128:256], pB[:, 0:128], D[:, 256:384], ALU.mult)
                if DBG.get("cut", 9) <= 1:
                    continue
                TT = _invert(k, R, A0[:], NBt[:, 0:128], CM, "gd") if DBG.get("cut", 9) > 2 else NBt[:, 0:128]
                if DBG.get("cut", 9) <= 3:
                    continue
                p2 = k.ps()
                k.mm(p2[:, 0:128], TT, vb)
                k.mm(p2[:, 128:256], kbg, TT)
                U = R["U"].next()
                k.cp("act", U[:], p2[:, 0:128])
                WT = R["WT"].next()
                k.ts("dve", WT[:], p2[:, 128:256], -1.0, None, ALU.mult)
                pX = k.ps()
                k.mm(pX[:, 0:128], WT[:], Zo[:])
                X = R["X"].next()
                k.tt("dve", X[:], pX[:, 0:128], U[:], ALU.add)
                pZ = k.ps()
                k.mm(pZ[:, 0:128], kg, X[:])
                pO = k.ps()
                k.mm(pO[:, 0:128], qgT, Zo[:], start=True, stop=False)
                k.mm(pO[:, 0:128], NBt[:, 128:256], X[:], start=False, stop=True)
                k.stt("dve", Zn[:], Zo[:], gdv[:, 0:1], pZ[:, 0:128], ALU.mult, ALU.add)
                O = R["O"].next()
                k.cp("act", O[:], pO[:, 0:128])
                k.dma(gd_o[g, d, :, :], O[:])


def _phase_a3(k, nc, NCH, rowc, CM, rw_post, rw_y, gd_post, gd_o, yT, ydt=F32):
    ident = CM(IDN)
    yt_r = k.ring("a3y", 2, [128, 256])
    po_r = k.ring("a3po", 2, [128, 258])
    y_r = k.ring("a3ys", 2, [128, 128])
    c_r = k.ring("a3c", 2, [128, 128])
    st_r = k.ring("a3st", 2, [128, 4])
    go_r = k.ring("a3go", 2, [128, 256])
    sz_r = k.ring("a3sz", 2, [128, 128])
    o_r = k.ring("a3o", 2, [128, 128])
    yo_r = k.ring("a3yo", 2, [128, 256], ydt)

    def h2(v):
        return V(v.ap.rearrange("p (a b) -> p a b", a=2), v.res)

    def bch(v):
        return V(v.ap.unsqueeze(2).to_broadcast([128, 2, 64]), v.res)

    for g in range(NCH):
        yt = yt_r.next()
        k.dma(yt[:], rw_y[g, :, :])
        po = po_r.next()
        k.dma(po[:], rw_post[g, :, :])
        y = y_r.next()
        k.tt("pool", y[:], yt[:, 0:128], yt[:, 128:256], ALU.add)
        st = st_r.next()
        k.red("dve", st[:, 0:2], h2(y[:]))
        k.ts("dve", st[:, 0:2], st[:, 0:2], 1.0 / 64, None, ALU.mult)
        c = c_r.next()
        k.tt("dve", h2(c[:]), h2(y[:]), bch(st[:, 0:2]), ALU.subtract)
        k.tt("pool", y[:], c[:], c[:], ALU.mult)
        k.red("dve", st[:, 2:4], h2(y[:]))
        k.rsqrt(st[:, 2:4], st[:, 2:4], scale=1.0 / 64, bias=64e-5)
        k.tt("dve", h2(c[:]), h2(c[:]), bch(st[:, 2:4]), ALU.mult)
        k.tt("pool", c[:], c[:], rowc[:, RC_GNW:RC_GNW + 128], ALU.mult)
        k.tt("pool", c[:], c[:], rowc[:, RC_GNB:RC_GNB + 128], ALU.add)
        k.tt("dve", h2(y[:]), h2(po[:, 0:128]), bch(po[:, 256:258]), ALU.mult)
        k.tt("pool", c[:], c[:], y[:], ALU.add)
        k.tt("dve", c[:], c[:], po[:, 128:256], ALU.mult)
        go = go_r.next()
        k.dma(go.v(lambda h: h[:].rearrange("p (d c) -> p d c", d=2)),
              gd_o.v(lambda h: h[g, :, :, :].rearrange("d p c -> p d c")))
        sz = sz_r.next()
        k.dma(sz[:], gd_post[g, :, :])
        o = o_r.next()
        k.tt("pool", o[:], go[:, 0:128], go[:, 128:256], ALU.add)
        o2 = o_r.next()
        k.tt("pool", o2[:], o[:], o[:], ALU.mult)
        st2 = st_r.next()
        k.red("dve", st2[:, 0:1], o2[:])
        k.rsqrt(st2[:, 0:1], st2[:, 0:1], scale=1.0 / 128, bias=1e-6)
        k.ts("dve", o[:], o[:], st2[:, 0:1], None, ALU.mult)
        k.tt("pool", o[:], o[:], rowc[:, RC_GNORM:RC_GNORM + 128], ALU.mult)
        k.tt("dve", o[:], o[:], sz[:], ALU.mult)
        p = k.ps()
        k.tr(p[:, 0:128], c[:], ident)
        k.tr(p[:, 128:256], o[:], ident)
        yo = yo_r.next()
        k.cp("act", yo[:], p[:, 0:256])
        k.dma_multi([(yT[0:128, g * 128:(g + 1) * 128], yo[:, 0:128]),
                     (yT[128:256, g * 128:(g + 1) * 128], yo[:, 128:256])])


def phase_a_inputs(c, inp, consts):
    W = inp["w_in"][0]
    rc = np.arange(128 * c, 128 * c + 128)
    qh = c // 2
    cols = [rc, 1024 + rc, 2048 + rc, np.arange(3072, 3200), np.arange(3200, 3328), np.arange(3328, 3456),
            np.concatenate([np.arange(3456, 3488), G0 + 3072 + np.array([c, 8 + c, 16 + c, 24 + c])]),
            G0 + qh * 128 + np.arange(128), G0 + 512 + qh * 128 + np.arange(128),
            G0 + 1024 + c * 128 + np.arange(128), G0 + 2048 + c * 128 + np.arange(128)]
    WA = np.zeros((11, 128, NJ, 128), np.float32)
    for t, cl in enumerate(cols):
        WA[t, :, :, :len(cl)] = W[:, cl].reshape(NJ, 128, len(cl)).transpose(1, 0, 2)
    prm = np.zeros((128, NPRM), np.float32)
    prm[:, PR_GAIN:PR_GAIN + 16] = inp["norm_pre_mix"][0].reshape(NJ, 128).T
    mu = inp["rw_shift_mu"][0]
    for t in range(7):
        cl = cols[t]
        n = min(len(cl), 128)
        if t == 6:
            prm[:32, PR_MU + t] = mu[cl[:32]]
        else:
            prm[:n, PR_MU + t] = mu[cl]
    cwh = inp["gdn_conv_w"][0]
    for t, base in enumerate([qh * 128, 512 + qh * 128, 1024 + c * 128]):
        prm[:, PR_CW + t * 5:PR_CW + t * 5 + 5] = cwh[:, base:base + 128].T
    rowc = np.zeros((128, NROWC), np.float32)

    def row(v):
        return np.broadcast_to(np.asarray(v, np.float32)[None, :], (128, len(v)))

    rowc[:, RC_W0:RC_W0 + 256] = row(np.concatenate([inp["rw_w0_f"][0][rc], inp["rw_w0_b"][0][rc]]))
    rowc[:, RC_A0:RC_A0 + 256] = row(np.concatenate([inp["rw_a0_f"][0][rc], inp["rw_a0_b"][0][rc]]))
    rowc[:, RC_KK:RC_KK + 128] = row(inp["rw_k_k"][0][rc])
    rowc[:, RC_KA:RC_KA + 256] = row(np.concatenate([inp["rw_k_a"][0][rc]] * 2))
    rowc[:, RC_RK:RC_RK + 128] = row(inp["rw_r_k"][0].reshape(-1)[rc])
    rowc[:, RC_GNW:RC_GNW + 128] = row(inp["rw_gn_w"][0][rc])
    rowc[:, RC_GNB:RC_GNB + 128] = row(inp["rw_gn_b"][0][rc])
    rowc[:, RC_DTB:RC_DTB + 2] = row(np.array([inp["gdn_dt_bias_f"][0][c], inp["gdn_dt_bias_b"][0][c]]))
    rowc[:, RC_ALOG:RC_ALOG + 2] = row(np.array([inp["gdn_a_log_f"][0][c], inp["gdn_a_log_b"][0][c]]))
    rowc[:, RC_GNORM:RC_GNORM + 128] = row(inp["gdn_norm_w"][0])
    lw = np.zeros((128, 768), np.float32)
    lw[0:64, 0:128] = inp["rw_w2_f"][0][:, rc]
    lw[64:128, 128:256] = inp["rw_w2_b"][0][:, rc]
    lw[0:64, 256:384] = inp["rw_a2_f"][0][:, rc]
    lw[64:128, 384:512] = inp["rw_a2_b"][0][:, rc]
    lw[:, 512:640] = inp["rw_g2"][0][0:128, rc]
    lw[0:32, 640:768] = inp["rw_g2"][0][128:160, rc]
    return dict(WA=WA.reshape(11, 128, NJ * 128), prm=prm, rowc=rowc, lw=lw, cst=consts.reshape(128, NCST * 128))


def build_phase_b(TB):
    N = min(512, TB)
    NBK = TB // N
    nc = bass.Bass("TRN2", target_bir_lowering=False)
    k = KB(nc)
    with k.st:
        xTs = k.dram("xTs", [D_MODEL, TB], kind="ExternalInput")
        yTs = k.dram("yTs", [D_MODEL, TB], kind="ExternalInput")
        WG = k.dram("WG", [32, 128, 2048], kind="ExternalInput")
        WP = k.dram("WP", [16, 128, 2048], kind="ExternalInput")
        WO = k.dram("WO", [16, 128, 2048], kind="ExternalInput")
        WFG = k.dram("WFG", [44, 128, 2048], kind="ExternalInput")
        WFU = k.dram("WFU", [44, 128, 2048], kind="ExternalInput")
        WD = k.dram("WD", [16, 128, 44 * 128], kind="ExternalInput")
        gn_d = k.dram("gn", [128, 64], kind="ExternalInput")
        cst_d = k.dram("cst", [128, NCST * 128], kind="ExternalInput")
        outT = k.dram("outT", [D_MODEL, TB], kind="ExternalOutput")
        _phase_b(k, nc, TB, N, NBK, xTs, yTs, WG, WP, WO, WFG, WFU, WD, gn_d, cst_d, outT)
        k.S.emit(final_waits=[outT.res])
    return nc


def _phase_b(k, nc, TB, N, NBK, xTs, yTs, WG, WP, WO, WFG, WFU, WD, gn_d, cst_d, outT, yg=None):
    if k.psr:
        psn = k.psr[7]
        k.psr = k.psr[0:7]
    else:
        k.psr = [T(k.st.enter_context(nc.psum_tensor("psb%d" % i, [128, 512], F32)), "psb%d" % i, True) for i in range(7)]
        psn = T(k.st.enter_context(nc.psum_tensor("psn", [128, 512], F32)), "psn", True)
    k.psi = 0
    ones = k.sb("b_ones", [128, 128])
    k.dma(ones[:], cst_d[:, ONE * 128:(ONE + 1) * 128])
    gn = k.sb("b_gn", [128, 64])
    k.dma(gn[:], gn_d[:, :])
    xt = k.sb("b_xt", [128, NJ, N])
    u = k.sb("b_u", [128, NJ, N], BF16)
    ybf = k.sb("b_ybf", [128, NJ, N], BF16)
    mg = k.sb("b_mg", [128, NJ, N], BF16)
    o = k.sb("b_o", [128, NJ, N])
    f = k.sb("b_f", [128, 44, N], BF16)
    ystg = k.ring("b_ystg", 2, [128, N])
    sqt = k.ring("b_sqt", 2, [128, N])
    rstd = k.sb("b_rstd", [128, N])
    sg = k.ring("b_sg", 4, [128, N])
    mt = k.ring("b_mt", 2, [128, N])
    wstg = k.ring("b_wstg", 2, [128, NJ, 128])
    wb = k.ring("b_wb", 3, [128, NJ, 128], BF16)
    cnt = [0]

    def unit(src_v, nj, gain_col=None):
        s = wstg.next()
        k.dma(s.v(lambda h: h[:, 0:nj, :].rearrange("p j c -> p (j c)")), src_v)
        w = wb.next()
        eng = "pool" if cnt[0] % 2 == 0 else "dve"
        cnt[0] += 1
        if gain_col is None:
            k.cp(eng, w[:, 0:nj, :], s[:, 0:nj, :])
        else:
            gcol = gn[:, gain_col:gain_col + nj]
            k.tt(eng, w[:, 0:nj, :], s[:, 0:nj, :],
                 V(gcol.ap.unsqueeze(2).to_broadcast([128, nj, 128]), gcol.res), ALU.mult)
        return w

    def bc_j(t_):
        return V(t_.h[:].unsqueeze(1).to_broadcast([128, NJ, N]), t_.res)

    def norm_stats(src_fn, nrow):
        for r in range(nrow):
            s2 = sqt.next()
            k.act(s2[:], src_fn(r), AF.Square)
            k.mm(psn[:, 0:N], ones[:], s2[:], start=(r == 0), stop=(r == nrow - 1))
        k.rsqrt(rstd[:], psn[:, 0:N], scale=1.0 / D_MODEL, bias=1e-6)

    for bk in range(NBK):
        ts_ = slice(bk * N, (bk + 1) * N)
        k.dma_multi([(xt[:, jq * 4:(jq + 1) * 4, :],
                      xTs.v(lambda h, jq=jq: h[jq * 512:(jq + 1) * 512, ts_].rearrange("(j p) t -> p j t", p=128)))
                     for jq in range(4)])
        norm_stats(lambda r: xt[:, r, :], NJ)
        k.tt("dve", u[:], xt[:], bc_j(rstd), ALU.mult)
        if yg is None:
            for j in range(NJ):
                ys = ystg.next()
                k.dma(ys[:], yTs[j * 128:(j + 1) * 128, ts_])
                k.cp("pool", ybf[:, j, :], ys[:])
        else:
            def ld(e, bk=bk):
                if "pid" not in PIDC:
                    PIDC["pid"] = e.partition_id()
                pid = PIDC["pid"]
                off = e.snap(pid * (TB // N) + bk)
                ygv = yg.h.rearrange("(r h p) t -> h p r t", h=2, p=128)
                return [e.dma_start(out=ybf.h[:, hh * 8:(hh + 1) * 8, :], in_=ygv[hh, :, :, bass.ts(off, N)])
                        for hh in range(2)]
            k.S.op("sp", ld, reads=[yg.res], writes=[ybf.res], dma=True, ndma=2)
        for r in range(NJ):
            sgs = []
            for gi in range(2):
                w = unit(WG[gi * 16 + r, :, :], NJ, gain_col=0)
                p = k.ps()
                for j in range(NJ):
                    k.mm(p[:, 0:N], w[:, j, :], u[:, j, :], start=(j == 0), stop=(j == NJ - 1))
                s_ = sg.next()
                k.act(s_[:], p[:, 0:N], AF.Sigmoid)
                sgs.append(s_)
            w = unit(WP[r, :, :], NJ)
            pa = k.ps()
            pb = k.ps()
            for j in range(8):
                k.mm(pa[:, 0:N], w[:, j, :], ybf[:, j, :], start=(j == 0), stop=(j == 7))
            for j in range(8, 16):
                k.mm(pb[:, 0:N], w[:, j, :], ybf[:, j, :], start=(j == 8), stop=(j == 15))
            m1 = mt.next()
            k.tt("dve", m1[:], pa[:, 0:N], sgs[0][:], ALU.mult)
            m2 = mt.next()
            k.tt("dve", m2[:], pb[:, 0:N], sgs[1][:], ALU.mult)
            k.tt("pool", mg[:, r, :], m1[:], m2[:], ALU.add)
        for r in range(NJ):
            w = unit(WO[r, :, :], NJ)
            p = k.ps()
            for j in range(NJ):
                k.mm(p[:, 0:N], w[:, j, :], mg[:, j, :], start=(j == 0), stop=(j == NJ - 1))
            k.cp("act", o[:, r, :], p[:, 0:N])
        norm_stats(lambda r: o[:, r, :], NJ)
        for r in range(NJ):
            m1 = mt.next()
            k.tt("dve", m1[:], o[:, r, :], rstd[:], ALU.mult)
            k.stt("dve", xt[:, r, :], m1[:], gn[:, 16 + r:17 + r], xt[:, r, :], ALU.mult, ALU.add)
        norm_stats(lambda r: xt[:, r, :], NJ)
        k.tt("dve", u[:], xt[:], bc_j(rstd), ALU.mult)
        for r in range(44):
            w = unit(WFG[r, :, :], NJ, gain_col=32)
            pg = k.ps()
            for j in range(NJ):
                k.mm(pg[:, 0:N], w[:, j, :], u[:, j, :], start=(j == 0), stop=(j == NJ - 1))
            w2 = unit(WFU[r, :, :], NJ, gain_col=32)
            pu = k.ps()
            for j in range(NJ):
                k.mm(pu[:, 0:N], w2[:, j, :], u[:, j, :], start=(j == 0), stop=(j == NJ - 1))
            s_ = sg.next()
            k.act(s_[:], pg[:, 0:N], AF.Silu)
            k.tt("dve", f[:, r, :], s_[:], pu[:, 0:N], ALU.mult)
        for r in range(NJ):
            p = k.ps()
            j0 = 0
            for piece in (16, 16, 12):
                w = unit(WD.v(lambda h: h[r, :, j0 * 128:(j0 + piece) * 128]), piece)
                for jj in range(piece):
                    j = j0 + jj
                    k.mm(p[:, 0:N], w[:, jj, :], f[:, j, :], start=(j == 0), stop=(j == 43))
                j0 += piece
            k.cp("act", o[:, r, :], p[:, 0:N])
        norm_stats(lambda r: o[:, r, :], NJ)
        for r in range(NJ):
            m1 = mt.next()
            k.tt("dve", m1[:], o[:, r, :], rstd[:], ALU.mult)
            k.stt("dve", xt[:, r, :], m1[:], gn[:, 48 + r:49 + r], xt[:, r, :], ALU.mult, ALU.add)
        k.dma_multi([(outT.v(lambda h, jq=jq: h[jq * 512:(jq + 1) * 512, ts_].rearrange("(j p) t -> p j t", p=128)),
                      xt[:, jq * 4:(jq + 1) * 4, :]) for jq in range(4)])


def _tiles(W, ktiles):
    K_, R_ = W.shape
    return np.ascontiguousarray(W.reshape(ktiles, 128, R_ // 128, 128).transpose(2, 1, 0, 3)).reshape(R_ // 128, 128, ktiles * 128)


def phase_b_weights(inp, consts):
    W = inp["w_in"][0]
    d = {}
    d["WG"] = _tiles(W[:, GATE0:GATE0 + 4096], 16)
    wp = np.concatenate([inp["w_branch_rw"][0], inp["w_branch_gdn"][0]], axis=0)
    d["WP"] = _tiles(wp, 16)
    d["WO"] = _tiles(inp["w_out"][0], 16)
    d["WFG"] = _tiles(inp["w_ffn_gate"][0], 16)
    d["WFU"] = _tiles(inp["w_ffn_up"][0], 16)
    d["WD"] = _tiles(inp["w_ffn_down"][0], 44)
    gn = np.zeros((128, 64), np.float32)
    for i, nm in enumerate(["norm_pre_mix", "norm_post_mix", "norm_pre_ffn", "norm_post_ffn"]):
        gn[:, i * 16:(i + 1) * 16] = inp[nm][0].reshape(NJ, 128).T
    d["gn"] = gn
    d["cst"] = consts.reshape(128, NCST * 128)
    return d


def build_fused(NB, SEQ, n=8):
    NTOK = NB * SEQ
    NCH = NTOK // 128
    NBLK = NTOK // 256
    CPS = SEQ // 128
    TB = NTOK // n
    N = min(512, TB)
    NBK = TB // N
    nc = bass.Bass("TRN2", target_bir_lowering=False)
    PIDC.clear()
    k = KB(nc)
    rg = [list(range(n))]
    with k.st:
        xT = k.dram("xT", [D_MODEL, NTOK], kind="ExternalInput")
        WA = k.dram("WA", [11, 128, NJ * 128], kind="ExternalInput")
        prm_d = k.dram("prm", [128, NPRM], kind="ExternalInput")
        rowc_d = k.dram("rowc", [128, NROWC], kind="ExternalInput")
        lw_d = k.dram("lw", [128, 768], kind="ExternalInput")
        cst_d = k.dram("cst", [128, NCST * 128], kind="ExternalInput")
        xTs = k.dram("xTs", [D_MODEL, TB], kind="ExternalInput")
        WG = k.dram("WG", [32, 128, 2048], kind="ExternalInput")
        WP = k.dram("WP", [16, 128, 2048], kind="ExternalInput")
        WO = k.dram("WO", [16, 128, 2048], kind="ExternalInput")
        WFG = k.dram("WFG", [44, 128, 2048], kind="ExternalInput")
        WFU = k.dram("WFU", [44, 128, 2048], kind="ExternalInput")
        WD = k.dram("WD", [16, 128, 44 * 128], kind="ExternalInput")
        gn_d = k.dram("gn", [128, 64], kind="ExternalInput")
        outT = k.dram("outT", [D_MODEL, TB], kind="ExternalOutput")
        yT = k.dram("yT_loc", [256, NTOK], BF16)
        yg = T(nc.dram_tensor("yg", [n * 256, NTOK], BF16, addr_space="Shared").ap(), "yg")
        fin = k.dram("fence_in", [1, 64])
        fout = k.dram("fence_out", [1, 64])
        rw_fm = k.dram("rw_fm", [NCH, 4, 64, 512])
        rw_tm = k.dram("rw_tm", [NCH, 128, 4 * 256])
        rw_dv = k.dram("rw_dv", [NCH, 64, 4])
        rw_post = k.dram("rw_post", [NCH, 128, 258])
        rw_y = k.dram("rw_y", [NCH, 128, 256])
        gd_fm = k.dram("gd_fm", [NCH, 2, 128, 512])
        gd_tm = k.dram("gd_tm", [NCH, 2, 128, 384])
        gd_lr = k.dram("gd_lr", [NCH, 2, 2, 256])
        gd_dv = k.dram("gd_dv", [NCH, 2, 128, 1])
        gd_post = k.dram("gd_post", [NCH, 128, 128])
        gd_o = k.dram("gd_o", [NCH, 2, 128, 128])

        k.init_psum(8)
        with k.phase():
            cst = k.sb("cst", [128, NCST, 128])
            k.dma(cst.v(lambda h: h[:].rearrange("p a b -> p (a b)")), cst_d[:, :])
            prm = k.sb("prm", [128, NPRM])
            k.dma(prm[:], prm_d[:, :])
            rowc = k.sb("rowc", [128, NROWC])
            k.dma(rowc[:], rowc_d[:, :])
            lw = k.sb("lw", [128, 768])
            k.dma(lw[:], lw_d[:, :])

            def CM(i):
                return cst[:, i, :]

            k.dma(fin[:, :], cst[0:1, ONE, 0:64])
            with k.phase():
                _A1R.clear()
                _A1G.clear()
                _phase_a1(k, nc, NB, SEQ, NTOK, NCH, NBLK, xT, WA, prm, rowc, lw, cst, CM,
                          rw_fm, rw_tm, rw_dv, rw_post, gd_fm, gd_tm, gd_lr, gd_dv, gd_post)
            with k.phase():
                _phase_a2(k, nc, NB, SEQ, NCH, CPS, cst, CM, rw_fm, rw_tm, rw_dv, rw_y, gd_fm, gd_tm, gd_lr, gd_dv, gd_o)
            with k.phase():
                _phase_a3(k, nc, NCH, rowc, CM, rw_post, rw_y, gd_post, gd_o, yT, ydt=BF16)
        yin, yout = yT.h.opt(), yg.h.opt()
        k.S.op("pool", lambda e: e.collective_compute("AllGather", ALU.bypass, replica_groups=rg, ins=[yin], outs=[yout]),
               reads=[yT.res], writes=[yg.res], cc=True)
        fi, fo = fin.h.opt(), fout.h.opt()
        k.S.op("pool", lambda e: e.collective_compute("AllReduce", ALU.add, replica_groups=rg, ins=[fi], outs=[fo]),
               reads=[fin.res, yg.res], writes=[fout.res, yg.res], cc=True)
        k.S.barrier()
        _phase_b(k, nc, TB, N, NBK, xTs, None, WG, WP, WO, WFG, WFU, WD, gn_d, cst_d, outT, yg=yg)
        k.S.emit(final_waits=[outT.res])
    return nc


def kernel_unfused(**inp):
    return _kernel_impl(False, **inp)


def kernel(**inp):
    return _kernel_impl(False, **inp)


def _kernel_impl(fused, **inp):
    inp = {kk: np.asarray(v) for kk, v in inp.items()}
    x = inp["x"]
    NB, SEQ, _ = x.shape
    NTOK = NB * SEQ
    n = 8
    TB = NTOK // n
    consts = make_consts()
    xT = np.ascontiguousarray(x.reshape(NTOK, D_MODEL).T)
    wts = phase_b_weights(inp, consts)
    if fused:
        nc = build_fused(NB, SEQ, n)
        ims = []
        for c in range(n):
            d = phase_a_inputs(c, inp, consts)
            d.update(wts)
            d["xT"] = xT
            d["xTs"] = np.ascontiguousarray(xT[:, c * TB:(c + 1) * TB])
            ims.append(d)
        rb = run_bass_kernel_spmd(nc, ims, core_ids=list(range(n)))
    else:
        nca = build_phase_a(NB, SEQ)
        in_a = []
        for c in range(n):
            d = phase_a_inputs(c, inp, consts)
            d["xT"] = xT
            in_a.append(d)
        ra = run_bass_kernel_spmd(nca, in_a, core_ids=list(range(n)))
        yT = np.zeros((D_MODEL, NTOK), np.float32)
        for c in range(n):
            y = np.asarray(ra.results[c]["yT"])
            yT[128 * c:128 * c + 128] = y[0:128]
            yT[1024 + 128 * c:1024 + 128 * c + 128] = y[128:256]
        ncb = build_phase_b(TB)
        in_b = []
        for c in range(n):
            d = dict(wts)
            d["xTs"] = np.ascontiguousarray(xT[:, c * TB:(c + 1) * TB])
            d["yTs"] = np.ascontiguousarray(yT[:, c * TB:(c + 1) * TB])
            in_b.append(d)
        rb = run_bass_kernel_spmd(ncb, in_b, core_ids=list(range(n)))
    out = np.zeros((NTOK, D_MODEL), np.float32)
    for c in range(n):
        out[c * TB:(c + 1) * TB] = np.asarray(rb.results[c]["outT"]).T
    return out.reshape(NB, SEQ, D_MODEL)


def _invert_g(k, R, B, A0, N0, CM):
    ident = CM(IDN)
    TN = [R["TN0"], R["TN1"]]
    Ak = [R["Ak0"], R["Ak1"]]
    k.tt("pool", TN[0][:, 0:128], N0, ident, ALU.add)
    k.mm(B[:, 0:128], A0, N0)
    k.mm(B[:, 128:256], N0, A0)
    yield
    k.cp("act", TN[0][:, 128:256], B[:, 0:128])
    k.cp("dve", Ak[0][:], B[:, 128:256])
    yield
    cur = 0
    for lv in range(2, 8):
        a_prev, tn_prev, tn_new = Ak[cur], TN[cur], TN[1 - cur]
        if lv < 7:
            k.mm(B[:, 0:256], a_prev[:], tn_prev[:, 0:256])
            k.mm(B[:, 256:384], tn_prev[:, 128:256], a_prev[:])
            yield
            k.tt("dve", tn_new[:, 0:128], tn_prev[:, 0:128], B[:, 0:128], ALU.add)
            k.cp("act", tn_new[:, 128:256], B[:, 128:256])
            k.cp("act" if lv % 2 else "dve", Ak[1 - cur][:], B[:, 256:384])
            yield
        else:
            k.mm(B[:, 0:128], a_prev[:], tn_prev[:, 0:128])
            yield
            k.tt("dve", tn_new[:, 0:128], tn_prev[:, 0:128], B[:, 0:128], ALU.add)
            yield
        cur = 1 - cur
    return TN[cur][:, 0:128]


def _rw_step_g(k, R, B, cst, CM, dd, g, hd, Zo, Zn, rw_fm, rw_tm, rw_dv, rw_y):
    fm = R["rfm"].next()
    k.dma(fm[:], rw_fm[g, hd, :, :])
    tm = R["rtm"].next()
    k.dma(tm[:], rw_tm.v(lambda hh: hh[g, :, hd * 256:(hd + 1) * 256]))
    rdv = R["dv"].next()
    k.dma(rdv[0:64, 0:4], rw_dv[g, :, :])
    aT, rT, bT, kT = fm[:, 0:128], fm[:, 128:256], fm[:, 256:384], fm[:, 384:512]
    A_tm, Bh, Kh, Vt = tm[:, 0:64], tm[:, 64:128], tm[:, 128:192], tm[:, 192:256]
    msk = V(cst.h[:, dd["MTs"]:dd["MTs"] + 2, :].rearrange("p a b -> p (a b)"), cst.res)
    k.mm(B[:, 0:128], aT, bT)
    k.mm(B[:, 128:384], bT, fm[:, 0:256])
    yield
    A0, NBt, CK = R["A0"], R["NB_"], R["CK"]
    k.tt("dve", A0[:], B[:, 0:128], CM(dd["Ms"]), ALU.mult)
    k.tt("dve", NBt[:], B[:, 128:384], msk, ALU.mult)
    yield
    k.mm(B[:, 0:256], kT, fm[:, 0:256])
    yield
    k.tt("dve", CK[:], B[:, 0:256], msk, ALU.mult)
    yield
    TT = yield from _invert_g(k, R, B, A0[:], NBt[:, 0:128], CM)
    k.mm(B[:, 0:64], CK[:, 0:128], Vt)
    yield
    akv, U, WT, X, O = R["akv"], R["U"], R["WT"], R["X"], R["O"]
    k.cp("act", akv[:, 0:64], B[:, 0:64])
    yield
    k.mm(B[:, 0:64], TT, akv[:, 0:64])
    k.mm(B[0:64, 128:256], A_tm, TT)
    yield
    k.cp("act", U[:, 0:64], B[:, 0:64])
    k.cp("dve", WT[0:64, :], B[0:64, 128:256])
    yield
    k.mm(B[:, 0:64], WT[0:64, :], Zo[:])
    yield
    k.tt("dve", X[:, 0:64], B[:, 0:64], U[:, 0:64], ALU.add)
    yield
    k.mm(B[0:64, 64:128], Kh, Vt, start=True, stop=False)
    k.mm(B[0:64, 64:128], Bh, X[:, 0:64], start=False, stop=True)
    k.mm(B[:, 128:192], rT, Zo[:], start=True, stop=False)
    k.mm(B[:, 128:192], NBt[:, 128:256], X[:, 0:64], start=False, stop=False)
    k.mm(B[:, 128:192], CK[:, 128:256], Vt, start=False, stop=True)
    yield
    k.stt("dve", Zn[:], Zo[:], rdv[0:64, hd:hd + 1], B[0:64, 64:128], ALU.mult, ALU.add)
    k.cp("act", O[:, 0:64], B[:, 128:192])
    k.dma(rw_y.v(lambda hh: hh[g, :, hd * 64:(hd + 1) * 64]), O[:, 0:64])


def _gd_step_g(k, R, B, cst, CM, dd, g, d, Zo, Zn, gd_fm, gd_tm, gd_lr, gd_dv, gd_o):
    ident = CM(IDN)
    fm = R["fm"].next()
    k.dma(fm[:], gd_fm[g, d, :, :])
    tm = R["tm"].next()
    k.dma(tm[:], gd_tm[g, d, :, :])
    lr = R["lr"].next()
    k.dma(lr[:], gd_lr[g, d, :, :])
    gdv = R["dv"].next()
    k.dma(gdv[:, 0:1], gd_dv[g, d, :, :])
    kT, qT, kbT, qgT = fm[:, 0:128], fm[:, 128:256], fm[:, 256:384], fm[:, 384:512]
    vb, kbg, kg = tm[:, 0:128], tm[:, 128:256], tm[:, 256:384]
    k.mm(B[:, 0:128], lr[:, 0:128], lr[:, 128:256])
    k.mm(B[:, 128:256], lr[:, 128:256], lr[:, 0:128])
    yield
    m, D, A0, NBt = R["m"], R["D"], R["A0"], R["NB_"]
    k.stt("dve", m[:, 0:128], B[:, 0:128], 0.0, CM(dd["Ns"]), ALU.min, ALU.add)
    k.stt("dve", m[:, 128:256], B[:, 128:256], 0.0, CM(dd["NTs"]), ALU.min, ALU.add)
    yield
    k.act(D[:, 0:256], m[:, 0:256], AF.Exp)
    k.mm(B[:, 0:128], kbT, kT)
    k.mm(B[:, 128:384], kT, fm[:, 128:384])
    yield
    k.tt("pool", D[:, 256:384], D[:, 128:256], ident, ALU.add)
    k.stt("dve", A0[:], B[:, 0:128], -1.0, D[:, 0:128], ALU.mult, ALU.mult)
    k.stt("dve", NBt[:, 0:128], B[:, 256:384], -1.0, D[:, 128:256], ALU.mult, ALU.mult)
    yield
    k.tt("dve", NBt[:, 128:256], B[:, 128:256], D[:, 256:384], ALU.mult)
    yield
    TT = yield from _invert_g(k, R, B, A0[:], NBt[:, 0:128], CM)
    k.mm(B[:, 0:128], TT, vb)
    k.mm(B[:, 128:256], kbg, TT)
    yield
    U, WT, X, O = R["U"], R["WT"], R["X"], R["O"]
    k.cp("act", U[:], B[:, 0:128])
    k.ts("dve", WT[:], B[:, 128:256], -1.0, None, ALU.mult)
    yield
    k.mm(B[:, 0:128], WT[:], Zo[:])
    yield
    k.tt("dve", X[:], B[:, 0:128], U[:], ALU.add)
    yield
    k.mm(B[:, 128:256], kg, X[:])
    k.mm(B[:, 256:384], qgT, Zo[:], start=True, stop=False)
    k.mm(B[:, 256:384], NBt[:, 128:256], X[:], start=False, stop=True)
    yield
    k.stt("dve", Zn[:], Zo[:], gdv[:, 0:1], B[:, 128:256], ALU.mult, ALU.add)
    k.cp("act", O[:], B[:, 256:384])
    k.dma(gd_o[g, d, :, :], O[:])


def _phase_a2(k, nc, NB, SEQ, NCH, CPS, cst, CM, rw_fm, rw_tm, rw_dv, rw_y, gd_fm, gd_tm, gd_lr, gd_dv, gd_o):
    NRW = NB * 4
    NGD = NB * 2
    Zrw = [[k.sb("zrw%d_%d" % (s, i), [64, 64]) for i in range(2)] for s in range(NRW)]
    Zgd = [[k.sb("zgd%d_%d" % (s, i), [128, 128]) for i in range(2)] for s in range(NGD)]
    for s in range(NRW):
        k.memset("pool", Zrw[s][0][:], 0.0)
    for s in range(NGD):
        k.memset("pool", Zgd[s][0][:], 0.0)
    slots = []
    for sl in range(6):
        t = "s%d_" % sl
        R = dict(TN0=k.sb(t + "TN0", [128, 256]), TN1=k.sb(t + "TN1", [128, 256]),
                 Ak0=k.sb(t + "Ak0", [128, 128]), Ak1=k.sb(t + "Ak1", [128, 128]),
                 A0=k.sb(t + "A0", [128, 128]), NB_=k.sb(t + "NB", [128, 256]),
                 U=k.sb(t + "U", [128, 128]), WT=k.sb(t + "WT", [128, 128]),
                 X=k.sb(t + "X", [128, 128]), O=k.sb(t + "O", [128, 128]),
                 dv=k.ring(t + "dv", 2, [128, 4]))
        if sl < 4:
            R.update(CK=k.sb(t + "CK", [128, 256]), akv=k.sb(t + "akv", [128, 64]),
                     rfm=k.ring(t + "rfm", 2, [64, 512]), rtm=k.ring(t + "rtm", 2, [128, 256]))
        else:
            R.update(D=k.sb(t + "D", [128, 384]), m=k.sb(t + "m", [128, 256]),
                     fm=k.ring(t + "fm", 2, [128, 512]), tm=k.ring(t + "tm", 2, [128, 384]),
                     lr=k.ring(t + "lr", 2, [2, 256]))
        slots.append(R)
    banks = k.psr[0:6]
    for step in range(CPS):
        par = step % 2
        for bi in range(NB):
            gens = []
            for d in range(2):
                dd = DIRS[d]
                ci = step if d == 0 else CPS - 1 - step
                g = bi * CPS + ci
                for h in range(2):
                    hd = d * 2 + h
                    s = bi * 4 + hd
                    gens.append(_rw_step_g(k, slots[hd], banks[hd], cst, CM, dd, g, hd,
                                           Zrw[s][par], Zrw[s][1 - par], rw_fm, rw_tm, rw_dv, rw_y))
                s = bi * 2 + d
                gens.append(_gd_step_g(k, slots[4 + d], banks[4 + d], cst, CM, dd, g, d,
                                       Zgd[s][par], Zgd[s][1 - par], gd_fm, gd_tm, gd_lr, gd_dv, gd_o))
            act_ = list(gens)
            while act_:
                for gg in list(act_):
                    try:
                        next(gg)
                    except StopIteration:
                        act_.remove(gg)
```

```python
import contextlib
import numpy as np
import concourse.bass as bass
import concourse.mybir as mybir
from concourse.bass_utils import run_bass_kernel_spmd

F32 = mybir.dt.float32
BF16 = mybir.dt.bfloat16
AF = mybir.ActivationFunctionType
ALU = mybir.AluOpType
AX = mybir.AxisListType

D_MODEL = 2048
NJ = 16
RW_IN = 3488
G0 = 3488
GATE0 = 3488 + 3104
FFN = 5632
C0 = -0.6065306597126334
NEG = -30000.0
DBG = {}
PIDC = {}


class Res:
    __slots__ = ("name", "w", "rs", "excl")

    def __init__(self, name="", excl=False):
        self.name = name
        self.w = None
        self.rs = []
        self.excl = excl


class V:
    __slots__ = ("ap", "res")

    def __init__(self, ap, res):
        self.ap = ap
        self.res = res

    def __getitem__(self, idx):
        return V(self.ap[idx], self.res)


class T:
    def __init__(self, h, name, excl=False):
        self.h = h
        self.res = Res(name, excl)

    def __getitem__(self, idx):
        return V(self.h[idx], self.res)

    def v(self, fn):
        return V(fn(self.h), self.res)


class Sched:
    ENG = ("pe", "dve", "act", "pool", "sp")

    def __init__(self, nc, n_dma_sems=16):
        self.nc = nc
        self.ops = {e: [] for e in self.ENG}
        self.cnt = {e: 0 for e in self.ENG}
        self.seen = {e: {} for e in self.ENG}
        self.n_dma_sems = n_dma_sems
        self.dma_use = [0] * n_dma_sems
        self.dma_rr = 0
        self.total = 0
        self.pending = {e: {} for e in self.ENG}
        self.ncc = 0
        self.last_ev = None
        self.last_cev = None
        self.last_g = None
        self.last_dev = None

    def barrier(self):
        allev = {}
        for e in self.ENG:
            if self.cnt[e] > 0 and e != "sp":
                allev[e] = self.cnt[e]
        for i in range(self.n_dma_sems):
            if self.dma_use[i] > 0:
                allev[("dma", i)] = 16 * self.dma_use[i]
        if self.ncc > 0:
            allev["cc"] = self.ncc
        for e in self.ENG:
            for kk, vv in allev.items():
                if self.pending[e].get(kk, 0) < vv:
                    self.pending[e][kk] = vv

    @staticmethod
    def _add(deps, ev):
        if ev is None:
            return
        k, v = ev
        if deps.get(k, 0) < v:
            deps[k] = v

    def op(self, eng, fn, reads=(), writes=(), dma=False, ndma=1, cc=False):
        if eng == "pool" and DBG.get("nopool", 0):
            eng = "dve"
        deps = {}
        for r in reads:
            self._add(deps, r.w)
            if r.excl:
                for ev in r.rs:
                    self._add(deps, ev)
        for w in writes:
            self._add(deps, w.w)
            for ev in w.rs:
                self._add(deps, ev)
        if cc:
            self.ncc += 1
            ev = ("cc", self.ncc)
        elif dma:
            i = self.dma_rr
            self.dma_rr = (self.dma_rr + 1) % self.n_dma_sems
            if self.dma_use[i] > 0:
                self._add(deps, (("dma", i), 16 * self.dma_use[i]))
            self.dma_use[i] += ndma
            ev = (("dma", i), 16 * self.dma_use[i])
        else:
            self.cnt[eng] += 1
            ev = (eng, self.cnt[eng])
        if eng == "pe":
            deps.pop("pe", None)
        sm = DBG.get("serial", 0)
        if sm == 1 and self.last_ev is not None:
            self._add(deps, self.last_ev)
        if sm == 2 and not dma and self.last_cev is not None:
            self._add(deps, self.last_cev)
        grp = {5: ("act", "dve", "pool"), 6: ("pe", "act"), 7: ("pe", "dve"), 8: ("act", "dve")}.get(sm)
        if grp and not dma and eng in grp and self.last_g is not None:
            self._add(deps, self.last_g)
        if sm == 3 and dma and self.last_ev is not None:
            self._add(deps, self.last_ev)
        if sm == 3 and not dma and self.last_dev is not None:
            self._add(deps, self.last_dev)
        if self.pending[eng]:
            for kk, vv in self.pending[eng].items():
                if deps.get(kk, 0) < vv:
                    deps[kk] = vv
            self.pending[eng] = {}
        seen = self.seen[eng]
        waits = []
        for k, v in deps.items():
            if seen.get(k, 0) < v:
                seen[k] = v
                waits.append((k, v))
        self.ops[eng].append((fn, waits, ev))
        self.last_ev = ev
        grp = {5: ("act", "dve", "pool"), 6: ("pe", "act"), 7: ("pe", "dve"), 8: ("act", "dve")}.get(DBG.get("serial", 0))
        if grp and not dma and eng in grp:
            self.last_g = ev
        if dma:
            self.last_dev = ev
        else:
            self.last_cev = ev
        for r in reads:
            r.rs.append(ev)
        for w in writes:
            w.w = ev
            w.rs = []
        self.total += 1
        return ev

    def emit(self, final_waits=()):
        nc = self.nc
        sems = {}
        with contextlib.ExitStack() as st:
            for e in self.ENG:
                sems[e] = st.enter_context(nc.semaphore("s_" + e))
            for i in range(self.n_dma_sems):
                sems[("dma", i)] = st.enter_context(nc.semaphore("s_dma%d" % i))
            sems["cc"] = st.enter_context(nc.semaphore("s_cc"))
            deps = {}
            for r in final_waits:
                self._add(deps, r.w)
            fw = list(deps.items())
            block = st.enter_context(nc.Block())

            def mk(ename):
                def body(eng):
                    for fn, waits, ev in self.ops[ename]:
                        for k, v in waits:
                            eng.wait_ge(sems[k], v)
                        ins = fn(eng)
                        k, v = ev
                        if k == "cc":
                            ins.then_inc(sems[k])
                        elif isinstance(ins, list):
                            for i_ in ins:
                                i_.then_inc(sems[k], 16)
                        else:
                            ins.then_inc(sems[k], 16 if isinstance(k, tuple) else 1)
                    if ename == "sp":
                        for k, v in fw:
                            eng.wait_ge(sems[k], v)
                        for i in range(self.n_dma_sems):
                            if self.dma_use[i] > 0:
                                eng.wait_ge(sems[("dma", i)], 16 * self.dma_use[i])
                return body

            block.tensor(mk("pe"))
            block.vector(mk("dve"))
            block.scalar(mk("act"))
            block.gpsimd(mk("pool"))
            block.sync(mk("sp"))


class KB:
    def __init__(self, nc):
        self.nc = nc
        self.S = Sched(nc)
        self.st = contextlib.ExitStack()
        self.cur = self.st
        self.nps = 0
        self.psr = []
        self.psi = 0

    def sb(self, name, shape, dt=F32):
        return T(self.cur.enter_context(self.nc.sbuf_tensor("sb_" + name, list(shape), dt)), name)

    @contextlib.contextmanager
    def phase(self):
        old = self.cur
        with contextlib.ExitStack() as sub:
            self.cur = sub
            yield
            self.cur = old
        self.S.barrier()

    def ring(self, name, n, shape, dt=F32):
        return Ring([self.sb("%s%d" % (name, i), shape, dt) for i in range(n)])

    def init_psum(self, n=8):
        self.psr = [T(self.st.enter_context(self.nc.psum_tensor("ps%d" % i, [128, 512], F32)), "ps%d" % i, True)
                    for i in range(n)]

    def ps(self):
        t = self.psr[self.psi]
        self.psi = (self.psi + 1) % len(self.psr)
        return t

    def dram(self, name, shape, dt=F32, kind="Internal"):
        return T(self.nc.dram_tensor(name, list(shape), dt, kind=kind).ap(), name)

    @staticmethod
    def _rs(*vs):
        return [x.res for x in vs if isinstance(x, V)]

    @staticmethod
    def _a(x):
        return x.ap if isinstance(x, V) else x

    def mm(self, out, lhsT, rhs, start=True, stop=True):
        self.S.op("pe", lambda e: e.matmul(out.ap, lhsT=lhsT.ap, rhs=rhs.ap, start=start, stop=stop),
                  reads=[lhsT.res, rhs.res], writes=[out.res])

    def tr(self, out, in_, ident):
        self.S.op("pe", lambda e: e.transpose(out.ap, in_.ap, ident.ap),
                  reads=[in_.res, ident.res], writes=[out.res])

    def act(self, out, in_, func, scale=1.0, bias=0.0, eng="act"):
        a = self._a
        self.S.op(eng, lambda e: e.activation(out=out.ap, in_=in_.ap, func=func, bias=a(bias), scale=a(scale)),
                  reads=self._rs(in_, scale, bias), writes=[out.res])

    def tt(self, eng, out, a, b, op):
        self.S.op(eng, lambda e: e.tensor_tensor(out=out.ap, in0=a.ap, in1=b.ap, op=op),
                  reads=[a.res, b.res], writes=[out.res])

    def ts(self, eng, out, a, s1, s2, op0, op1=None):
        g = self._a
        if op1 is None:
            self.S.op(eng, lambda e: e.tensor_scalar(out=out.ap, in0=a.ap, scalar1=g(s1), scalar2=None, op0=op0),
                      reads=self._rs(a, s1), writes=[out.res])
        else:
            self.S.op(eng, lambda e: e.tensor_scalar(out=out.ap, in0=a.ap, scalar1=g(s1), scalar2=g(s2),
                                                     op0=op0, op1=op1),
                      reads=self._rs(a, s1, s2), writes=[out.res])

    def stt(self, eng, out, a, s, b, op0, op1):
        g = self._a
        self.S.op(eng, lambda e: e.scalar_tensor_tensor(out=out.ap, in0=a.ap, scalar=g(s), in1=b.ap,
                                                        op0=op0, op1=op1),
                  reads=self._rs(a, s, b), writes=[out.res])

    def cp(self, eng, out, a):
        if eng == "act":
            self.S.op(eng, lambda e: e.copy(out=out.ap, in_=a.ap), reads=[a.res], writes=[out.res])
        else:
            self.S.op(eng, lambda e: e.tensor_copy(out=out.ap, in_=a.ap), reads=[a.res], writes=[out.res])

    def red(self, eng, out, a, op=None):
        self.S.op(eng, lambda e: e.tensor_reduce(out=out.ap, in_=a.ap, axis=AX.X, op=op or ALU.add),
                  reads=[a.res], writes=[out.res])

    def memset(self, eng, out, val):
        self.S.op(eng, lambda e: e.memset(out.ap, val), writes=[out.res])

    def dma(self, out, in_, eng="sp"):
        self.S.op(eng, lambda e: e.dma_start(out=out.ap, in_=in_.ap), reads=[in_.res], writes=[out.res], dma=True)

    def dma_multi(self, pairs, eng="sp"):
        self.S.op(eng, lambda e: [e.dma_start(out=o.ap, in_=i.ap) for o, i in pairs],
                  reads=[i.res for o, i in pairs], writes=[o.res for o, i in pairs], dma=True, ndma=len(pairs))

    def rsqrt(self, out, in_, scale=1.0, bias=0.0):
        self.act(out, in_, AF.Ln, scale=scale, bias=bias)
        self.act(out, out, AF.Exp, scale=-0.5)


class Ring:
    def __init__(self, tiles):
        self.tiles = tiles
        self.i = 0

    def next(self):
        t = self.tiles[self.i]
        self.i = (self.i + 1) % len(self.tiles)
        return t


IDN, ONE, LS, LI, US, UI, NLS, NUS = range(8)
NCST = 8
DIRS = {
    0: dict(Ms=LS, MTs=US, MTi=UI, Ti=UI, Ts=US, Tr=LS, Ns=NLS, NTs=NUS, last=127),
    1: dict(Ms=US, MTs=LS, MTi=LI, Ti=LI, Ts=LS, Tr=US, Ns=NUS, NTs=NLS, last=0),
}


def make_consts():
    i = np.arange(128)
    ls = (i[:, None] > i[None, :]).astype(np.float32)
    li = (i[:, None] >= i[None, :]).astype(np.float32)
    us = ls.T.copy()
    ui = li.T.copy()
    c = np.zeros((128, NCST, 128), np.float32)
    c[:, IDN] = np.eye(128)
    c[:, ONE] = 1.0
    c[:, LS] = ls
    c[:, LI] = li
    c[:, US] = us
    c[:, UI] = ui
    c[:, NLS] = (ls - 1.0) * (-NEG)
    c[:, NUS] = (us - 1.0) * (-NEG)
    return c


RC_W0, RC_A0, RC_KK, RC_KA, RC_RK, RC_GNW, RC_GNB, RC_DTB, RC_ALOG, RC_GNORM = 0, 256, 512, 640, 896, 1024, 1152, 1280, 1282, 1284
NROWC = 1284 + 128
PR_GAIN, PR_MU, PR_CW = 0, 16, 23
NPRM = 23 + 15


def build_phase_a(NB, SEQ, upto='a3'):
    NTOK = NB * SEQ
    NCH = NTOK // 128
    NBLK = NTOK // 256
    CPS = SEQ // 128
    nc = bass.Bass("TRN2", target_bir_lowering=False)
    k = KB(nc)
    with k.st:
        xT = k.dram("xT", [D_MODEL, NTOK], kind="ExternalInput")
        WA = k.dram("WA", [11, 128, NJ * 128], kind="ExternalInput")
        prm_d = k.dram("prm", [128, NPRM], kind="ExternalInput")
        rowc_d = k.dram("rowc", [128, NROWC], kind="ExternalInput")
        lw_d = k.dram("lw", [128, 768], kind="ExternalInput")
        cst_d = k.dram("cst", [128, NCST * 128], kind="ExternalInput")
        yT = k.dram("yT", [256, NTOK], kind="ExternalOutput")
        rw_fm = k.dram("rw_fm", [NCH, 4, 64, 512])
        rw_tm = k.dram("rw_tm", [NCH, 128, 4 * 256])
        rw_dv = k.dram("rw_dv", [NCH, 64, 4])
        rw_post = k.dram("rw_post", [NCH, 128, 258])
        rw_y = k.dram("rw_y", [NCH, 128, 256])
        gd_fm = k.dram("gd_fm", [NCH, 2, 128, 512])
        gd_tm = k.dram("gd_tm", [NCH, 2, 128, 384])
        gd_lr = k.dram("gd_lr", [NCH, 2, 2, 256])
        gd_dv = k.dram("gd_dv", [NCH, 2, 128, 1])
        gd_post = k.dram("gd_post", [NCH, 128, 128])
        gd_o = k.dram("gd_o", [NCH, 2, 128, 128])

        k.init_psum(8)
        cst = k.sb("cst", [128, NCST, 128])
        k.dma(cst.v(lambda h: h[:].rearrange("p a b -> p (a b)")), cst_d[:, :])
        prm = k.sb("prm", [128, NPRM])
        k.dma(prm[:], prm_d[:, :])
        rowc = k.sb("rowc", [128, NROWC])
        k.dma(rowc[:], rowc_d[:, :])
        lw = k.sb("lw", [128, 768])
        k.dma(lw[:], lw_d[:, :])

        def CM(i):
            return cst[:, i, :]

        ident = CM(IDN)
        ones = CM(ONE)

        with k.phase():
            _A1R.clear()
            _A1G.clear()
            if upto != 'a0':
                _phase_a1(k, nc, NB, SEQ, NTOK, NCH, NBLK, xT, WA, prm, rowc, lw, cst, CM,
                          rw_fm, rw_tm, rw_dv, rw_post, gd_fm, gd_tm, gd_lr, gd_dv, gd_post)
        if upto in ('a2', 'a3'):
            with k.phase():
                _phase_a2(k, nc, NB, SEQ, NCH, CPS, cst, CM, rw_fm, rw_tm, rw_dv, rw_y, gd_fm, gd_tm, gd_lr, gd_dv, gd_o)
        if upto == 'a3':
            with k.phase():
                _phase_a3(k, nc, NCH, rowc, CM, rw_post, rw_y, gd_post, gd_o, yT)
        else:
            k.dma(yT[0:128, 0:128], cst[:, 0, :])
        k.S.emit(final_waits=[yT.res])
    return nc


def _phase_a1(k, nc, NB, SEQ, NTOK, NCH, NBLK, xT, WA, prm, rowc, lw, cst, CM,
              rw_fm, rw_tm, rw_dv, rw_post, gd_fm, gd_tm, gd_lr, gd_dv, gd_post):
    ident = CM(IDN)
    ones = CM(ONE)
    WAs = k.sb("WAs", [128, 11, NJ, 128], BF16)
    sq_r = k.ring("sq", 1, [128, 8, 260])
    sq0 = sq_r.tiles[0]
    gain = prm[:, PR_GAIN:PR_GAIN + 16]
    for t in range(11):
        stg_flat = sq0.v(lambda h: h[:].rearrange("p a b -> p (a b)")[:, 0:NJ * 128])
        k.dma(stg_flat, WA[t, :, :])
        stg3 = sq0.v(lambda h: h[:].rearrange("p a b -> p (a b)")[:, 0:NJ * 128].rearrange("p (j c) -> p j c", j=NJ))
        k.tt("dve" if t % 2 == 0 else "pool", WAs[:, t, :, :], stg3,
             V(gain.ap.unsqueeze(2).to_broadcast([128, NJ, 128]), gain.res), ALU.mult)
    omm = k.sb("omm", [128, 7])
    hmu = k.sb("hmu", [128, 7])
    k.ts("dve", omm[:], prm[:, PR_MU:PR_MU + 7], -1.0, 1.0, ALU.mult, ALU.add)
    k.ts("dve", hmu[:], prm[:, PR_MU:PR_MU + 7], 0.5, None, ALU.mult)
    omka = k.sb("omka", [128, 256])
    k.ts("dve", omka[:], rowc[:, RC_KA:RC_KA + 256], -1.0, 1.0, ALU.mult, ALU.add)
    nA = k.sb("nA", [128, 2])
    k.act(nA[:], rowc[:, RC_ALOG:RC_ALOG + 2], AF.Exp)
    k.ts("dve", nA[:], nA[:], -1.0, None, ALU.mult)
    c10 = k.sb("c10", [2, 2])
    k.cp("dve", c10[:], V(ident.ap[0:2, 0:2], ident.res))
    cw = prm[:, PR_CW:PR_CW + 15]

    xt_r = k.ring("xt", 1, [128, NJ, 260])
    u_r = k.ring("u", 2, [128, NJ, 260], BF16)
    rstd_r = k.ring("rstd", 2, [128, 260])
    pa_r = k.ring("pa", 3, [128, 260])
    s_r = k.ring("shs", 2, [128, 256])
    m1_r = k.ring("shm", 2, [128, 256])
    pm = [k.ring("pm%d" % t, 2, [128, 256]) for t in range(7)]
    cacc = k.ring("cacc", 2, [128, 256])
    qkv = [k.ring("qkv%d" % t, 2, [128, 256]) for t in range(3)]
    sqn = k.ring("sqn", 2, [128, 256])
    rn_r = k.ring("rnq", 2, [128, 256])

    banks = k.psr
    k.psr = banks[6:8]
    k.psi = 0

    def proj_gen(b, outd):
            t0 = b * 256
            seq0 = (t0 // SEQ) * SEQ
            lo = t0 - 2
            hi = t0 + 258
            xt = xt_r.next()
            c_lo, c_hi = 0, 260
            if lo < seq0:
                c_lo = 2
                k.memset("pool", xt[:, :, 0:2], 0.0)
            if hi > seq0 + SEQ:
                c_hi = 258
                k.memset("pool", xt[:, :, 258:260], 0.0)
            k.dma_multi([(xt[:, jq * 4:(jq + 1) * 4, c_lo:c_hi],
                          xT.v(lambda h, jq=jq: h[jq * 512:(jq + 1) * 512, lo + c_lo:lo + c_hi].rearrange("(j p) t -> p j t", p=128)))
                         for jq in range(4)])
            pss = k.ps()
            for hf in range(2):
                sq = sq_r.next()
                k.act(sq[:], xt[:, hf * 8:(hf + 1) * 8, :], AF.Square)
                for j in range(8):
                    k.mm(pss[:, 0:260], ones, sq[:, j, :], start=(hf == 0 and j == 0), stop=(hf == 1 and j == 7))
            rstd = rstd_r.next()
            k.rsqrt(rstd[:], pss[:, 0:260], scale=1.0 / D_MODEL, bias=1e-6)
            u = u_r.next()
            k.tt("dve", u[:], xt[:], V(rstd.h[:].unsqueeze(1).to_broadcast([128, NJ, 260]), rstd.res), ALU.mult)
            yield

            pmt = []
            for t in range(10):
                pp = k.ps()
                for j in range(NJ):
                    k.mm(pp[:, 0:260], WAs[:, t, j, :], u[:, j, :], start=(j == 0), stop=(j == NJ - 1))
                pa = pa_r.next()
                k.cp("act", pa[:], pp[:, 0:260])
                if t < 7:
                    s = s_r.next()
                    k.tt("pool", s[:], pa[:, 1:257], pa[:, 3:259], ALU.add)
                    m1 = m1_r.next()
                    k.act(m1[:], pa[:, 2:258], AF.Copy, scale=omm[:, t:t + 1])
                    o = pm[t].next()
                    k.stt("dve", o[:], s[:], hmu[:, t:t + 1], m1[:], ALU.mult, ALU.add)
                    pmt.append(o)
                else:
                    tc_ = t - 7
                    acc = cacc.next()
                    k.ts("dve", acc[:], pa[:, 0:256], cw[:, tc_ * 5:tc_ * 5 + 1], None, ALU.mult)
                    for kk_ in range(1, 5):
                        k.stt("dve", acc[:], pa[:, kk_:kk_ + 256], cw[:, tc_ * 5 + kk_:tc_ * 5 + kk_ + 1], acc[:],
                              ALU.mult, ALU.add)
                    o = qkv[tc_].next()
                    k.act(o[:], acc[:], AF.Silu)
                    pmt.append(o)
                yield
            qn = []
            for qi in range(2):
                src = pmt[7 + qi]
                s2 = sqn.next()
                k.tt("pool", s2[:], src[:], src[:], ALU.mult)
                pq = k.ps()
                k.mm(pq[:, 0:256], ones, s2[:])
                rn = rn_r.next()
                k.rsqrt(rn[:], pq[:, 0:256], scale=1.0, bias=1e-6)
                if qi == 0:
                    k.stt("dve", src[:], src[:], 128.0 ** -0.5, rn[:], ALU.mult, ALU.mult)
                else:
                    k.tt("dve", src[:], src[:], rn[:], ALU.mult)
            outd["pmt"] = pmt
            outd["u"] = u


    def run_all(gens):
        act_ = list(gens)
        while act_:
            for gg in list(act_):
                try:
                    next(gg)
                except StopIteration:
                    act_.remove(gg)

    pending = []
    for b in range(NBLK):
        outd = {}
        run_all([proj_gen(b, outd)] + pending)
        pmt, u = outd["pmt"], outd["u"]
        pending = []
        for cc in range(2):
            g = b * 2 + cc
            cs = slice(cc * 128, cc * 128 + 128)
            pending.append(_a1_rwkv_chunk(k, banks[2 * cc], banks[2 * cc + 1], g, cs, pmt, rowc, lw, CM, omka,
                                          rw_fm, rw_tm, rw_dv, rw_post))
            pending.append(_a1_gdn_chunk(k, banks[4 + cc], g, cs, cc, u, WAs, pmt, rowc, CM, nA, c10,
                                         gd_fm, gd_tm, gd_lr, gd_dv, gd_post))
    run_all(pending)
    k.psr = banks
    k.psi = 0


_A1R = {}


def _a1_rwkv_chunk(k, BX, BY, g, cs, pmt, rowc, lw, CM, omka, rw_fm, rw_tm, rw_dv, rw_post):
    ident = CM(IDN)
    ones = CM(ONE)
    R = _A1R
    if not R:
        R["th"] = k.ring("a1th", 2, [128, 128])
        R["sg0"] = k.ring("a1sg0", 2, [128, 128])
        R["sg1"] = k.ring("a1sg1", 2, [32, 128])
        R["rkv"] = k.ring("a1rkv", 2, [128, 384])
        R["sw"] = k.ring("a1sw", 2, [128, 256])
        R["aa"] = k.ring("a1aa", 2, [128, 256])
        R["E"] = [k.ring("a1E%d" % i, 2, [128, 256]) for i in range(4)]
        R["kx"] = k.ring("a1kx", 2, [128, 128])
        R["kx2"] = k.ring("a1kx2", 2, [128, 128])
        R["ss"] = k.ring("a1ss", 2, [128, 2])
        R["kk"] = k.ring("a1kk", 2, [128, 128])
        R["kka"] = k.ring("a1kka", 2, [128, 256])
        R["t1"] = k.ring("a1t1", 2, [128, 256])
        R["kd"] = k.ring("a1kd", 2, [128, 256])
        R["q4"] = k.ring("a1q4", 2, [128, 4, 256])
        R["tm"] = k.ring("a1tm", 2, [128, 4, 4, 64])
        R["fm"] = k.ring("a1fm", 2, [64, 512])
        R["post"] = k.ring("a1post", 2, [128, 258])
        R["bs"] = k.ring("a1bs", 2, [128, 128])
        R["dv"] = k.ring("a1dv", 2, [64, 4])
    th = R["th"].next()
    k.act(th[:], pmt[3][:, cs], AF.Tanh)
    sg0 = R["sg0"].next()
    k.act(sg0[:], pmt[5][:, cs], AF.Sigmoid)
    sg1 = R["sg1"].next()
    k.act(sg1[:], pmt[6][0:32, cs], AF.Sigmoid)
    yield
    p_aw = BX[:, 0:256]
    p_aa = BX[:, 256:512]
    p_g = BY[:, 0:128]
    p_t = BY[:, 128:512]
    k.mm(p_aw, th[:], lw[:, 0:256], start=True, stop=False)
    k.mm(p_aw, V(ones.ap[0:1, :], ones.res), rowc[0:1, RC_W0:RC_W0 + 256], start=False, stop=True)
    k.mm(p_aa, pmt[4][:, cs], lw[:, 256:512], start=True, stop=False)
    k.mm(p_aa, V(ones.ap[0:1, :], ones.res), rowc[0:1, RC_A0:RC_A0 + 256], start=False, stop=True)
    k.mm(p_g, sg0[:], lw[:, 512:640], start=True, stop=False)
    k.mm(p_g, sg1[:], lw[0:32, 640:768], start=False, stop=True)
    for i in range(3):
        k.tr(p_t[:, i * 128:(i + 1) * 128], pmt[i][:, cs], ident)
    yield
    rkv = R["rkv"].next()
    k.cp("act", rkv[:], p_t)
    r_tm, k_tm, v_tm = rkv[:, 0:128], rkv[:, 128:256], rkv[:, 256:384]
    post = R["post"].next()
    k.cp("pool", post[:, 0:128], v_tm)
    k.cp("act", post[:, 128:256], p_g)
    sw = R["sw"].next()
    k.act(sw[:], p_aw, AF.Sigmoid)
    aa = R["aa"].next()
    k.act(aa[:], p_aa, AF.Sigmoid)
    yield
    pL = [BX[:, 0:256], BX[:, 256:512], BY[:, 0:256]]
    for d in range(2):
        dd = DIRS[d]
        for i, key in enumerate(("Ti", "Ts", "Tr")):
            k.mm(pL[i][:, d * 128:(d + 1) * 128], CM(dd[key]), sw[:, d * 128:(d + 1) * 128])
    p_dv = BY[0:64, 256:260]
    for hd in range(4):
        k.mm(p_dv[:, hd:hd + 1], sw[:, hd * 64:(hd + 1) * 64], V(ones.ap[:, 0:1], ones.res))
    yield
    E = [r.next() for r in R["E"]]
    k.act(E[0][:], pL[0], AF.Exp, scale=C0)
    k.act(E[1][:], pL[0], AF.Exp, scale=-C0)
    k.act(E[2][:], pL[1], AF.Exp, scale=C0)
    k.act(E[3][:], pL[2], AF.Exp, scale=C0)
    dv = R["dv"].next()
    k.act(dv[:], p_dv, AF.Exp, scale=C0)
    k.dma(rw_dv[g, :, :], dv[:])
    yield
    kx = R["kx"].next()
    k.tt("dve", kx[:], k_tm, rowc[:, RC_KK:RC_KK + 128], ALU.mult)
    kx2 = R["kx2"].next()
    k.tt("pool", kx2[:], kx[:], kx[:], ALU.mult)
    yield
    ss = R["ss"].next()
    k.red("dve", ss[:], kx2.v(lambda h: h[:].rearrange("p (a b) -> p a b", a=2)))
    k.rsqrt(ss[:], ss[:], scale=1.0, bias=1e-6)
    kk = R["kk"].next()
    k.tt("dve", kk.v(lambda h: h[:].rearrange("p (a b) -> p a b", a=2)),
         kx.v(lambda h: h[:].rearrange("p (a b) -> p a b", a=2)),
         V(ss.h[:].unsqueeze(2).to_broadcast([128, 2, 64]), ss.res), ALU.mult)

    def bc2(v):
        return V(v.ap.unsqueeze(1).to_broadcast([128, 2, 128]), v.res)

    def as2(t_):
        return t_.v(lambda h: h[:].rearrange("p (a b) -> p a b", a=2))

    yield
    kka = R["kka"].next()
    k.tt("dve", as2(kka), bc2(kk[:]), as2(aa), ALU.mult)
    t1 = R["t1"].next()
    k.tt("pool", t1[:], aa[:], rowc[:, RC_KA:RC_KA + 256], ALU.mult)
    k.tt("pool", t1[:], t1[:], omka[:], ALU.add)
    kd = R["kd"].next()
    k.tt("dve", as2(kd), as2(t1), bc2(k_tm), ALU.mult)
    q4 = R["q4"].next()
    tm = R["tm"].next()

    def q4v(i):
        return q4.v(lambda h: h[:, i, :].rearrange("p (a b) -> p a b", a=2))

    def tmv(i):
        return tm.v(lambda h: h[:, :, i, :])

    def as4(t_):
        return t_.v(lambda h: h[:].rearrange("p (a b) -> p a b", a=4))

    yield
    k.stt("dve", q4v(0), bc2(kk[:]), -1.0, as2(E[2]), ALU.mult, ALU.mult)
    k.tt("pool", q4v(1), bc2(r_tm), as2(E[0]), ALU.mult)
    k.tt("dve", q4v(2), as2(kka), as2(E[1]), ALU.mult)
    k.tt("pool", q4v(3), as2(kd), as2(E[1]), ALU.mult)
    yield
    k.cp("pool", tmv(0), q4.v(lambda h: h[:, 0, :].rearrange("p (a b) -> p a b", a=4)))
    k.tt("dve", tmv(1), as4(kka), as4(E[3]), ALU.mult)
    k.tt("pool", tmv(2), as4(kd), as4(E[3]), ALU.mult)
    k.cp("pool", tm.v(lambda h: h[:, :, 3, :].rearrange("p (d h) b -> p d h b", d=2)), V(_vdup(v_tm.ap), v_tm.res))
    k.dma(rw_tm[g, :, :], tm.v(lambda h: h[:].rearrange("p a b c -> p (a b c)")))
    bs = R["bs"].next()
    k.tt("pool", bs[:], kd[:, 0:128], kd[:, 128:256], ALU.add)
    k.tt("pool", bs[:], bs[:], r_tm, ALU.mult)
    k.stt("dve", bs[:], bs[:], 0.5, rowc[:, RC_RK:RC_RK + 128], ALU.mult, ALU.mult)
    k.red("dve", post[:, 256:258], bs.v(lambda h: h[:].rearrange("p (a b) -> p a b", a=2)))
    k.dma(rw_post[g, :, :], post[:])
    for hd in range(4):
        pf = (BX if hd % 2 == 0 else BY)
        yield
        for i in range(4):
            k.tr(pf[0:64, i * 128:(i + 1) * 128], q4[:, i, hd * 64:(hd + 1) * 64], ident)
        yield
        fm = R["fm"].next()
        k.cp("act" if hd % 2 == 0 else "dve", fm[:], pf[0:64, 0:512])
        k.dma(rw_fm[g, hd, :, :], fm[:])


def _vdup(ap):
    return ap.rearrange("p (h b) -> p h b", h=2).unsqueeze(1).to_broadcast([128, 2, 2, 64])


_A1G = {}


def _a1_gdn_chunk(k, BZ, g, cs, cc, u, WAs, pmt, rowc, CM, nA, c10, gd_fm, gd_tm, gd_lr, gd_dv, gd_post):
    ident = CM(IDN)
    ones = CM(ONE)
    R = _A1G
    if not R:
        R["sz"] = k.ring("g1sz", 2, [128, 128])
        R["kv"] = k.ring("g1kv", 2, [128, 256])
        R["t4"] = k.ring("g1t4", 2, [128, 4])
        R["gb"] = k.ring("g1gb", 2, [128, 6])
        R["gn2"] = k.ring("g1gn2", 2, [128, 4])
        R["bc"] = k.ring("g1bc", 2, [128, 4, 128])
        R["eg"] = k.ring("g1eg", 2, [128, 128])
        R["fm"] = k.ring("g1fm", 2, [128, 512])
        R["tm"] = k.ring("g1tm", 2, [128, 384])
        R["ec"] = k.ring("g1ec", 2, [128, 2])
        R["sc"] = k.ring("g1sc", 2, [128, 1])
        R["lr"] = k.ring("g1lr", 2, [2, 256])
        R["dv"] = k.ring("g1dv", 2, [128, 1])
    pz = BZ[:, 0:128]
    pt = BZ[:, 128:384]
    p4 = BZ[:, 384:388]
    for j in range(NJ):
        k.mm(pz, u[:, j, 2 + cc * 128:2 + cc * 128 + 128], WAs[:, 10, j, :], start=(j == 0), stop=(j == NJ - 1))
    k.tr(pt[:, 0:128], pmt[8][:, cs], ident)
    k.tr(pt[:, 128:256], pmt[9][:, cs], ident)
    k.tr(p4, pmt[6][32:36, cs], V(ident.ap[32:36, 32:36], ident.res))
    yield
    sz = R["sz"].next()
    k.act(sz[:], pz, AF.Silu)
    k.dma(gd_post[g, :, :], sz[:])
    kv = R["kv"].next()
    k.cp("act", kv[:], pt)
    k_tm, v_tm = kv[:, 0:128], kv[:, 128:256]
    t4 = R["t4"].next()
    k.tt("dve", t4[:, 0:2], p4[:, 0:2], rowc[:, RC_DTB:RC_DTB + 2], ALU.add)
    gb = R["gb"].next()
    k.act(gb[:, 2:4], p4[:, 2:4], AF.Sigmoid)
    yield
    k.act(t4[:, 0:2], t4[:, 0:2], AF.Exp)
    k.act(t4[:, 0:2], t4[:, 0:2], AF.Ln, bias=1.0)
    yield
    k.tt("dve", gb[:, 0:2], t4[:, 0:2], nA[:], ALU.mult)
    yield
    gn2 = R["gn2"].next()
    k.cp("pool", gn2.v(lambda h: h[:].rearrange("p (d s) -> p d s", s=2)[:, :, 0]), gb[:, 0:2])
    k.ts("dve", gn2.v(lambda h: h[:].rearrange("p (d s) -> p d s", s=2)[:, :, 1]), gb[:, 0:2], -1.0, None, ALU.mult)
    bc = R["bc"].next()
    k.cp("pool", bc[:], V(gb.h[:, 0:4].unsqueeze(2).to_broadcast([128, 4, 128]), gb.res))
    for d in range(2):
        dd = DIRS[d]
        fm = R["fm"].next()
        tm = R["tm"].next()
        yield
        pb = BZ[:, 0:256]
        pc = BZ[:, 256:258]
        pr = BZ[:, 384:512]
        k.mm(pb[:, 0:128], bc[:, 2 + d, :], ident)
        k.mm(pb[:, 128:256], bc[:, d, :], CM(dd["Ti"]))
        k.mm(pc[:, 0:1], CM(dd["Ti"]), gb[:, d:d + 1])
        k.mm(pc[:, 1:2], CM(dd["Tr"]), gb[:, d:d + 1])
        k.mm(pr[0:2, 0:128], gn2[:, 2 * d:2 * d + 2], CM(dd["Ti"]))
        yield
        eg = R["eg"].next()
        k.act(eg[:], pb[:, 128:256], AF.Exp)
        k.cp("pool", fm[:, 0:128], pmt[8][:, cs])
        k.cp("pool", fm[:, 128:256], pmt[7][:, cs])
        k.tt("dve", fm[:, 256:384], pmt[8][:, cs], pb[:, 0:128], ALU.mult)
        k.tt("pool", fm[:, 384:512], pmt[7][:, cs], eg[:], ALU.mult)
        k.dma(gd_fm[g, d, :, :], fm[:])
        dv = R["dv"].next()
        k.cp("act", dv[:], eg[:, dd["last"]:dd["last"] + 1])
        k.dma(gd_dv[g, d, :, :], dv[:])
        yield
        ec = R["ec"].next()
        k.act(ec[:], pc[:, 0:2], AF.Exp)
        sc = R["sc"].next()
        k.tt("dve", sc[:], ec[:, 0:1], gb[:, 2 + d:3 + d], ALU.mult)
        k.ts("dve", tm[:, 0:128], v_tm, gb[:, 2 + d:3 + d], None, ALU.mult)
        k.ts("pool", tm[:, 128:256], k_tm, sc[:, 0:1], None, ALU.mult)
        k.ts("dve", tm[:, 256:384], k_tm, ec[:, 1:2], None, ALU.mult)
        k.dma(gd_tm[g, d, :, :], tm[:])
        lr = R["lr"].next()
        k.ts("dve", lr[:, 0:128], pr[0:2, 0:128], c10[:, 0:1], c10[:, 1:2], ALU.mult, ALU.add)
        k.ts("dve", lr[:, 128:256], pr[0:2, 0:128], c10[:, 1:2], c10[:, 0:1], ALU.mult, ALU.add)
        k.dma(gd_lr[g, d, :, :], lr[:])


def _invert(k, R, A0, N0, CM, tag):
    ident = CM(IDN)
    TN = [R["TN0"].next(), R["TN1"].next()]
    Ak = [R["Ak0"].next(), R["Ak1"].next()]
    k.tt("pool", TN[0][:, 0:128], N0, ident, ALU.add)
    if DBG.get("lv", 8) == 1:
        return TN[0][:, 0:128]
    p = k.ps()
    iv = DBG.get("iv", 15)
    if iv & 1:
        k.mm(p[:, 0:128], A0, N0)
    if iv & 2:
        k.mm(p[:, 128:256], N0, A0)
    if iv & 4:
        k.cp("act", TN[0][:, 128:256], p[:, 0:128])
    if iv & 8:
        k.cp("dve", Ak[0][:], p[:, 128:256])
    cur = 0
    if DBG.get("lv", 8) == 0:
        return TN[0][:, 0:128]
    for lv in range(2, DBG.get("lv", 8)):
        a_prev = Ak[cur]
        tn_prev = TN[cur]
        tn_new = TN[1 - cur]
        p = k.ps()
        if lv < 7:
            k.mm(p[:, 0:256], a_prev[:], tn_prev[:, 0:256])
            p2 = k.ps()
            k.mm(p2[:, 0:128], tn_prev[:, 128:256], a_prev[:])
            k.tt("dve", tn_new[:, 0:128], tn_prev[:, 0:128], p[:, 0:128], ALU.add)
            k.cp("act", tn_new[:, 128:256], p[:, 128:256])
            k.cp("act", Ak[1 - cur][:], p2[:, 0:128])
        else:
            k.mm(p[:, 0:128], a_prev[:], tn_prev[:, 0:128])
            k.tt("dve", tn_new[:, 0:128], tn_prev[:, 0:128], p[:, 0:128], ALU.add)
        cur = 1 - cur
    return TN[cur][:, 0:128]


def _phase_a2(k, nc, NB, SEQ, NCH, CPS, cst, CM, rw_fm, rw_tm, rw_dv, rw_y, gd_fm, gd_tm, gd_lr, gd_dv, gd_o):
    ident = CM(IDN)
    NRW = NB * 4
    NGD = NB * 2
    Zrw = [[k.sb("zrw%d_%d" % (s, i), [64, 64]) for i in range(2)] for s in range(NRW)]
    Zgd = [[k.sb("zgd%d_%d" % (s, i), [128, 128]) for i in range(2)] for s in range(NGD)]
    for s in range(NRW):
        k.memset("pool", Zrw[s][0][:], 0.0)
    for s in range(NGD):
        k.memset("pool", Zgd[s][0][:], 0.0)
    NR = 3
    R = dict(
        TN0=k.ring("TN0", NR, [128, 256]), TN1=k.ring("TN1", NR, [128, 256]),
        Ak0=k.ring("Ak0", NR, [128, 128]), Ak1=k.ring("Ak1", NR, [128, 128]),
        fm=k.ring("s_fm", NR, [128, 512]), tm=k.ring("s_tm", NR, [128, 384]),
        rtm=k.ring("s_rtm", NR, [128, 256]), rfm=k.ring("s_rfm", NR, [64, 512]),
        dv=k.ring("s_dv", NR, [128, 4]), lr=k.ring("s_lr", NR, [2, 256]),
        A0=k.ring("s_A0", NR, [128, 128]), NB_=k.ring("s_NB", NR, [128, 256]), CK=k.ring("s_CK", NR, [128, 256]),
        D=k.ring("s_D", NR, [128, 384]), m=k.ring("s_m", NR, [128, 256]),
        akv=k.ring("s_akv", NR, [128, 128]), U=k.ring("s_U", NR, [128, 128]), WT=k.ring("s_WT", NR, [128, 128]),
        X=k.ring("s_X", NR, [128, 128]), O=k.ring("s_O", NR, [128, 128]),
    )
    for step in range(CPS):
        for bi in range(NB):
            for d in (range(2) if DBG.get("rw", 1) else []):
                dd = DIRS[d]
                ci = step if d == 0 else CPS - 1 - step
                g = bi * CPS + ci
                rdv = R["dv"].next()
                k.dma(rdv[0:64, 0:4], rw_dv[g, :, :])
                for h in range(2):
                    hd = d * 2 + h
                    s = bi * 4 + hd
                    par = step % 2
                    Zo, Zn = Zrw[s][par], Zrw[s][1 - par]
                    fm = R["rfm"].next()
                    k.dma(fm[:], rw_fm[g, hd, :, :])
                    tm = R["rtm"].next()
                    k.dma(tm[:], rw_tm.v(lambda hh: hh[g, :, hd * 256:(hd + 1) * 256]))
                    aT, rT, bT, kT = fm[:, 0:128], fm[:, 128:256], fm[:, 256:384], fm[:, 384:512]
                    A_tm, Bh, Kh, Vt = tm[:, 0:64], tm[:, 64:128], tm[:, 128:192], tm[:, 192:256]
                    pA = k.ps()
                    k.mm(pA[:, 0:128], aT, bT)
                    pB = k.ps()
                    k.mm(pB[:, 0:256], bT, fm[:, 0:256])
                    pC = k.ps()
                    k.mm(pC[:, 0:256], kT, fm[:, 0:256])
                    A0 = R["A0"].next()
                    k.tt("dve", A0[:], pA[:, 0:128], CM(dd["Ms"]), ALU.mult)
                    msk = V(cst.h[:, dd["MTs"]:dd["MTs"] + 2, :].rearrange("p a b -> p (a b)"), cst.res)
                    NBt = R["NB_"].next()
                    k.tt("dve", NBt[:], pB[:, 0:256], msk, ALU.mult)
                    CK = R["CK"].next()
                    k.tt("dve", CK[:], pC[:, 0:256], msk, ALU.mult)
                    TT = _invert(k, R, A0[:], NBt[:, 0:128], CM, "rw")
                    p1 = k.ps()
                    k.mm(p1[:, 0:64], CK[:, 0:128], Vt)
                    akv = R["akv"].next()
                    k.cp("act", akv[:, 0:64], p1[:, 0:64])
                    p2 = k.ps()
                    k.mm(p2[:, 0:64], TT, akv[:, 0:64])
                    k.mm(p2[0:64, 128:256], A_tm, TT)
                    U = R["U"].next()
                    k.cp("act", U[:, 0:64], p2[:, 0:64])
                    WT = R["WT"].next()
                    k.cp("dve", WT[0:64, :], p2[0:64, 128:256])
                    pX = k.ps()
                    k.mm(pX[:, 0:64], WT[0:64, :], Zo[:])
                    X = R["X"].next()
                    k.tt("dve", X[:, 0:64], pX[:, 0:64], U[:, 0:64], ALU.add)
                    pZ = k.ps()
                    k.mm(pZ[0:64, 0:64], Kh, Vt, start=True, stop=False)
                    k.mm(pZ[0:64, 0:64], Bh, X[:, 0:64], start=False, stop=True)
                    pO = k.ps()
                    k.mm(pO[:, 0:64], rT, Zo[:], start=True, stop=False)
                    k.mm(pO[:, 0:64], NBt[:, 128:256], X[:, 0:64], start=False, stop=False)
                    k.mm(pO[:, 0:64], CK[:, 128:256], Vt, start=False, stop=True)
                    k.stt("dve", Zn[:], Zo[:], rdv[0:64, hd:hd + 1], pZ[0:64, 0:64], ALU.mult, ALU.add)
                    O = R["O"].next()
                    k.cp("act", O[:, 0:64], pO[:, 0:64])
                    k.dma(rw_y.v(lambda hh: hh[g, :, hd * 64:(hd + 1) * 64]), O[:, 0:64])
            for d in (range(2) if DBG.get("gd", 1) else []):
                dd = DIRS[d]
                ci = step if d == 0 else CPS - 1 - step
                g = bi * CPS + ci
                s = bi * 2 + d
                par = step % 2
                Zo, Zn = Zgd[s][par], Zgd[s][1 - par]
                fm = R["fm"].next()
                k.dma(fm[:], gd_fm[g, d, :, :])
                tm = R["tm"].next()
                k.dma(tm[:], gd_tm[g, d, :, :])
                lr = R["lr"].next()
                k.dma(lr[:], gd_lr[g, d, :, :])
                gdv = R["dv"].next()
                k.dma(gdv[:, 0:1], gd_dv[g, d, :, :])
                if DBG.get("cut", 9) <= 0:
                    continue
                kT, qT, kbT, qgT = fm[:, 0:128], fm[:, 128:256], fm[:, 256:384], fm[:, 384:512]
                vb, kbg, kg = tm[:, 0:128], tm[:, 128:256], tm[:, 256:384]
                pt = k.ps()
                k.mm(pt[:, 0:128], lr[:, 0:128], lr[:, 128:256])
                k.mm(pt[:, 128:256], lr[:, 128:256], lr[:, 0:128])
                m = R["m"].next()
                k.stt("dve", m[:, 0:128], pt[:, 0:128], 0.0, CM(dd["Ns"]), ALU.min, ALU.add)
                k.stt("dve", m[:, 128:256], pt[:, 128:256], 0.0, CM(dd["NTs"]), ALU.min, ALU.add)
                D = R["D"].next()
                k.act(D[:, 0:256], m[:, 0:256], AF.Exp)
                k.tt("pool", D[:, 256:384], D[:, 128:256], ident, ALU.add)
                pA = k.ps()
                k.mm(pA[:, 0:128], kbT, kT)
                pB = k.ps()
                k.mm(pB[:, 0:256], kT, fm[:, 128:384])
                A0 = R["A0"].next()
                k.stt("dve", A0[:], pA[:, 0:128], -1.0, D[:, 0:128], ALU.mult, ALU.mult)
                NBt = R["NB_"].next()
                k.stt("dve", NBt[:, 0:128], pB[:, 128:256], -1.0, D[:, 128:256], ALU.mult, ALU.mult)
                k.tt("dve", NBt[:, 128:256], pB[:, 0:128], D[:, 256:384], ALU.mult)
                if DBG.get("cut", 9) <= 1:
                    continue
                TT = _invert(k, R, A0[:], NBt[:, 0:128], CM, "gd") if DBG.get("cut", 9) > 2 else NBt[:, 0:128]
                if DBG.get("cut", 9) <= 3:
                    continue
                p2 = k.ps()
                k.mm(p2[:, 0:128], TT, vb)
                k.mm(p2[:, 128:256], kbg, TT)
                U = R["U"].next()
                k.cp("act", U[:], p2[:, 0:128])
                WT = R["WT"].next()
                k.ts("dve", WT[:], p2[:, 128:256], -1.0, None, ALU.mult)
                pX = k.ps()
                k.mm(pX[:, 0:128], WT[:], Zo[:])
                X = R["X"].next()
                k.tt("dve", X[:], pX[:, 0:128], U[:], ALU.add)
                pZ = k.ps()
                k.mm(pZ[:, 0:128], kg, X[:])
                pO = k.ps()
                k.mm(pO[:, 0:128], qgT, Zo[:], start=True, stop=False)
                k.mm(pO[:, 0:128], NBt[:, 128:256], X[:], start=False, stop=True)
                k.stt("dve", Zn[:], Zo[:], gdv[:, 0:1], pZ[:, 0:128], ALU.mult, ALU.add)
                O = R["O"].next()
                k.cp("act", O[:], pO[:, 0:128])
                k.dma(gd_o[g, d, :, :], O[:])


def _phase_a3(k, nc, NCH, rowc, CM, rw_post, rw_y, gd_post, gd_o, yT, ydt=F32):
    ident = CM(IDN)
    yt_r = k.ring("a3y", 2, [128, 256])
    po_r = k.ring("a3po", 2, [128, 258])
    y_r = k.ring("a3ys", 2, [128, 128])
    c_r = k.ring("a3c", 2, [128, 128])
    st_r = k.ring("a3st", 2, [128, 4])
    go_r = k.ring("a3go", 2, [128, 256])
    sz_r = k.ring("a3sz", 2, [128, 128])
    o_r = k.ring("a3o", 2, [128, 128])
    yo_r = k.ring("a3yo", 2, [128, 256], ydt)

    def h2(v):
        return V(v.ap.rearrange("p (a b) -> p a b", a=2), v.res)

    def bch(v):
        return V(v.ap.unsqueeze(2).to_broadcast([128, 2, 64]), v.res)

    for g in range(NCH):
        yt = yt_r.next()
        k.dma(yt[:], rw_y[g, :, :])
        po = po_r.next()
        k.dma(po[:], rw_post[g, :, :])
        y = y_r.next()
        k.tt("pool", y[:], yt[:, 0:128], yt[:, 128:256], ALU.add)
        st = st_r.next()
        k.red("dve", st[:, 0:2], h2(y[:]))
        k.ts("dve", st[:, 0:2], st[:, 0:2], 1.0 / 64, None, ALU.mult)
        c = c_r.next()
        k.tt("dve", h2(c[:]), h2(y[:]), bch(st[:, 0:2]), ALU.subtract)
        k.tt("pool", y[:], c[:], c[:], ALU.mult)
        k.red("dve", st[:, 2:4], h2(y[:]))
        k.rsqrt(st[:, 2:4], st[:, 2:4], scale=1.0 / 64, bias=64e-5)
        k.tt("dve", h2(c[:]), h2(c[:]), bch(st[:, 2:4]), ALU.mult)
        k.tt("pool", c[:], c[:], rowc[:, RC_GNW:RC_GNW + 128], ALU.mult)
        k.tt("pool", c[:], c[:], rowc[:, RC_GNB:RC_GNB + 128], ALU.add)
        k.tt("dve", h2(y[:]), h2(po[:, 0:128]), bch(po[:, 256:258]), ALU.mult)
        k.tt("pool", c[:], c[:], y[:], ALU.add)
        k.tt("dve", c[:], c[:], po[:, 128:256], ALU.mult)
        go = go_r.next()
        k.dma(go.v(lambda h: h[:].rearrange("p (d c) -> p d c", d=2)),
              gd_o.v(lambda h: h[g, :, :, :].rearrange("d p c -> p d c")))
        sz = sz_r.next()
        k.dma(sz[:], gd_post[g, :, :])
        o = o_r.next()
        k.tt("pool", o[:], go[:, 0:128], go[:, 128:256], ALU.add)
        o2 = o_r.next()
        k.tt("pool", o2[:], o[:], o[:], ALU.mult)
        st2 = st_r.next()
        k.red("dve", st2[:, 0:1], o2[:])
        k.rsqrt(st2[:, 0:1], st2[:, 0:1], scale=1.0 / 128, bias=1e-6)
        k.ts("dve", o[:], o[:], st2[:, 0:1], None, ALU.mult)
        k.tt("pool", o[:], o[:], rowc[:, RC_GNORM:RC_GNORM + 128], ALU.mult)
        k.tt("dve", o[:], o[:], sz[:], ALU.mult)
        p = k.ps()
        k.tr(p[:, 0:128], c[:], ident)
        k.tr(p[:, 128:256], o[:], ident)
        yo = yo_r.next()
        k.cp("act", yo[:], p[:, 0:256])
        k.dma_multi([(yT[0:128, g * 128:(g + 1) * 128], yo[:, 0:128]),
                     (yT[128:256, g * 128:(g + 1) * 128], yo[:, 128:256])])


def phase_a_inputs(c, inp, consts):
    W = inp["w_in"][0]
    rc = np.arange(128 * c, 128 * c + 128)
    qh = c // 2
    cols = [rc, 1024 + rc, 2048 + rc, np.arange(3072, 3200), np.arange(3200, 3328), np.arange(3328, 3456),
            np.concatenate([np.arange(3456, 3488), G0 + 3072 + np.array([c, 8 + c, 16 + c, 24 + c])]),
            G0 + qh * 128 + np.arange(128), G0 + 512 + qh * 128 + np.arange(128),
            G0 + 1024 + c * 128 + np.arange(128), G0 + 2048 + c * 128 + np.arange(128)]
    WA = np.zeros((11, 128, NJ, 128), np.float32)
    for t, cl in enumerate(cols):
        WA[t, :, :, :len(cl)] = W[:, cl].reshape(NJ, 128, len(cl)).transpose(1, 0, 2)
    prm = np.zeros((128, NPRM), np.float32)
    prm[:, PR_GAIN:PR_GAIN + 16] = inp["norm_pre_mix"][0].reshape(NJ, 128).T
    mu = inp["rw_shift_mu"][0]
    for t in range(7):
        cl = cols[t]
        n = min(len(cl), 128)
        if t == 6:
            prm[:32, PR_MU + t] = mu[cl[:32]]
        else:
            prm[:n, PR_MU + t] = mu[cl]
    cwh = inp["gdn_conv_w"][0]
    for t, base in enumerate([qh * 128, 512 + qh * 128, 1024 + c * 128]):
        prm[:, PR_CW + t * 5:PR_CW + t * 5 + 5] = cwh[:, base:base + 128].T
    rowc = np.zeros((128, NROWC), np.float32)

    def row(v):
        return np.broadcast_to(np.asarray(v, np.float32)[None, :], (128, len(v)))

    rowc[:, RC_W0:RC_W0 + 256] = row(np.concatenate([inp["rw_w0_f"][0][rc], inp["rw_w0_b"][0][rc]]))
    rowc[:, RC_A0:RC_A0 + 256] = row(np.concatenate([inp["rw_a0_f"][0][rc], inp["rw_a0_b"][0][rc]]))
    rowc[:, RC_KK:RC_KK + 128] = row(inp["rw_k_k"][0][rc])
    rowc[:, RC_KA:RC_KA + 256] = row(np.concatenate([inp["rw_k_a"][0][rc]] * 2))
    rowc[:, RC_RK:RC_RK + 128] = row(inp["rw_r_k"][0].reshape(-1)[rc])
    rowc[:, RC_GNW:RC_GNW + 128] = row(inp["rw_gn_w"][0][rc])
    rowc[:, RC_GNB:RC_GNB + 128] = row(inp["rw_gn_b"][0][rc])
    rowc[:, RC_DTB:RC_DTB + 2] = row(np.array([inp["gdn_dt_bias_f"][0][c], inp["gdn_dt_bias_b"][0][c]]))
    rowc[:, RC_ALOG:RC_ALOG + 2] = row(np.array([inp["gdn_a_log_f"][0][c], inp["gdn_a_log_b"][0][c]]))
    rowc[:, RC_GNORM:RC_GNORM + 128] = row(inp["gdn_norm_w"][0])
    lw = np.zeros((128, 768), np.float32)
    lw[0:64, 0:128] = inp["rw_w2_f"][0][:, rc]
    lw[64:128, 128:256] = inp["rw_w2_b"][0][:, rc]
    lw[0:64, 256:384] = inp["rw_a2_f"][0][:, rc]
    lw[64:128, 384:512] = inp["rw_a2_b"][0][:, rc]
    lw[:, 512:640] = inp["rw_g2"][0][0:128, rc]
    lw[0:32, 640:768] = inp["rw_g2"][0][128:160, rc]
    return dict(WA=WA.reshape(11, 128, NJ * 128), prm=prm, rowc=rowc, lw=lw, cst=consts.reshape(128, NCST * 128))


def build_phase_b(TB):
    N = min(512, TB)
    NBK = TB // N
    nc = bass.Bass("TRN2", target_bir_lowering=False)
    k = KB(nc)
    with k.st:
        xTs = k.dram("xTs", [D_MODEL, TB], kind="ExternalInput")
        yTs = k.dram("yTs", [D_MODEL, TB], kind="ExternalInput")
        WG = k.dram("WG", [32, 128, 2048], kind="ExternalInput")
        WP = k.dram("WP", [16, 128, 2048], kind="ExternalInput")
        WO = k.dram("WO", [16, 128, 2048], kind="ExternalInput")
        WFG = k.dram("WFG", [44, 128, 2048], kind="ExternalInput")
        WFU = k.dram("WFU", [44, 128, 2048], kind="ExternalInput")
        WD = k.dram("WD", [16, 128, 44 * 128], kind="ExternalInput")
        gn_d = k.dram("gn", [128, 64], kind="ExternalInput")
        cst_d = k.dram("cst", [128, NCST * 128], kind="ExternalInput")
        outT = k.dram("outT", [D_MODEL, TB], kind="ExternalOutput")
        _phase_b(k, nc, TB, N, NBK, xTs, yTs, WG, WP, WO, WFG, WFU, WD, gn_d, cst_d, outT)
        k.S.emit(final_waits=[outT.res])
    return nc


def _phase_b(k, nc, TB, N, NBK, xTs, yTs, WG, WP, WO, WFG, WFU, WD, gn_d, cst_d, outT, yg=None):
    if k.psr:
        psn = k.psr[7]
        k.psr = k.psr[0:7]
    else:
        k.psr = [T(k.st.enter_context(nc.psum_tensor("psb%d" % i, [128, 512], F32)), "psb%d" % i, True) for i in range(7)]
        psn = T(k.st.enter_context(nc.psum_tensor("psn", [128, 512], F32)), "psn", True)
    k.psi = 0
    ones = k.sb("b_ones", [128, 128])
    k.dma(ones[:], cst_d[:, ONE * 128:(ONE + 1) * 128])
    gn = k.sb("b_gn", [128, 64])
    k.dma(gn[:], gn_d[:, :])
    xt = k.sb("b_xt", [128, NJ, N])
    u = k.sb("b_u", [128, NJ, N], BF16)
    ybf = k.sb("b_ybf", [128, NJ, N], BF16)
    mg = k.sb("b_mg", [128, NJ, N], BF16)
    o = k.sb("b_o", [128, NJ, N])
    f = k.sb("b_f", [128, 44, N], BF16)
    ystg = k.ring("b_ystg", 2, [128, N])
    sqt = k.ring("b_sqt", 2, [128, N])
    rstd = k.sb("b_rstd", [128, N])
    sg = k.ring("b_sg", 4, [128, N])
    mt = k.ring("b_mt", 2, [128, N])
    wstg = k.ring("b_wstg", 2, [128, NJ, 128])
    wb = k.ring("b_wb", 3, [128, NJ, 128], BF16)
    cnt = [0]

    def unit(src_v, nj, gain_col=None):
        s = wstg.next()
        k.dma(s.v(lambda h: h[:, 0:nj, :].rearrange("p j c -> p (j c)")), src_v)
        w = wb.next()
        eng = "pool" if cnt[0] % 2 == 0 else "dve"
        cnt[0] += 1
        if gain_col is None:
            k.cp(eng, w[:, 0:nj, :], s[:, 0:nj, :])
        else:
            gcol = gn[:, gain_col:gain_col + nj]
            k.tt(eng, w[:, 0:nj, :], s[:, 0:nj, :],
                 V(gcol.ap.unsqueeze(2).to_broadcast([128, nj, 128]), gcol.res), ALU.mult)
        return w

    def bc_j(t_):
        return V(t_.h[:].unsqueeze(1).to_broadcast([128, NJ, N]), t_.res)

    def norm_stats(src_fn, nrow):
        for r in range(nrow):
            s2 = sqt.next()
            k.act(s2[:], src_fn(r), AF.Square)
            k.mm(psn[:, 0:N], ones[:], s2[:], start=(r == 0), stop=(r == nrow - 1))
        k.rsqrt(rstd[:], psn[:, 0:N], scale=1.0 / D_MODEL, bias=1e-6)

    for bk in range(NBK):
        ts_ = slice(bk * N, (bk + 1) * N)
        k.dma_multi([(xt[:, jq * 4:(jq + 1) * 4, :],
                      xTs.v(lambda h, jq=jq: h[jq * 512:(jq + 1) * 512, ts_].rearrange("(j p) t -> p j t", p=128)))
                     for jq in range(4)])
        norm_stats(lambda r: xt[:, r, :], NJ)
        k.tt("dve", u[:], xt[:], bc_j(rstd), ALU.mult)
        if yg is None:
            for j in range(NJ):
                ys = ystg.next()
                k.dma(ys[:], yTs[j * 128:(j + 1) * 128, ts_])
                k.cp("pool", ybf[:, j, :], ys[:])
        else:
            def ld(e, bk=bk):
                if "pid" not in PIDC:
                    PIDC["pid"] = e.partition_id()
                pid = PIDC["pid"]
                off = e.snap(pid * (TB // N) + bk)
                ygv = yg.h.rearrange("(r h p) t -> h p r t", h=2, p=128)
                return [e.dma_start(out=ybf.h[:, hh * 8:(hh + 1) * 8, :], in_=ygv[hh, :, :, bass.ts(off, N)])
                        for hh in range(2)]
            k.S.op("sp", ld, reads=[yg.res], writes=[ybf.res], dma=True, ndma=2)
        for r in range(NJ):
            sgs = []
            for gi in range(2):
                w = unit(WG[gi * 16 + r, :, :], NJ, gain_col=0)
                p = k.ps()
                for j in range(NJ):
                    k.mm(p[:, 0:N], w[:, j, :], u[:, j, :], start=(j == 0), stop=(j == NJ - 1))
                s_ = sg.next()
                k.act(s_[:], p[:, 0:N], AF.Sigmoid)
                sgs.append(s_)
            w = unit(WP[r, :, :], NJ)
            pa = k.ps()
            pb = k.ps()
            for j in range(8):
                k.mm(pa[:, 0:N], w[:, j, :], ybf[:, j, :], start=(j == 0), stop=(j == 7))
            for j in range(8, 16):
                k.mm(pb[:, 0:N], w[:, j, :], ybf[:, j, :], start=(j == 8), stop=(j == 15))
            m1 = mt.next()
            k.tt("dve", m1[:], pa[:, 0:N], sgs[0][:], ALU.mult)
            m2 = mt.next()
            k.tt("dve", m2[:], pb[:, 0:N], sgs[1][:], ALU.mult)
            k.tt("pool", mg[:, r, :], m1[:], m2[:], ALU.add)
        for r in range(NJ):
            w = unit(WO[r, :, :], NJ)
            p = k.ps()
            for j in range(NJ):
                k.mm(p[:, 0:N], w[:, j, :], mg[:, j, :], start=(j == 0), stop=(j == NJ - 1))
            k.cp("act", o[:, r, :], p[:, 0:N])
        norm_stats(lambda r: o[:, r, :], NJ)
        for r in range(NJ):
            m1 = mt.next()
            k.tt("dve", m1[:], o[:, r, :], rstd[:], ALU.mult)
            k.stt("dve", xt[:, r, :], m1[:], gn[:, 16 + r:17 + r], xt[:, r, :], ALU.mult, ALU.add)
        norm_stats(lambda r: xt[:, r, :], NJ)
        k.tt("dve", u[:], xt[:], bc_j(rstd), ALU.mult)
        for r in range(44):
            w = unit(WFG[r, :, :], NJ, gain_col=32)
            pg = k.ps()
            for j in range(NJ):
                k.mm(pg[:, 0:N], w[:, j, :], u[:, j, :], start=(j == 0), stop=(j == NJ - 1))
            w2 = unit(WFU[r, :, :], NJ, gain_col=32)
            pu = k.ps()
            for j in range(NJ):
                k.mm(pu[:, 0:N], w2[:, j, :], u[:, j, :], start=(j == 0), stop=(j == NJ - 1))
            s_ = sg.next()
            k.act(s_[:], pg[:, 0:N], AF.Silu)
            k.tt("dve", f[:, r, :], s_[:], pu[:, 0:N], ALU.mult)
        for r in range(NJ):
            p = k.ps()
            j0 = 0
            for piece in (16, 16, 12):
                w = unit(WD.v(lambda h: h[r, :, j0 * 128:(j0 + piece) * 128]), piece)
                for jj in range(piece):
                    j = j0 + jj
                    k.mm(p[:, 0:N], w[:, jj, :], f[:, j, :], start=(j == 0), stop=(j == 43))
                j0 += piece
            k.cp("act", o[:, r, :], p[:, 0:N])
        norm_stats(lambda r: o[:, r, :], NJ)
        for r in range(NJ):
            m1 = mt.next()
            k.tt("dve", m1[:], o[:, r, :], rstd[:], ALU.mult)
            k.stt("dve", xt[:, r, :], m1[:], gn[:, 48 + r:49 + r], xt[:, r, :], ALU.mult, ALU.add)
        k.dma_multi([(outT.v(lambda h, jq=jq: h[jq * 512:(jq + 1) * 512, ts_].rearrange("(j p) t -> p j t", p=128)),
                      xt[:, jq * 4:(jq + 1) * 4, :]) for jq in range(4)])


def _tiles(W, ktiles):
    K_, R_ = W.shape
    return np.ascontiguousarray(W.reshape(ktiles, 128, R_ // 128, 128).transpose(2, 1, 0, 3)).reshape(R_ // 128, 128, ktiles * 128)


def phase_b_weights(inp, consts):
    W = inp["w_in"][0]
    d = {}
    d["WG"] = _tiles(W[:, GATE0:GATE0 + 4096], 16)
    wp = np.concatenate([inp["w_branch_rw"][0], inp["w_branch_gdn"][0]], axis=0)
    d["WP"] = _tiles(wp, 16)
    d["WO"] = _tiles(inp["w_out"][0], 16)
    d["WFG"] = _tiles(inp["w_ffn_gate"][0], 16)
    d["WFU"] = _tiles(inp["w_ffn_up"][0], 16)
    d["WD"] = _tiles(inp["w_ffn_down"][0], 44)
    gn = np.zeros((128, 64), np.float32)
    for i, nm in enumerate(["norm_pre_mix", "norm_post_mix", "norm_pre_ffn", "norm_post_ffn"]):
        gn[:, i * 16:(i + 1) * 16] = inp[nm][0].reshape(NJ, 128).T
    d["gn"] = gn
    d["cst"] = consts.reshape(128, NCST * 128)
    return d


def build_fused(NB, SEQ, n=8):
    NTOK = NB * SEQ
    NCH = NTOK // 128
    NBLK = NTOK // 256
    CPS = SEQ // 128
    TB = NTOK // n
    N = min(512, TB)
    NBK = TB // N
    nc = bass.Bass("TRN2", target_bir_lowering=False)
    PIDC.clear()
    k = KB(nc)
    rg = [list(range(n))]
    with k.st:
        xT = k.dram("xT", [D_MODEL, NTOK], kind="ExternalInput")
        WA = k.dram("WA", [11, 128, NJ * 128], kind="ExternalInput")
        prm_d = k.dram("prm", [128, NPRM], kind="ExternalInput")
        rowc_d = k.dram("rowc", [128, NROWC], kind="ExternalInput")
        lw_d = k.dram("lw", [128, 768], kind="ExternalInput")
        cst_d = k.dram("cst", [128, NCST * 128], kind="ExternalInput")
        xTs = k.dram("xTs", [D_MODEL, TB], kind="ExternalInput")
        WG = k.dram("WG", [32, 128, 2048], kind="ExternalInput")
        WP = k.dram("WP", [16, 128, 2048], kind="ExternalInput")
        WO = k.dram("WO", [16, 128, 2048], kind="ExternalInput")
        WFG = k.dram("WFG", [44, 128, 2048], kind="ExternalInput")
        WFU = k.dram("WFU", [44, 128, 2048], kind="ExternalInput")
        WD = k.dram("WD", [16, 128, 44 * 128], kind="ExternalInput")
        gn_d = k.dram("gn", [128, 64], kind="ExternalInput")
        outT = k.dram("outT", [D_MODEL, TB], kind="ExternalOutput")
        yT = k.dram("yT_loc", [256, NTOK], BF16)
        yg = T(nc.dram_tensor("yg", [n * 256, NTOK], BF16, addr_space="Shared").ap(), "yg")
        fin = k.dram("fence_in", [1, 64])
        fout = k.dram("fence_out", [1, 64])
        rw_fm = k.dram("rw_fm", [NCH, 4, 64, 512])
        rw_tm = k.dram("rw_tm", [NCH, 128, 4 * 256])
        rw_dv = k.dram("rw_dv", [NCH, 64, 4])
        rw_post = k.dram("rw_post", [NCH, 128, 258])
        rw_y = k.dram("rw_y", [NCH, 128, 256])
        gd_fm = k.dram("gd_fm", [NCH, 2, 128, 512])
        gd_tm = k.dram("gd_tm", [NCH, 2, 128, 384])
        gd_lr = k.dram("gd_lr", [NCH, 2, 2, 256])
        gd_dv = k.dram("gd_dv", [NCH, 2, 128, 1])
        gd_post = k.dram("gd_post", [NCH, 128, 128])
        gd_o = k.dram("gd_o", [NCH, 2, 128, 128])

        k.init_psum(8)
        with k.phase():
            cst = k.sb("cst", [128, NCST, 128])
            k.dma(cst.v(lambda h: h[:].rearrange("p a b -> p (a b)")), cst_d[:, :])
            prm = k.sb("prm", [128, NPRM])
            k.dma(prm[:], prm_d[:, :])
            rowc = k.sb("rowc", [128, NROWC])
            k.dma(rowc[:], rowc_d[:, :])
            lw = k.sb("lw", [128, 768])
            k.dma(lw[:], lw_d[:, :])

            def CM(i):
                return cst[:, i, :]

            k.dma(fin[:, :], cst[0:1, ONE, 0:64])
            with k.phase():
                _A1R.clear()
                _A1G.clear()
                _phase_a1(k, nc, NB, SEQ, NTOK, NCH, NBLK, xT, WA, prm, rowc, lw, cst, CM,
                          rw_fm, rw_tm, rw_dv, rw_post, gd_fm, gd_tm, gd_lr, gd_dv, gd_post)
            with k.phase():
                _phase_a2(k, nc, NB, SEQ, NCH, CPS, cst, CM, rw_fm, rw_tm, rw_dv, rw_y, gd_fm, gd_tm, gd_lr, gd_dv, gd_o)
            with k.phase():
                _phase_a3(k, nc, NCH, rowc, CM, rw_post, rw_y, gd_post, gd_o, yT, ydt=BF16)
        yin, yout = yT.h.opt(), yg.h.opt()
        k.S.op("pool", lambda e: e.collective_compute("AllGather", ALU.bypass, replica_groups=rg, ins=[yin], outs=[yout]),
               reads=[yT.res], writes=[yg.res], cc=True)
        fi, fo = fin.h.opt(), fout.h.opt()
        k.S.op("pool", lambda e: e.collective_compute("AllReduce", ALU.add, replica_groups=rg, ins=[fi], outs=[fo]),
               reads=[fin.res, yg.res], writes=[fout.res, yg.res], cc=True)
        k.S.barrier()
        _phase_b(k, nc, TB, N, NBK, xTs, None, WG, WP, WO, WFG, WFU, WD, gn_d, cst_d, outT, yg=yg)
        k.S.emit(final_waits=[outT.res])
    return nc


def kernel_unfused(**inp):
    return _kernel_impl(False, **inp)


def kernel(**inp):
    return _kernel_impl(False, **inp)


def _kernel_impl(fused, **inp):
    inp = {kk: np.asarray(v) for kk, v in inp.items()}
    x = inp["x"]
    NB, SEQ, _ = x.shape
    NTOK = NB * SEQ
    n = 8
    TB = NTOK // n
    consts = make_consts()
    xT = np.ascontiguousarray(x.reshape(NTOK, D_MODEL).T)
    wts = phase_b_weights(inp, consts)
    if fused:
        nc = build_fused(NB, SEQ, n)
        ims = []
        for c in range(n):
            d = phase_a_inputs(c, inp, consts)
            d.update(wts)
            d["xT"] = xT
            d["xTs"] = np.ascontiguousarray(xT[:, c * TB:(c + 1) * TB])
            ims.append(d)
        rb = run_bass_kernel_spmd(nc, ims, core_ids=list(range(n)))
    else:
        nca = build_phase_a(NB, SEQ)
        in_a = []
        for c in range(n):
            d = phase_a_inputs(c, inp, consts)
            d["xT"] = xT
            in_a.append(d)
        ra = run_bass_kernel_spmd(nca, in_a, core_ids=list(range(n)))
        yT = np.zeros((D_MODEL, NTOK), np.float32)
        for c in range(n):
            y = np.asarray(ra.results[c]["yT"])
            yT[128 * c:128 * c + 128] = y[0:128]
            yT[1024 + 128 * c:1024 + 128 * c + 128] = y[128:256]
        ncb = build_phase_b(TB)
        in_b = []
        for c in range(n):
            d = dict(wts)
            d["xTs"] = np.ascontiguousarray(xT[:, c * TB:(c + 1) * TB])
            d["yTs"] = np.ascontiguousarray(yT[:, c * TB:(c + 1) * TB])
            in_b.append(d)
        rb = run_bass_kernel_spmd(ncb, in_b, core_ids=list(range(n)))
    out = np.zeros((NTOK, D_MODEL), np.float32)
    for c in range(n):
        out[c * TB:(c + 1) * TB] = np.asarray(rb.results[c]["outT"]).T
    return out.reshape(NB, SEQ, D_MODEL)


def _invert_g(k, R, B, A0, N0, CM):
    ident = CM(IDN)
    TN = [R["TN0"], R["TN1"]]
    Ak = [R["Ak0"], R["Ak1"]]
    k.tt("pool", TN[0][:, 0:128], N0, ident, ALU.add)
    k.mm(B[:, 0:128], A0, N0)
    k.mm(B[:, 128:256], N0, A0)
    yield
    k.cp("act", TN[0][:, 128:256], B[:, 0:128])
    k.cp("dve", Ak[0][:], B[:, 128:256])
    yield
    cur = 0
    for lv in range(2, 8):
        a_prev, tn_prev, tn_new = Ak[cur], TN[cur], TN[1 - cur]
        if lv < 7:
            k.mm(B[:, 0:256], a_prev[:], tn_prev[:, 0:256])
            k.mm(B[:, 256:384], tn_prev[:, 128:256], a_prev[:])
            yield
            k.tt("dve", tn_new[:, 0:128], tn_prev[:, 0:128], B[:, 0:128], ALU.add)
            k.cp("act", tn_new[:, 128:256], B[:, 128:256])
            k.cp("act" if lv % 2 else "dve", Ak[1 - cur][:], B[:, 256:384])
            yield
        else:
            k.mm(B[:, 0:128], a_prev[:], tn_prev[:, 0:128])
            yield
            k.tt("dve", tn_new[:, 0:128], tn_prev[:, 0:128], B[:, 0:128], ALU.add)
            yield
        cur = 1 - cur
    return TN[cur][:, 0:128]


def _rw_step_g(k, R, B, cst, CM, dd, g, hd, Zo, Zn, rw_fm, rw_tm, rw_dv, rw_y):
    fm = R["rfm"].next()
    k.dma(fm[:], rw_fm[g, hd, :, :])
    tm = R["rtm"].next()
    k.dma(tm[:], rw_tm.v(lambda hh: hh[g, :, hd * 256:(hd + 1) * 256]))
    rdv = R["dv"].next()
    k.dma(rdv[0:64, 0:4], rw_dv[g, :, :])
    aT, rT, bT, kT = fm[:, 0:128], fm[:, 128:256], fm[:, 256:384], fm[:, 384:512]
    A_tm, Bh, Kh, Vt = tm[:, 0:64], tm[:, 64:128], tm[:, 128:192], tm[:, 192:256]
    msk = V(cst.h[:, dd["MTs"]:dd["MTs"] + 2, :].rearrange("p a b -> p (a b)"), cst.res)
    k.mm(B[:, 0:128], aT, bT)
    k.mm(B[:, 128:384], bT, fm[:, 0:256])
    yield
    A0, NBt, CK = R["A0"], R["NB_"], R["CK"]
    k.tt("dve", A0[:], B[:, 0:128], CM(dd["Ms"]), ALU.mult)
    k.tt("dve", NBt[:], B[:, 128:384], msk, ALU.mult)
    yield
    k.mm(B[:, 0:256], kT, fm[:, 0:256])
    yield
    k.tt("dve", CK[:], B[:, 0:256], msk, ALU.mult)
    yield
    TT = yield from _invert_g(k, R, B, A0[:], NBt[:, 0:128], CM)
    k.mm(B[:, 0:64], CK[:, 0:128], Vt)
    yield
    akv, U, WT, X, O = R["akv"], R["U"], R["WT"], R["X"], R["O"]
    k.cp("act", akv[:, 0:64], B[:, 0:64])
    yield
    k.mm(B[:, 0:64], TT, akv[:, 0:64])
    k.mm(B[0:64, 128:256], A_tm, TT)
    yield
    k.cp("act", U[:, 0:64], B[:, 0:64])
    k.cp("dve", WT[0:64, :], B[0:64, 128:256])
    yield
    k.mm(B[:, 0:64], WT[0:64, :], Zo[:])
    yield
    k.tt("dve", X[:, 0:64], B[:, 0:64], U[:, 0:64], ALU.add)
    yield
    k.mm(B[0:64, 64:128], Kh, Vt, start=True, stop=False)
    k.mm(B[0:64, 64:128], Bh, X[:, 0:64], start=False, stop=True)
    k.mm(B[:, 128:192], rT, Zo[:], start=True, stop=False)
    k.mm(B[:, 128:192], NBt[:, 128:256], X[:, 0:64], start=False, stop=False)
    k.mm(B[:, 128:192], CK[:, 128:256], Vt, start=False, stop=True)
    yield
    k.stt("dve", Zn[:], Zo[:], rdv[0:64, hd:hd + 1], B[0:64, 64:128], ALU.mult, ALU.add)
    k.cp("act", O[:, 0:64], B[:, 128:192])
    k.dma(rw_y.v(lambda hh: hh[g, :, hd * 64:(hd + 1) * 64]), O[:, 0:64])


def _gd_step_g(k, R, B, cst, CM, dd, g, d, Zo, Zn, gd_fm, gd_tm, gd_lr, gd_dv, gd_o):
    ident = CM(IDN)
    fm = R["fm"].next()
    k.dma(fm[:], gd_fm[g, d, :, :])
    tm = R["tm"].next()
    k.dma(tm[:], gd_tm[g, d, :, :])
    lr = R["lr"].next()
    k.dma(lr[:], gd_lr[g, d, :, :])
    gdv = R["dv"].next()
    k.dma(gdv[:, 0:1], gd_dv[g, d, :, :])
    kT, qT, kbT, qgT = fm[:, 0:128], fm[:, 128:256], fm[:, 256:384], fm[:, 384:512]
    vb, kbg, kg = tm[:, 0:128], tm[:, 128:256], tm[:, 256:384]
    k.mm(B[:, 0:128], lr[:, 0:128], lr[:, 128:256])
    k.mm(B[:, 128:256], lr[:, 128:256], lr[:, 0:128])
    yield
    m, D, A0, NBt = R["m"], R["D"], R["A0"], R["NB_"]
    k.stt("dve", m[:, 0:128], B[:, 0:128], 0.0, CM(dd["Ns"]), ALU.min, ALU.add)
    k.stt("dve", m[:, 128:256], B[:, 128:256], 0.0, CM(dd["NTs"]), ALU.min, ALU.add)
    yield
    k.act(D[:, 0:256], m[:, 0:256], AF.Exp)
    k.mm(B[:, 0:128], kbT, kT)
    k.mm(B[:, 128:384], kT, fm[:, 128:384])
    yield
    k.tt("pool", D[:, 256:384], D[:, 128:256], ident, ALU.add)
    k.stt("dve", A0[:], B[:, 0:128], -1.0, D[:, 0:128], ALU.mult, ALU.mult)
    k.stt("dve", NBt[:, 0:128], B[:, 256:384], -1.0, D[:, 128:256], ALU.mult, ALU.mult)
    yield
    k.tt("dve", NBt[:, 128:256], B[:, 128:256], D[:, 256:384], ALU.mult)
    yield
    TT = yield from _invert_g(k, R, B, A0[:], NBt[:, 0:128], CM)
    k.mm(B[:, 0:128], TT, vb)
    k.mm(B[:, 128:256], kbg, TT)
    yield
    U, WT, X, O = R["U"], R["WT"], R["X"], R["O"]
    k.cp("act", U[:], B[:, 0:128])
    k.ts("dve", WT[:], B[:, 128:256], -1.0, None, ALU.mult)
    yield
    k.mm(B[:, 0:128], WT[:], Zo[:])
    yield
    k.tt("dve", X[:], B[:, 0:128], U[:], ALU.add)
    yield
    k.mm(B[:, 128:256], kg, X[:])
    k.mm(B[:, 256:384], qgT, Zo[:], start=True, stop=False)
    k.mm(B[:, 256:384], NBt[:, 128:256], X[:], start=False, stop=True)
    yield
    k.stt("dve", Zn[:], Zo[:], gdv[:, 0:1], B[:, 128:256], ALU.mult, ALU.add)
    k.cp("act", O[:], B[:, 256:384])
    k.dma(gd_o[g, d, :, :], O[:])


def _phase_a2(k, nc, NB, SEQ, NCH, CPS, cst, CM, rw_fm, rw_tm, rw_dv, rw_y, gd_fm, gd_tm, gd_lr, gd_dv, gd_o):
    NRW = NB * 4
    NGD = NB * 2
    Zrw = [[k.sb("zrw%d_%d" % (s, i), [64, 64]) for i in range(2)] for s in range(NRW)]
    Zgd = [[k.sb("zgd%d_%d" % (s, i), [128, 128]) for i in range(2)] for s in range(NGD)]
    for s in range(NRW):
        k.memset("pool", Zrw[s][0][:], 0.0)
    for s in range(NGD):
        k.memset("pool", Zgd[s][0][:], 0.0)
    slots = []
    for sl in range(6):
        t = "s%d_" % sl
        R = dict(TN0=k.sb(t + "TN0", [128, 256]), TN1=k.sb(t + "TN1", [128, 256]),
                 Ak0=k.sb(t + "Ak0", [128, 128]), Ak1=k.sb(t + "Ak1", [128, 128]),
                 A0=k.sb(t + "A0", [128, 128]), NB_=k.sb(t + "NB", [128, 256]),
                 U=k.sb(t + "U", [128, 128]), WT=k.sb(t + "WT", [128, 128]),
                 X=k.sb(t + "X", [128, 128]), O=k.sb(t + "O", [128, 128]),
                 dv=k.ring(t + "dv", 2, [128, 4]))
        if sl < 4:
            R.update(CK=k.sb(t + "CK", [128, 256]), akv=k.sb(t + "akv", [128, 64]),
                     rfm=k.ring(t + "rfm", 2, [64, 512]), rtm=k.ring(t + "rtm", 2, [128, 256]))
        else:
            R.update(D=k.sb(t + "D", [128, 384]), m=k.sb(t + "m", [128, 256]),
                     fm=k.ring(t + "fm", 2, [128, 512]), tm=k.ring(t + "tm", 2, [128, 384]),
                     lr=k.ring(t + "lr", 2, [2, 256]))
        slots.append(R)
    banks = k.psr[0:6]
    for step in range(CPS):
        par = step % 2
        for bi in range(NB):
            gens = []
            for d in range(2):
                dd = DIRS[d]
                ci = step if d == 0 else CPS - 1 - step
                g = bi * CPS + ci
                for h in range(2):
                    hd = d * 2 + h
                    s = bi * 4 + hd
                    gens.append(_rw_step_g(k, slots[hd], banks[hd], cst, CM, dd, g, hd,
                                           Zrw[s][par], Zrw[s][1 - par], rw_fm, rw_tm, rw_dv, rw_y))
                s = bi * 2 + d
                gens.append(_gd_step_g(k, slots[4 + d], banks[4 + d], cst, CM, dd, g, d,
                                       Zgd[s][par], Zgd[s][1 - par], gd_fm, gd_tm, gd_lr, gd_dv, gd_o))
            act_ = list(gens)
            while act_:
                for gg in list(act_):
                    try:
                        next(gg)
                    except StopIteration:
                        act_.remove(gg)
```
